# Optimizing a Trainium2 kernel written in Bass

```python
import math
import jax
import jax.numpy as jnp
from jax import lax
import numpy as np

D_MODEL = 2048
BATCH = 32
SEQ = 256
DEPTH = 4
DEC_BATCH = 8
DEC_SEQ = 2048
PAST_LEN = 512

GRID_W = 64
BRANCH_W = D_MODEL // 4
HG_DK = 128
HG_H = BRANCH_W // HG_DK
GD_DK = 128
GD_H = BRANCH_W // GD_DK
RW_DK = 64
RW_H = BRANCH_W // RW_DK
RW_LORA_W = D_MODEL // 32
RW_LORA_A = D_MODEL // 32
RW_LORA_G = D_MODEL // 16
D_FF = 5632
N_MOD = 9
HG_CHUNK = 32
GD_CHUNK = 64
CONV_K = 3
NORM_EPS = 1e-6
RW_GN_EPS = 64e-5
GATE_FLOOR = 1e-30
IN_SPLITS = (
    ('hg_q', BRANCH_W), ('hg_i', BRANCH_W), ('hg_f', 2 * BRANCH_W), ('hg_g', BRANCH_W),
    ('gd_qkv', 3 * BRANCH_W), ('gd_z', BRANCH_W), ('gd_a', 2 * GD_H), ('gd_b', 2 * GD_H),
    ('rw_rkv', 3 * BRANCH_W), ('rw_wd', 2 * RW_LORA_W), ('rw_ad', 2 * RW_LORA_A), ('rw_gd', RW_LORA_G),
    ('merge', 3 * D_MODEL),
)
IN_WIDTH = sum(size for _, size in IN_SPLITS)

kernel_name = 'hybrid_bidir_flow_trunk_step'


def _rms_norm(x, g):
    xf = x.astype(jnp.float32)
    y = xf * lax.rsqrt(jnp.mean(xf * xf, axis=-1, keepdims=True) + NORM_EPS)
    return (y * g.astype(jnp.float32)).astype(x.dtype)


def _l2_normalize(x):
    xf = x.astype(jnp.float32)
    return xf * lax.rsqrt(jnp.sum(xf * xf, axis=-1, keepdims=True) + NORM_EPS)


def _group_norm(x, g, b):
    xf = x.astype(jnp.float32)
    mu = jnp.mean(xf, axis=-1, keepdims=True)
    var = jnp.mean(jnp.square(xf - mu), axis=-1, keepdims=True)
    y = (xf - mu) * lax.rsqrt(var + RW_GN_EPS)
    return y * g.reshape(x.shape[-2:]).astype(jnp.float32) + b.reshape(x.shape[-2:]).astype(jnp.float32)


def _heads(t, d):
    return t.reshape(t.shape[:-1] + (t.shape[-1] // d, d))


def _split_cols(proj):
    cols, start = {}, 0
    for name, size in IN_SPLITS:
        cols[name] = proj[..., start:start + size]
        start += size
    return cols


def _swiglu(h, w_up, w_down):
    a, b = jnp.split(h @ w_up, 2, axis=-1)
    return (jax.nn.silu(a) * b) @ w_down


def _masked_decay(rel, mask):
    return jnp.where(mask, jnp.exp(jnp.minimum(rel, 0.0)), 0.0)


def _short_conv(x, w, on_grid):
    B, L, C = x.shape
    w = w.astype(x.dtype)
    if on_grid:
        rows = L // GRID_W
        y = lax.conv_general_dilated(x.reshape(B, rows, GRID_W, C), w[:, :, None, :], (1, 1), 'SAME',
                                     dimension_numbers=('NHWC', 'HWIO', 'NHWC'), feature_group_count=C)
        return y.reshape(B, L, C)
    return lax.conv_general_dilated(x, w[CONV_K // 2][:, None, :], (1,), 'SAME',
                                    dimension_numbers=('NWC', 'WIO', 'NWC'), feature_group_count=C)


def _gla_chunked(q, k, v, log_f, s0):
    f32 = jnp.float32
    B, L, H, dk = q.shape
    dv = v.shape[-1]
    n = L // HG_CHUNK

    def blocks(t):
        return t.astype(f32).reshape(B, n, HG_CHUNK, H, t.shape[-1]).transpose(1, 0, 3, 2, 4)

    qb, kb, vb = blocks(q), blocks(k), blocks(v)
    cum = jnp.cumsum(blocks(log_f), axis=3)
    incl = jnp.tril(jnp.ones((HG_CHUNK, HG_CHUNK), bool))

    def step(s, inp):
        qc, kc, vc, bc = inp
        last = bc[:, :, -1:, :]
        inter = jnp.einsum('bhtd,bhde->bhte', qc * jnp.exp(bc), s)
        rel = bc[:, :, :, None, :] - bc[:, :, None, :, :]
        decay = _masked_decay(rel, incl[:, :, None])
        scores = jnp.sum(qc[:, :, :, None, :] * kc[:, :, None, :, :] * decay, axis=-1)
        intra = jnp.einsum('bhts,bhse->bhte', scores, vc)
        s_new = jnp.exp(last[:, :, 0, :])[..., None] * s + jnp.einsum('bhsd,bhse->bhde', kc * jnp.exp(last - bc), vc)
        return s_new, inter + intra

    s_fin, ob = lax.scan(step, s0.astype(f32), (qb, kb, vb, cum))
    return ob.transpose(1, 0, 3, 2, 4).reshape(B, L, H, dv), s_fin


def _gated_delta_chunked(q, k, v, log_g, beta, s0):
    f32 = jnp.float32
    B, L, H, dk = q.shape
    dv = v.shape[-1]
    n = L // GD_CHUNK

    def blocks(t):
        return t.astype(f32).reshape(B, n, GD_CHUNK, H, t.shape[-1]).transpose(1, 0, 3, 2, 4)

    def scal(t):
        return t.astype(f32).reshape(B, n, GD_CHUNK, H).transpose(1, 0, 3, 2)

    qb, kb, vb = blocks(q), blocks(k), blocks(v)
    bb = scal(beta)
    G = jnp.cumsum(scal(log_g), axis=-1)
    rel = G[..., :, None] - G[..., None, :]
    strict = jnp.tril(jnp.ones((GD_CHUNK, GD_CHUNK), bool), -1)
    incl = jnp.tril(jnp.ones((GD_CHUNK, GD_CHUNK), bool))
    dec_strict = _masked_decay(rel, strict)
    dec_incl = _masked_decay(rel, incl)
    a_mat = jnp.eye(GD_CHUNK, dtype=f32) + bb[..., :, None] * jnp.einsum('nbhid,nbhjd->nbhij', kb, kb) * dec_strict
    w = lax.linalg.triangular_solve(a_mat, (bb * jnp.exp(G))[..., None] * kb,
                                    left_side=True, lower=True, unit_diagonal=True)
    u = lax.linalg.triangular_solve(a_mat, bb[..., None] * vb, left_side=True, lower=True, unit_diagonal=True)
    p_mat = jnp.einsum('nbhid,nbhjd->nbhij', qb, kb) * dec_incl
    q_dec = qb * jnp.exp(G)[..., None]
    k_dec = kb * jnp.exp(G[..., -1:] - G)[..., None]
    g_last = jnp.exp(G[..., -1])

    def step(s, inp):
        q_c, k_c, w_c, u_c, p_c, gl = inp
        v_new = u_c - jnp.einsum('bhcd,bhde->bhce', w_c, s)
        o = jnp.einsum('bhcd,bhde->bhce', q_c, s) + jnp.einsum('bhij,bhje->bhie', p_c, v_new)
        s = gl[..., None, None] * s + jnp.einsum('bhcd,bhce->bhde', k_c, v_new)
        return s, o

    s_fin, ob = lax.scan(step, s0.astype(f32), (q_dec, k_dec, w, u, p_mat, g_last))
    return ob.transpose(1, 0, 3, 2, 4).reshape(B, L, H, dv), s_fin


def _rwkv7_scan(r, decay, k, v, a, b, s0):
    f32 = jnp.float32

    def step(s, inp):
        r_t, w_t, k_t, v_t, a_t, b_t = inp
        sa = jnp.einsum('bhij,bhj->bhi', s, a_t)
        s = s * w_t[:, :, None, :] + sa[..., None] * b_t[:, :, None, :] + v_t[..., None] * k_t[:, :, None, :]
        return s, jnp.einsum('bhij,bhj->bhi', s, r_t)

    xs = tuple(t.astype(f32).transpose(1, 0, 2, 3) for t in (r, decay, k, v, a, b))
    s_fin, y = lax.scan(step, s0.astype(f32), xs)
    return y.transpose(1, 0, 2, 3), s_fin


def _bidir(scan_fn, fwd_args, bwd_args, s0):
    o_f, s_f = scan_fn(*fwd_args, s0[:, 0])
    o_b, s_b = scan_fn(*[jnp.flip(t, axis=1) for t in bwd_args], s0[:, 1])
    return o_f + jnp.flip(o_b, axis=1), jnp.stack([s_f, s_b], axis=1)


def _token_mix(h, p, s0, on_grid):
    f32 = jnp.float32
    B, L, _ = h.shape
    cols = _split_cols(h @ p['mix_in'])

    hg_q = _heads(jax.nn.silu(cols['hg_q'].astype(f32)) * HG_DK ** -0.5, HG_DK)
    hg_v = _heads(cols['hg_i'].astype(f32), HG_DK)
    f_logit = cols['hg_f'].astype(f32).reshape(B, L, 2, BRANCH_W)
    lb = p['hgrn_lb']
    f_gate = lb + (1.0 - lb) * jax.nn.sigmoid(f_logit)
    log_f = jnp.log(jnp.maximum(f_gate, GATE_FLOOR))
    hg_k = (1.0 - lb) * jax.nn.sigmoid(-f_logit)
    o_hg, s_hg = _bidir(_gla_chunked,
                        (hg_q, _heads(hg_k[:, :, 0], HG_DK), hg_v, _heads(log_f[:, :, 0], HG_DK)),
                        (hg_q, _heads(hg_k[:, :, 1], HG_DK), hg_v, _heads(log_f[:, :, 1], HG_DK)),
                        s0[0])
    o_hg = _rms_norm(o_hg, p['hgrn_norm'].reshape(HG_H, HG_DK)).reshape(B, L, BRANCH_W)
    o_hg = o_hg * jax.nn.silu(cols['hg_g'].astype(f32))

    qkv = jax.nn.silu(_short_conv(cols['gd_qkv'], p['gdn_conv'], on_grid).astype(f32))
    gq, gk, gv = jnp.split(qkv, 3, axis=-1)
    gq = _l2_normalize(_heads(gq, GD_DK)) * GD_DK ** -0.5
    gk = _l2_normalize(_heads(gk, GD_DK))
    gv = _heads(gv, GD_DK)
    a_in = cols['gd_a'].astype(f32).reshape(B, L, 2, GD_H)
    beta = jax.nn.sigmoid(cols['gd_b'].astype(f32).reshape(B, L, 2, GD_H))
    log_g = -jnp.exp(p['gdn_a_log'].astype(f32)) * jax.nn.softplus(a_in + p['gdn_dt_bias'].astype(f32))
    o_gd, s_gd = _bidir(_gated_delta_chunked,
                        (gq, gk, gv, log_g[:, :, 0], beta[:, :, 0]),
                        (gq, gk, gv, log_g[:, :, 1], beta[:, :, 1]),
                        s0[1])
    o_gd = _rms_norm(o_gd, p['gdn_norm']).reshape(B, L, BRANCH_W) * jax.nn.silu(cols['gd_z'].astype(f32))

    rr, rk, rv = jnp.split(cols['rw_rkv'].astype(f32), 3, axis=-1)
    w_low = jnp.tanh(cols['rw_wd'].astype(f32).reshape(B, L, 2, RW_LORA_W))
    a_low = cols['rw_ad'].astype(f32).reshape(B, L, 2, RW_LORA_A)
    log_w = -jax.nn.softplus(-(p['rwkv_w0'] + jnp.einsum('bldr,drc->bldc', w_low, p['rwkv_w2']))) - 0.5
    decay = jnp.exp(-jnp.exp(log_w)).reshape(B, L, 2, RW_H, RW_DK)
    icl = jax.nn.sigmoid(p['rwkv_a0'] + jnp.einsum('bldr,drc->bldc', a_low, p['rwkv_a2']))
    out_gate = jax.nn.sigmoid(cols['rw_gd'].astype(f32)) @ p['rwkv_g2'].astype(f32)
    kk = _l2_normalize(_heads(rk * p['rwkv_kk'], RW_DK))
    k_dir = (rk[:, :, None, :] * (1.0 + (icl - 1.0) * p['rwkv_ka'])).reshape(B, L, 2, RW_H, RW_DK)
    icl_h = icl.reshape(B, L, 2, RW_H, RW_DK)
    r_h, v_h = _heads(rr, RW_DK), _heads(rv, RW_DK)
    o_rw, s_rw = _bidir(_rwkv7_scan,
                        (r_h, decay[:, :, 0], k_dir[:, :, 0], v_h, -kk, kk * icl_h[:, :, 0]),
                        (r_h, decay[:, :, 1], k_dir[:, :, 1], v_h, -kk, kk * icl_h[:, :, 1]),
                        s0[2])
    bonus = jnp.sum(r_h[:, :, None] * k_dir * p['rwkv_rk'], axis=(2, 4))[..., None] * v_h
    o_rw = (_group_norm(o_rw, p['rwkv_ln_g'], p['rwkv_ln_b']) + bonus).reshape(B, L, BRANCH_W) * out_gate

    gates = jax.nn.sigmoid(cols['merge'].astype(f32)).reshape(B, L, 3, D_MODEL)
    wb = p['mix_branch']
    merged = (gates[:, :, 0] * (o_hg.astype(h.dtype) @ wb[0])
              + gates[:, :, 1] * (o_gd.astype(h.dtype) @ wb[1])
              + gates[:, :, 2] * (o_rw.astype(h.dtype) @ wb[2]))
    return merged.astype(h.dtype) @ p['mix_out'], (s_hg, s_gd, s_rw)


def _layer(x, cond, p, s0, on_grid):
    mod = (jax.nn.silu(cond) @ p['mod_w'] + p['mod_b'])[:, None, :]
    sh1, sc1, g1, sh2, sc2, g2, sh3, sc3, g3 = jnp.split(mod, N_MOD, axis=-1)
    h = _rms_norm(x, p['norm_g'][0]) * (1.0 + sc1) + sh1
    x = x + 0.5 * g1 * _swiglu(h, p['ffn_up'][0], p['ffn_down'][0])
    h = _rms_norm(x, p['norm_g'][1]) * (1.0 + sc2) + sh2
    mixed, states = _token_mix(h, p, s0, on_grid)
    x = x + g2 * mixed
    h = _rms_norm(x, p['norm_g'][2]) * (1.0 + sc3) + sh3
    x = x + 0.5 * g3 * _swiglu(h, p['ffn_up'][1], p['ffn_down'][1])
    return x, states


def setup_inputs(seed: int = 0) -> dict:
    key = jax.random.key(seed)
    ks = iter(jax.random.split(key, 40))
    f32 = jnp.float32

    def nrm(shape, scale):
        return scale * jax.random.normal(next(ks), shape, f32)

    def unif(shape, lo, hi):
        return jax.random.uniform(next(ks), shape, f32, lo, hi)

    D = D_MODEL
    x_prompt = nrm((BATCH, SEQ, D), 1.0)
    x_sample = nrm((DEC_BATCH, DEC_SEQ, D), 1.0)
    state_hgrn = nrm((DEC_BATCH, DEPTH, 2, HG_H, HG_DK, HG_DK), 0.3)
    state_gdn = nrm((DEC_BATCH, DEPTH, 2, GD_H, GD_DK, GD_DK), 0.1)
    state_rwkv = nrm((DEC_BATCH, DEPTH, 2, RW_H, RW_DK, RW_DK), 0.3)
    c = nrm((DEC_BATCH, D), 1.0)
    c_ctx = nrm((D,), 1.0)
    mod_w = nrm((DEPTH, D, N_MOD * D), 0.5 * D ** -0.5)
    mod_b = nrm((DEPTH, N_MOD * D), 0.02)
    norm_g = 1.0 + nrm((DEPTH, 3, D), 0.02)
    ffn_up = nrm((DEPTH, 2, D, 2 * D_FF), D ** -0.5)
    ffn_down = nrm((DEPTH, 2, D_FF, D), D_FF ** -0.5)
    mix_in = nrm((DEPTH, D, IN_WIDTH), D ** -0.5)
    hgrn_lb = nrm((DEPTH, 2, BRANCH_W), 1.0)
    hgrn_norm = 1.0 + nrm((DEPTH, BRANCH_W), 0.02)
    gdn_conv = nrm((DEPTH, CONV_K, CONV_K, 3 * BRANCH_W), 1.0 / CONV_K)
    gdn_a_log = jnp.log(unif((DEPTH, 2, GD_H), 1.0, 16.0))
    dt = jnp.exp(unif((DEPTH, 2, GD_H), math.log(1e-3), math.log(1e-1)))
    gdn_dt_bias = dt + jnp.log(-jnp.expm1(-dt))
    gdn_norm = 1.0 + nrm((DEPTH, GD_DK), 0.02)
    rwkv_w0 = unif((DEPTH, 2, BRANCH_W), -6.0, 1.0)
    rwkv_w2 = nrm((DEPTH, 2, RW_LORA_W, BRANCH_W), 0.5 * RW_LORA_W ** -0.5)
    rwkv_a0 = nrm((DEPTH, 2, BRANCH_W), 0.3)
    rwkv_a2 = nrm((DEPTH, 2, RW_LORA_A, BRANCH_W), RW_LORA_A ** -0.5)
    rwkv_g2 = nrm((DEPTH, RW_LORA_G, BRANCH_W), RW_LORA_G ** -0.5)
    rwkv_kk = 0.85 + nrm((DEPTH, BRANCH_W), 0.05)
    rwkv_ka = 1.0 + nrm((DEPTH, BRANCH_W), 0.05)
    rwkv_rk = nrm((DEPTH, RW_H, RW_DK), 0.1)
    rwkv_ln_g = 1.0 + nrm((DEPTH, BRANCH_W), 0.02)
    rwkv_ln_b = nrm((DEPTH, BRANCH_W), 0.02)
    mix_branch = nrm((DEPTH, 3, BRANCH_W, D), BRANCH_W ** -0.5)
    mix_out = nrm((DEPTH, D, D), D ** -0.5)
    final_norm = 1.0 + nrm((D,), 0.02)
    return {'x_prompt': x_prompt, 'x_sample': x_sample, 'state_hgrn': state_hgrn, 'state_gdn': state_gdn,
            'state_rwkv': state_rwkv, 'c': c, 'c_ctx': c_ctx, 'mod_w': mod_w, 'mod_b': mod_b, 'norm_g': norm_g,
            'ffn_up': ffn_up, 'ffn_down': ffn_down, 'mix_in': mix_in, 'hgrn_lb': hgrn_lb, 'hgrn_norm': hgrn_norm,
            'gdn_conv': gdn_conv, 'gdn_a_log': gdn_a_log, 'gdn_dt_bias': gdn_dt_bias, 'gdn_norm': gdn_norm,
            'rwkv_w0': rwkv_w0, 'rwkv_w2': rwkv_w2, 'rwkv_a0': rwkv_a0, 'rwkv_a2': rwkv_a2, 'rwkv_g2': rwkv_g2,
            'rwkv_kk': rwkv_kk, 'rwkv_ka': rwkv_ka, 'rwkv_rk': rwkv_rk, 'rwkv_ln_g': rwkv_ln_g,
            'rwkv_ln_b': rwkv_ln_b, 'mix_branch': mix_branch, 'mix_out': mix_out, 'final_norm': final_norm}


def reference(x_prompt, x_sample, state_hgrn, state_gdn, state_rwkv, c, c_ctx, mod_w, mod_b, norm_g,
              ffn_up, ffn_down, mix_in, hgrn_lb, hgrn_norm, gdn_conv, gdn_a_log, gdn_dt_bias, gdn_norm,
              rwkv_w0, rwkv_w2, rwkv_a0, rwkv_a2, rwkv_g2, rwkv_kk, rwkv_ka, rwkv_rk, rwkv_ln_g, rwkv_ln_b,
              mix_branch, mix_out, final_norm):
    f32 = jnp.float32
    lb_p = jax.nn.softmax(hgrn_lb.astype(f32), axis=0)
    lb_all = jnp.cumsum(lb_p, axis=0) - lb_p[0:1]
    bp = x_prompt.shape[0]
    ctx_s0 = (jnp.zeros((bp, 2, HG_H, HG_DK, HG_DK), f32),
              jnp.zeros((bp, 2, GD_H, GD_DK, GD_DK), f32),
              jnp.zeros((bp, 2, RW_H, RW_DK, RW_DK), f32))
    xp, xs = x_prompt, x_sample
    new_hg, new_gd, new_rw = [], [], []
    for l in range(DEPTH):
        p = {'mod_w': mod_w[l], 'mod_b': mod_b[l], 'norm_g': norm_g[l], 'ffn_up': ffn_up[l],
             'ffn_down': ffn_down[l], 'mix_in': mix_in[l], 'hgrn_lb': lb_all[l], 'hgrn_norm': hgrn_norm[l],
             'gdn_conv': gdn_conv[l], 'gdn_a_log': gdn_a_log[l], 'gdn_dt_bias': gdn_dt_bias[l],
             'gdn_norm': gdn_norm[l], 'rwkv_w0': rwkv_w0[l], 'rwkv_w2': rwkv_w2[l], 'rwkv_a0': rwkv_a0[l],
             'rwkv_a2': rwkv_a2[l], 'rwkv_g2': rwkv_g2[l], 'rwkv_kk': rwkv_kk[l], 'rwkv_ka': rwkv_ka[l],
             'rwkv_rk': rwkv_rk[l], 'rwkv_ln_g': rwkv_ln_g[l], 'rwkv_ln_b': rwkv_ln_b[l],
             'mix_branch': mix_branch[l], 'mix_out': mix_out[l]}
        xp, (s_hg, s_gd, s_rw) = _layer(xp, c_ctx[None, :], p, ctx_s0, False)
        xs, _ = _layer(xs, c, p, (state_hgrn[:, l], state_gdn[:, l], state_rwkv[:, l]), True)
        new_hg.append(s_hg)
        new_gd.append(s_gd)
        new_rw.append(s_rw)
    y_prompt = _rms_norm(xp, final_norm)
    y_sample = _rms_norm(xs, final_norm)
    return (y_prompt, y_sample, jnp.stack(new_hg, axis=1), jnp.stack(new_gd, axis=1), jnp.stack(new_rw, axis=1))
```

```python
import numpy as np
from contextlib import ExitStack
import concourse.bass as bass
import concourse.mybir as mybir
from concourse.bass_utils import run_bass_kernel_spmd

F32 = mybir.dt.float32
BF16 = mybir.dt.bfloat16
AF = mybir.ActivationFunctionType
ALU = mybir.AluOpType
import os as _os
RWL = int(_os.environ.get('RWL', '99'))

D = 2048
KC = D // 128
DFF = 5632
DEPTH = 4
NMOD = 9
BW = 512
EPS = 1e-6
IN_SPLITS = (('hg_q', 512), ('hg_i', 512), ('hg_f', 1024), ('hg_g', 512), ('gd_qkv', 1536), ('gd_z', 512),
             ('gd_a', 8), ('gd_b', 8), ('rw_rkv', 1536), ('rw_wd', 128), ('rw_ad', 128), ('rw_gd', 128),
             ('merge', 6144))
OFF = {}
_o = 0
for _n, _s in IN_SPLITS:
    OFF[_n] = _o
    _o += _s
INW = _o


class Buf:
    __slots__ = ('name', 'w', 'r')

    def __init__(self, name):
        self.name = name
        self.w = {}
        self.r = {}


class T:
    def __init__(self, h, name, buf=None, excl=False):
        self.h = h
        self.b = buf if buf is not None else Buf(name)
        self.excl = excl

    def __getitem__(self, k):
        return self.h[k]


class Ctx:
    def __init__(self, nc, es):
        self.nc = nc
        self.es = es
        self.eng = {'pe': nc.tensor, 'dve': nc.vector, 'act': nc.scalar, 'pool': nc.gpsimd, 'sp': nc.sync}
        self.sem = {k: es.enter_context(nc.semaphore('e_' + k)) for k in self.eng}
        self.cnt = {k: 0 for k in self.eng}
        self.seen = {k: {} for k in self.eng}
        self.dsem = {}
        self.dtot = {}
        self.gmap = {}
        self.nins = 0

    def _wait(self, e, need):
        for k, v in need.items():
            if k == 'pe' and e == 'pe':
                continue
            if k[0] == 'd' and k[1] == ':':
                v = self.dtot[k]
                sem = self.dsem[k]
            else:
                sem = self.sem[k]
            if self.seen[e].get(k, 0) >= v:
                continue
            self.eng[e].wait_ge(sem, v)
            self.seen[e][k] = v

    @staticmethod
    def _deps(r, w):
        need = {}
        for t in r:
            for k, v in t.b.w.items():
                if need.get(k, 0) < v:
                    need[k] = v
        for t in w:
            for k, v in t.b.w.items():
                if need.get(k, 0) < v:
                    need[k] = v
            for k, v in t.b.r.items():
                if need.get(k, 0) < v:
                    need[k] = v
        return need

    @staticmethod
    def _record(ev, r, w):
        k, v = ev
        for t in r:
            t.b.r[k] = v
        for t in w:
            t.b.w = {k: v}
            t.b.r = {}

    def op(self, e, fn, r=(), w=()):
        ex = [t for t in r if t.excl]
        if ex:
            w = list(w) + ex
        self._wait(e, self._deps(r, w))
        ins = fn()
        self.cnt[e] += 1
        ins.then_inc(self.sem[e], 1)
        self._record((e, self.cnt[e]), r, w)
        self.nins += 1
        return ins

    MAXDSEM = 40

    def dma(self, q, g, out, in_, r=(), w=(), **kw):
        if g not in self.gmap:
            if len(self.dsem) < self.MAXDSEM:
                key = 'd:%d' % len(self.dsem)
                self.dsem[key] = self.es.enter_context(self.nc.semaphore('d_%d' % len(self.dsem)))
                self.dtot[key] = 0
            else:
                key = 'd:%d' % (8 + (len(self.gmap) % (self.MAXDSEM - 8)))
            self.gmap[g] = key
        key = self.gmap[g]
        self._wait(q, self._deps(r, w))
        ins = self.eng[q].dma_start(out=out, in_=in_, **kw)
        self.dtot[key] += 16
        ins.then_inc(self.dsem[key], 16)
        self._record((key, self.dtot[key]), r, w)
        self.nins += 1
        return ins

    def barrier(self):
        need = {k: v for k, v in self.cnt.items() if v > 0}
        need.update({k: v for k, v in self.dtot.items() if v > 0})
        for e in self.eng:
            self._wait(e, dict(need))


def host_consts():
    c = {}
    c['ident'] = np.eye(128, dtype=np.float32)
    c['ones'] = np.ones((128, 128), np.float32)
    i = np.arange(128)
    s, t = i[:, None], i[None, :]
    c['bones64'] = ((s // 64) == (t // 64)).astype(np.float32)
    same32 = (s // 32) == (t // 32)
    c['hg_tri_f'] = (same32 & (s <= t)).astype(np.float32)
    c['hg_tri_b'] = (same32 & (s >= t)).astype(np.float32)
    c['hg_trx_f'] = (same32 & (s > t)).astype(np.float32)
    c['hg_trx_b'] = (same32 & (s < t)).astype(np.float32)
    c['hg_cm'] = np.repeat(((i[:, None] // 32) == np.arange(4)[None, :]).astype(np.float32), 32, axis=1)
    c['gd_tri_f'] = (s <= t).astype(np.float32)
    c['gd_tri_b'] = (s >= t).astype(np.float32)
    c['gd_neg_f'] = np.where(s <= t, 0.0, -30000.0).astype(np.float32)
    c['gd_neg_b'] = np.where(s >= t, 0.0, -30000.0).astype(np.float32)
    c['gd_st_f'] = (s < t).astype(np.float32)
    c['gd_st_b'] = (s > t).astype(np.float32)
    same64 = (s // 64) == (t // 64)
    c['rw_tri_f'] = (same64 & (s <= t)).astype(np.float32)
    c['rw_tri_b'] = (same64 & (s >= t)).astype(np.float32)
    c['rw_trx_f'] = (same64 & (s < t)).astype(np.float32)
    c['rw_trx_b'] = (same64 & (s > t)).astype(np.float32)
    ss, tt = s % 64, t % 64
    c['rw_bst_f'] = (same64 & (ss < tt)).astype(np.float32)
    c['rw_bst_b'] = (same64 & (ss > tt)).astype(np.float32)
    t64 = np.arange(64)[None, :]
    c['rw_sin_f'] = ((s % 64) <= t64).astype(np.float32)
    c['rw_sin_b'] = ((s % 64) >= t64).astype(np.float32)
    blk = np.zeros((128, 2, 2, 64), np.float32)
    blk[:64, :, 0, :] = 1.0
    blk[64:, :, 1, :] = 1.0
    c['rw_blk'] = blk.reshape(128, 256)
    names, cols, off = [], {}, 0
    arrs = []
    for k, v in c.items():
        cols[k] = (off, v.shape[1])
        off += v.shape[1]
        arrs.append(v)
    return np.concatenate(arrs, axis=1), cols


CONST_ARR, CONST_COLS = host_consts()


class Builder:
    def __init__(self, NP=4, LP=256, LS=2048, depth=DEPTH, do_mix=True, debug_outs=(),
                 mix_parts=('hgrn', 'gdn', 'rwkv', 'merge'), wdepth=DEPTH):
        self.wdepth = wdepth
        self.NP, self.LP, self.LS, self.depth, self.do_mix = NP, LP, LS, depth, do_mix
        self.NTP = NP * LP
        self.NT = self.NTP + LS
        self.debug_outs = debug_outs
        self.mix_parts = mix_parts
        self.nc = bass.Bass("TRN2", target_bir_lowering=False)
        self.es = ExitStack()
        self.C = Ctx(self.nc, self.es)
        self.declare()

    def sb(self, st, name, shape, dt=F32):
        self.uid = getattr(self, 'uid', 0) + 1
        return T(st.enter_context(self.nc.sbuf_tensor('%s_u%d' % (name, self.uid), list(shape), dt)), name)

    def din(self, name, shape, dt=F32):
        return T(self.nc.dram_tensor(name, list(shape), dt, kind="ExternalInput").ap(), name)

    def dout(self, name, shape, dt=F32):
        return T(self.nc.dram_tensor(name, list(shape), dt, kind="ExternalOutput").ap(), name)

    def dscr(self, name, shape, dt=F32):
        kind = "ExternalOutput" if name in self.debug_outs else None
        if kind:
            return T(self.nc.dram_tensor(name, list(shape), dt, kind=kind).ap(), name)
        return T(self.nc.dram_tensor(name, list(shape), dt).ap(), name)

    def declare(self):
        NP, LP, LS, NT, dp = self.NP, self.LP, self.LS, self.NT, self.depth
        I = {}
        I['x_prompt'] = self.din('x_prompt', [NP * LP, D])
        I['x_sample'] = self.din('x_sample', [LS, D])
        I['state_hgrn'] = self.din('state_hgrn', [DEPTH, 2, 4, 128, 128])
        I['state_gdn'] = self.din('state_gdn', [DEPTH, 2, 4, 128, 128])
        I['state_rwkv'] = self.din('state_rwkv', [DEPTH, 2, 8, 64, 64])
        I['cond'] = self.din('cond', [2, D])
        I['mod_w'] = self.din('mod_w', [self.wdepth, D, NMOD * D])
        I['mod_b'] = self.din('mod_b', [DEPTH, NMOD * D])
        I['norm_g'] = self.din('norm_g', [DEPTH, 3, D])
        I['ffn_up'] = self.din('ffn_up', [self.wdepth, 2, D, 2 * DFF])
        I['ffn_down'] = self.din('ffn_down', [self.wdepth, 2, DFF, D])
        I['mix_in'] = self.din('mix_in', [self.wdepth, D, INW])
        I['hgrn_lb'] = self.din('hgrn_lb', [DEPTH, 2, BW])
        I['hgrn_norm'] = self.din('hgrn_norm', [DEPTH, BW])
        I['gdn_conv'] = self.din('gdn_conv', [DEPTH, 3, 3, 1536])
        I['gdn_a_log'] = self.din('gdn_a_log', [DEPTH, 2, 4])
        I['gdn_dt_bias'] = self.din('gdn_dt_bias', [DEPTH, 2, 4])
        I['gdn_norm'] = self.din('gdn_norm', [DEPTH, 128])
        I['rwkv_w0'] = self.din('rwkv_w0', [DEPTH, 2, BW])
        I['rwkv_w2'] = self.din('rwkv_w2', [DEPTH, 2, 64, BW])
        I['rwkv_a0'] = self.din('rwkv_a0', [DEPTH, 2, BW])
        I['rwkv_a2'] = self.din('rwkv_a2', [DEPTH, 2, 64, BW])
        I['rwkv_g2'] = self.din('rwkv_g2', [DEPTH, 128, BW])
        I['rwkv_kk'] = self.din('rwkv_kk', [DEPTH, BW])
        I['rwkv_ka'] = self.din('rwkv_ka', [DEPTH, BW])
        I['rwkv_rk'] = self.din('rwkv_rk', [DEPTH, BW])
        I['rwkv_ln_g'] = self.din('rwkv_ln_g', [DEPTH, BW])
        I['rwkv_ln_b'] = self.din('rwkv_ln_b', [DEPTH, BW])
        I['mix_branch'] = self.din('mix_branch', [self.wdepth, 3, BW, D])
        I['mix_out'] = self.din('mix_out', [self.wdepth, D, D])
        I['final_norm'] = self.din('final_norm', [D])
        I['consts'] = self.din('consts', list(CONST_ARR.shape))
        self.I = I
        O = {}
        O['y_prompt'] = self.dout('y_prompt', [NP * LP, D])
        O['y_sample'] = self.dout('y_sample', [LS, D])
        O['ns_hgrn'] = self.dout('ns_hgrn', [NP, DEPTH, 2, 4, 128, 128])
        O['ns_gdn'] = self.dout('ns_gdn', [NP, DEPTH, 2, 4, 128, 128])
        O['ns_rwkv'] = self.dout('ns_rwkv', [NP, DEPTH, 2, 8, 64, 64])
        self.O = O
        S = {}
        S['xT'] = self.dscr('xT', [D, NT])
        self.S = S

    def setup(self):
        C, nc, es = self.C, self.nc, self.es
        self.cst = self.sb(es, 'cst', CONST_ARR.shape)
        C.dma('sp', 'cst', self.cst[:], self.I['consts'][:, :], r=[self.I['consts']], w=[self.cst])
        self.ps = [T(es.enter_context(nc.psum_tensor('ps%d' % i, [128, 512], F32)), 'ps%d' % i, excl=True)
                   for i in range(8)]
        self.onesD = self.sb(es, 'onesD', [128, 128])
        C.op('dve', lambda: nc.vector.tensor_scalar(self.onesD[:], self.cc('ones'), 1.0 / D, None, ALU.mult),
             r=[self.cst], w=[self.onesD])
        self.ones128 = self.sb(es, 'ones128', [128, 128])
        C.op('dve', lambda: nc.vector.tensor_scalar(self.ones128[:], self.cc('ones'), 1.0 / 128, None, ALU.mult),
             r=[self.cst], w=[self.ones128])
        self.epsc = self.sb(es, 'epsc', [128, 1])
        C.op('dve', lambda: nc.vector.memset(self.epsc[:], EPS), w=[self.epsc])
        self.stg = self.sb(es, 'fmstg', [128, 128])
        self.P = {}
        dp = DEPTH
        self.P['norm_g'] = self.load_fm('norm_g', self.I['norm_g'], "l j (c p) -> (l j c) p", dp * 3 * KC)
        self.P['final_norm'] = self.load_fm('final_norm', self.I['final_norm'], "(c p) -> c p", KC)
        self.P['mod_b'] = self.load_fm('mod_b', self.I['mod_b'], "l (c p) -> (l c) p", dp * 144)
        cf = self.load_fm('condf', self.I['cond'], "r (c p) -> (r c) p", 2 * KC)
        self.cond = self.sb(es, 'condb', [128, KC, 2], BF16)
        C.op('act', lambda: nc.scalar.activation(self.cond[:].rearrange("p c r -> p r c"),
                                                 cf[:].rearrange("p (r c) -> p r c", r=2), AF.Silu),
             r=[cf], w=[self.cond])
        self.modFM = self.sb(es, 'modFM', [128, 144, 2])
        self.scl = self.sb(es, 'scl', [128, 3, KC, 2])
        self.gat = self.sb(es, 'gat', [128, 3, KC, 2])

    def rsqrt(self, out_t, out_ap, in_t, in_ap, eps_t, scale=1.0):
        C, nc = self.C, self.nc
        npart = out_ap.shape[0]
        C.op('act', lambda: nc.scalar.activation(out_ap, in_ap, AF.Ln, bias=eps_t[0:npart, :], scale=scale),
             r=[in_t, eps_t], w=[out_t])
        C.op('act', lambda: nc.scalar.activation(out_ap, out_ap, AF.Exp, scale=-0.5), r=[out_t], w=[out_t])

    def cc(self, name, c0=0, n=None):
        o, w = CONST_COLS[name]
        if n is None:
            n = w
        return self.cst[:, o + c0:o + c0 + n]

    def load_fm(self, name, src, pattern, nrows):
        C, nc = self.C, self.nc
        dst = self.sb(self.es, 'P_' + name, [128, nrows])
        view = src[:].rearrange(pattern, p=128)
        for r0 in range(0, nrows, 128):
            n = min(128, nrows - r0)
            C.dma('sp', 'fmstg', self.stg[0:n, :], view[r0:r0 + n, :], r=[src], w=[self.stg])
            ps = self.ps[0]
            C.op('pe', lambda: nc.tensor.matmul(ps[:, 0:n], self.stg[0:n, :], self.cc('ident')[0:n, 0:n],
                                                start=True, stop=True), r=[self.stg, self.cst], w=[ps])
            C.op('dve', lambda: nc.vector.tensor_copy(dst[:, r0:r0 + n], ps[:, 0:n]), r=[ps], w=[dst])
        return dst

    def wload(self, wt, dst_ap, src_t, src_ap):
        self.C.dma('pool', wt.b.name, dst_ap, src_ap, r=[src_t], w=[wt])

    def mod_layer(self, l, st):
        C, nc = self.C, self.nc
        ps = self.ps[1]
        mw = self.I['mod_w']
        for nb in range(36):
            wt = self.wA[nb % 2]
            self.wload(wt, wt[:], mw, mw[l, :, nb * 512:(nb + 1) * 512].rearrange("(kc p) n -> p kc n", p=128))
            for c4 in range(4):
                ch = nb * 4 + c4
                for kc in range(KC):
                    C.op('pe', lambda: nc.tensor.matmul(ps[:, ch * 2:ch * 2 + 2], wt[:, kc, c4 * 128:(c4 + 1) * 128],
                                                        self.cond[:, kc, :], start=(kc == 0), stop=(kc == KC - 1)),
                         r=[wt, self.cond], w=[ps])
        mb = self.P['mod_b']
        C.op('dve', lambda: nc.vector.tensor_tensor(
            self.modFM[:], ps[:, 0:288].rearrange("p (c r) -> p c r", r=2),
            mb[:, l * 144:(l + 1) * 144].unsqueeze(2).to_broadcast([128, 144, 2]), ALU.add),
            r=[ps, mb], w=[self.modFM])
        ng = self.P['norm_g']
        for j in range(3):
            sc = self.modFM[:, (3 * j + 1) * KC:(3 * j + 2) * KC, :]
            gt = self.modFM[:, (3 * j + 2) * KC:(3 * j + 3) * KC, :]
            ngj = ng[:, (l * 3 + j) * KC:(l * 3 + j + 1) * KC].unsqueeze(2).to_broadcast([128, KC, 2])
            C.op('dve', lambda: nc.vector.scalar_tensor_tensor(self.scl[:, j], sc, 1.0, ngj, ALU.add, ALU.mult),
                 r=[self.modFM, ng], w=[self.scl])
            C.op('dve', lambda: nc.vector.tensor_scalar(self.gat[:, j], gt, 0.5 if j != 1 else 1.0, None, ALU.mult),
                 r=[self.modFM], w=[self.gat])

    def shift(self, j, kc, ci):
        return self.modFM[:, 3 * j * KC + kc, ci:ci + 1]

    def load_norm(self, tok0, TT, j, ci, xt, hT, sq, rstd, tmp):
        C, nc = self.C, self.nc
        xs = self.S['xT']
        C.dma('sp', 'xt', xt[:, :, 0:TT], xs[:, tok0:tok0 + TT].rearrange("(kc p) t -> p kc t", p=128),
              r=[xs], w=[xt])
        ps = self.ps[7]
        for kc in range(KC):
            s = sq[kc % 2]
            C.op('act', lambda: nc.scalar.activation(s[:, 0:TT], xt[:, kc, 0:TT], AF.Square), r=[xt], w=[s])
            C.op('pe', lambda: nc.tensor.matmul(ps[:, 0:TT], self.onesD[:], s[:, 0:TT], start=(kc == 0),
                                                stop=(kc == KC - 1)), r=[self.onesD, s], w=[ps])
        self.rsqrt(rstd, rstd[:, 0:TT], ps, ps[:, 0:TT], self.epsc)
        if hT is None:
            return
        for kc in range(KC):
            tm = tmp[kc % 2]
            C.op('dve', lambda: nc.vector.scalar_tensor_tensor(tm[:, 0:TT], xt[:, kc, 0:TT], self.scl[:, j, kc, ci:ci + 1],
                                                               rstd[:, 0:TT], ALU.mult, ALU.mult),
                 r=[xt, self.scl, rstd], w=[tm])
            C.op('act', lambda: nc.scalar.activation(hT[:, kc, 0:TT], tm[:, 0:TT], AF.Identity,
                                                     bias=self.shift(j, kc, ci), scale=1.0),
                 r=[tm, self.modFM], w=[hT])

    def store_x(self, tok0, TT, xt):
        xs = self.S['xT']
        self.C.dma('sp', 'xt', xs[:, tok0:tok0 + TT].rearrange("(kc p) t -> p kc t", p=128), xt[:, :, 0:TT],
                   r=[xt], w=[xs])

    def token_tiles(self):
        out = []
        for t0 in range(0, self.NTP, 512):
            out.append((t0, min(512, self.NTP - t0), 0))
        for t0 in range(self.NTP, self.NT, 512):
            out.append((t0, min(512, self.NT - t0), 1))
        return out

    def stage_x0(self):
        C, nc = self.C, self.nc
        with ExitStack() as st:
            xr = [self.sb(st, 'x0r%d' % i, [128, D]) for i in range(2)]
            xo = [self.sb(st, 'x0o%d' % i, [128, KC, 128]) for i in range(2)]
            xs = self.S['xT']
            for ti in range(self.NT // 128):
                tok0 = ti * 128
                if tok0 < self.NTP:
                    src_t, src = self.I['x_prompt'], self.I['x_prompt'][tok0:tok0 + 128, :]
                else:
                    src_t, src = self.I['x_sample'], self.I['x_sample'][tok0 - self.NTP:tok0 - self.NTP + 128, :]
                a, o = xr[ti % 2], xo[ti % 2]
                C.dma('sp', a.b.name, a[:], src, r=[src_t], w=[a])
                for g in range(4):
                    ps = self.ps[(ti % 2) * 4 + g]
                    for q in range(4):
                        kc = g * 4 + q
                        C.op('pe', lambda: nc.tensor.matmul(ps[:, q * 128:(q + 1) * 128], a[:, kc * 128:(kc + 1) * 128],
                                                            self.cc('ident'), start=True, stop=True),
                             r=[a, self.cst], w=[ps])
                    eng = 'dve' if g % 2 == 0 else 'act'
                    if eng == 'dve':
                        C.op('dve', lambda: nc.vector.tensor_copy(o[:, g * 4:(g + 1) * 4, :],
                                                                  ps[:].rearrange("p (q t) -> p q t", q=4)),
                             r=[ps], w=[o])
                    else:
                        C.op('act', lambda: nc.scalar.copy(o[:, g * 4:(g + 1) * 4, :],
                                                           ps[:].rearrange("p (q t) -> p q t", q=4)),
                             r=[ps], w=[o])
                C.dma('sp', o.b.name, xs[:, tok0:tok0 + 128].rearrange("(kc p) t -> p kc t", p=128), o[:],
                      r=[o], w=[xs])
            C.barrier()

    def stage_ffn(self, l, j):
        C, nc = self.C, self.nc
        fi = 0 if j == 0 else 1
        up, dn = self.I['ffn_up'], self.I['ffn_down']
        with ExitStack() as st:
            xt = self.sb(st, 'xt', [128, KC, 512])
            hT = self.sb(st, 'hT', [128, KC, 512], BF16)
            sq = [self.sb(st, 'sq%d' % i, [128, 512]) for i in range(2)]
            tmp = [self.sb(st, 'tmp%d' % i, [128, 512]) for i in range(2)]
            rstd = self.sb(st, 'rstd', [128, 512])
            act = self.sb(st, 'act', [128, 44, 512], BF16)
            sa = [self.sb(st, 'sa%d' % i, [128, 512]) for i in range(2)]
            wB = [self.sb(st, 'wB%d' % i, [128, 11, 512], BF16) for i in range(2)]
            nA = nB = 0
            for (tok0, TT, ci) in self.token_tiles():
                self.load_norm(tok0, TT, j, ci, xt, hT, sq, rstd, tmp)
                for fb in range(22):
                    wt = self.wA[nA % 2]
                    nA += 1
                    self.wload(wt, wt[:, :, 0:256], up,
                               up[l, fi, :, fb * 256:(fb + 1) * 256].rearrange("(kc p) n -> p kc n", p=128))
                    self.wload(wt, wt[:, :, 256:512], up,
                               up[l, fi, :, DFF + fb * 256:DFF + (fb + 1) * 256].rearrange("(kc p) n -> p kc n", p=128))
                    for c2 in range(2):
                        k = (fb * 2 + c2) % 2
                        pa, pb = self.ps[k * 2], self.ps[k * 2 + 1]
                        for half, pp in ((0, pa), (1, pb)):
                            for kc in range(KC):
                                C.op('pe', lambda: nc.tensor.matmul(
                                    pp[:, 0:TT], wt[:, kc, half * 256 + c2 * 128:half * 256 + (c2 + 1) * 128],
                                    hT[:, kc, 0:TT], start=(kc == 0), stop=(kc == KC - 1)), r=[wt, hT], w=[pp])
                        s = sa[k]
                        C.op('act', lambda: nc.scalar.activation(s[:, 0:TT], pa[:, 0:TT], AF.Silu), r=[pa], w=[s])
                        C.op('dve', lambda: nc.vector.tensor_tensor(act[:, fb * 2 + c2, 0:TT], s[:, 0:TT], pb[:, 0:TT],
                                                                    ALU.mult), r=[s, pb], w=[act])
                for ob in range(4):
                    for qk in range(4):
                        wt = wB[nB % 2]
                        nB += 1
                        self.wload(wt, wt[:], dn,
                                   dn[l, fi, qk * 1408:(qk + 1) * 1408, ob * 512:(ob + 1) * 512]
                                   .rearrange("(kc p) n -> p kc n", p=128))
                        for oc in range(4):
                            pp = self.ps[4 + oc]
                            for kk in range(11):
                                C.op('pe', lambda: nc.tensor.matmul(
                                    pp[:, 0:TT], wt[:, kk, oc * 128:(oc + 1) * 128], act[:, qk * 11 + kk, 0:TT],
                                    start=(qk == 0 and kk == 0), stop=(qk == 3 and kk == 10)), r=[wt, act], w=[pp])
                    for oc in range(4):
                        pp = self.ps[4 + oc]
                        kc = ob * 4 + oc
                        C.op('dve', lambda: nc.vector.scalar_tensor_tensor(
                            xt[:, kc, 0:TT], pp[:, 0:TT], self.gat[:, j, kc, ci:ci + 1], xt[:, kc, 0:TT],
                            ALU.mult, ALU.add), r=[pp, self.gat, xt], w=[xt])
                self.store_x(tok0, TT, xt)
            C.barrier()

    PF = {'hg_q': 0, 'hg_k': 512, 'hg_g': 1536, 'gd_qkv': 2048, 'gd_z': 3584, 'rw_rkv': 4096, 'rw_wd': 5632,
          'rw_ad': 5760, 'rw_gd': 5888, 'merge': 6016}
    PF_ROWS = 12160
    PT = {'v': 0, 'logf': 512, 'k': 1536, 'lg': 2560, 'beta': 2568}
    PT_COLS = 2576

    def seqs(self):
        out = [(i * self.LP, self.LP, 'p', i) for i in range(self.NP)]
        out.append((self.NTP, self.LS, 's', 0))
        return out

    def softmax_cum(self, x, e, L, R):
        C, nc = self.C, self.nc
        C.op('act', lambda: nc.scalar.activation(e[:], x[:], AF.Exp), r=[x], w=[e])
        C.op('dve', lambda: nc.vector.tensor_tensor(x[:, 0], e[:, 0], e[:, 1], ALU.add), r=[e], w=[x])
        for l in range(2, L):
            C.op('dve', lambda: nc.vector.tensor_tensor(x[:, 0], x[:, 0], e[:, l], ALU.add), r=[e, x], w=[x])
        C.op('dve', lambda: nc.vector.reciprocal(x[:, 0], x[:, 0]), r=[x], w=[x])
        for l in range(1, L):
            C.op('dve', lambda: nc.vector.tensor_tensor(e[:, l], e[:, l], x[:, 0], ALU.mult), r=[e, x], w=[e])
        C.op('dve', lambda: nc.vector.memset(x[:, 0], 0.0), w=[x])
        for l in range(1, L):
            C.op('dve', lambda: nc.vector.tensor_tensor(x[:, l], x[:, l - 1], e[:, l], ALU.add), r=[e, x], w=[x])

    def bcast_load(self, name, src_t, flat_ap, n):
        t = self.sb(self.es, name, [128, n])
        self.C.dma('sp', name, t[:], flat_ap.partition_broadcast(128), r=[src_t], w=[t])
        return t

    def setup_mix(self):
        C, nc, es, I = self.C, self.nc, self.es, self.I
        dp = DEPTH
        lbf = self.load_fm('lbf', I['hgrn_lb'], "l d (c p) -> (l d c) p", dp * 8)
        ef = self.sb(es, 'lbf_e', [128, dp * 8])
        self.softmax_cum(T(lbf[:].rearrange("p (l r) -> p l r", l=dp), 'lbfv', buf=lbf.b),
                         T(ef[:].rearrange("p (l r) -> p l r", l=dp), 'lbfe', buf=ef.b), dp, 8)
        self.lbf = lbf
        self.omlbf = self.sb(es, 'omlbf', [128, dp * 8])
        C.op('dve', lambda: nc.vector.tensor_scalar(self.omlbf[:], lbf[:], -1.0, 1.0, ALU.mult, ALU.add),
             r=[lbf], w=[self.omlbf])
        lbr = self.bcast_load('lbr', I['hgrn_lb'], I['hgrn_lb'][:].rearrange("l d c -> (l d c)"), dp * 1024)
        with ExitStack() as st_:
            er = self.sb(st_, 'lbr_e', [128, dp * 1024])
            self.softmax_cum(T(lbr[:].rearrange("p (l r) -> p l r", l=dp), 'lbrv', buf=lbr.b),
                             T(er[:].rearrange("p (l r) -> p l r", l=dp), 'lbre', buf=er.b), dp, 1024)
            C.barrier()
        self.lbr = lbr
        self.negA = self.bcast_load('negA', I['gdn_a_log'], I['gdn_a_log'][:].rearrange("l d h -> (l d h)"), dp * 8)
        C.op('act', lambda: nc.scalar.activation(self.negA[:], self.negA[:], AF.Exp), r=[self.negA], w=[self.negA])
        C.op('dve', lambda: nc.vector.tensor_scalar(self.negA[:], self.negA[:], -1.0, None, ALU.mult),
             r=[self.negA], w=[self.negA])
        self.dtb = self.bcast_load('dtb', I['gdn_dt_bias'], I['gdn_dt_bias'][:].rearrange("l d h -> (l d h)"), dp * 8)
        self.onec = self.sb(es, 'onec', [128, 1])
        C.op('dve', lambda: nc.vector.memset(self.onec[:], 1.0), w=[self.onec])
        self.eps128 = self.sb(es, 'eps128', [128, 1])
        C.op('dve', lambda: nc.vector.memset(self.eps128[:], 128.0 * EPS), w=[self.eps128])
        self.epsgn = self.sb(es, 'epsgn', [128, 1])
        C.op('dve', lambda: nc.vector.memset(self.epsgn[:], 64e-5), w=[self.epsgn])
        P = self.P
        P['hgrn_norm'] = self.load_fm('hgrn_norm', I['hgrn_norm'], "l (c p) -> (l c) p", dp * 4)
        P['gdn_conv'] = self.load_fm('gdn_conv', I['gdn_conv'], "l a b (c p) -> (l a b c) p", dp * 108)
        P['gdn_norm'] = self.load_fm('gdn_norm', I['gdn_norm'], "l p -> l p", dp)
        P['rwkv_a0'] = self.load_fm('rwkv_a0', I['rwkv_a0'], "l d (c p) -> (l d c) p", dp * 8)
        for k in ('rwkv_kk', 'rwkv_ka', 'rwkv_rk', 'rwkv_ln_g', 'rwkv_ln_b'):
            P[k] = self.load_fm(k, I[k], "l (c p) -> (l c) p", dp * 4)
        self.pq = [T(self.ps[i][:, j * 128:(j + 1) * 128], 'pq%d_%d' % (i, j), buf=self.ps[i].b, excl=True)
                   for j in range(4) for i in range(8)]
        self.pql = self.pq[-4:]
        self.pq = self.pq[:-4]
        self.pqi = 0
        self.pqli = 0
        NT = self.NT
        self.S['pf'] = self.dscr('pf', [self.PF_ROWS, NT])
        self.S['pt'] = self.dscr('pt', [NT, self.PT_COLS])
        self.S['ob'] = self.dscr('ob', [1536, NT])
        self.S['br'] = self.dscr('br', [1536, NT])

    def pnl(self):
        self.pqli = (self.pqli + 1) % len(self.pql)
        return self.pql[self.pqli]

    def pn(self):
        self.pqi = (self.pqi + 1) % len(self.pq)
        return self.pq[self.pqi]

    def stage_mixproj(self, l):
        C, nc = self.C, self.nc
        mi = self.I['mix_in']
        pf, pt = self.S['pf'], self.S['pt']
        PF, PT = self.PF, self.PT
        fm_blocks = [(0, 512, PF['hg_q'], ['silu'] * 4), (1024, 512, PF['hg_k'], ['hgk0'] * 4),
                     (1536, 512, PF['hg_k'] + 512, ['hgk1'] * 4), (2048, 512, PF['hg_g'], ['silu'] * 4)]
        fm_blocks += [(2560 + 512 * i, 512, PF['gd_qkv'] + 512 * i, ['copy'] * 4) for i in range(3)]
        fm_blocks += [(4096, 512, PF['gd_z'], ['silu'] * 4)]
        fm_blocks += [(4624 + 512 * i, 512, PF['rw_rkv'] + 512 * i, ['copy'] * 4) for i in range(3)]
        fm_blocks += [(6160, 384, PF['rw_wd'], ['tanh', 'copy', 'sigmoid'])]
        fm_blocks += [(6544 + 512 * i, 512, PF['merge'] + 512 * i, ['sigmoid'] * 4) for i in range(12)]
        tm_blocks = [(512, 512, 'v'), (1024, 512, 'f0'), (1536, 512, 'f1'), (4608, 16, 'ab')]
        funcs = {'silu': AF.Silu, 'copy': AF.Copy, 'tanh': AF.Tanh, 'sigmoid': AF.Sigmoid, 'hgk0': AF.Sigmoid,
                 'hgk1': AF.Sigmoid}
        with ExitStack() as st:
            xt = self.sb(st, 'xt', [128, KC, 512])
            hT = self.sb(st, 'hT', [128, KC, 512], BF16)
            sq = [self.sb(st, 'sq%d' % i, [128, 512]) for i in range(2)]
            tmp = [self.sb(st, 'tmp%d' % i, [128, 512]) for i in range(2)]
            rstd = self.sb(st, 'rstd', [128, 512])
            stg = [self.sb(st, 'stg%d' % i, [128, 512]) for i in range(3)]
            tst = [self.sb(st, 'tst%d' % i, [128, 1024]) for i in range(2)]
            omlbr = self.sb(st, 'omlbr', [128, 1024])
            C.op('dve', lambda: nc.vector.tensor_scalar(omlbr[:], self.lbr[:, l * 1024:(l + 1) * 1024], -1.0, 1.0,
                                                        ALU.mult, ALU.add), r=[self.lbr], w=[omlbr])
            nA = ns = nt = 0
            for (tok0, TT, ci) in self.token_tiles():
                self.load_norm(tok0, TT, 1, ci, xt, hT, sq, rstd, tmp)
                for (c0, ncol, prow, kinds) in fm_blocks:
                    wt = self.wA[nA % 2]
                    nA += 1
                    self.wload(wt, wt[:, :, 0:ncol], mi, mi[l, :, c0:c0 + ncol].rearrange("(kc p) n -> p kc n", p=128))
                    for ch, kind in enumerate(kinds):
                        pp = self.ps[ns % 4]
                        sg = stg[ns % 3]
                        ns += 1
                        for kc in range(KC):
                            C.op('pe', lambda: nc.tensor.matmul(pp[:, 0:TT], wt[:, kc, ch * 128:(ch + 1) * 128],
                                                                hT[:, kc, 0:TT], start=(kc == 0), stop=(kc == KC - 1)),
                                 r=[wt, hT], w=[pp])
                        C.op('act', lambda: nc.scalar.activation(sg[:, 0:TT], pp[:, 0:TT], funcs[kind]), r=[pp], w=[sg])
                        if kind in ('hgk0', 'hgk1'):
                            d = int(kind[-1])
                            col = l * 8 + d * 4 + ch
                            C.op('dve', lambda: nc.vector.tensor_scalar(sg[:, 0:TT], sg[:, 0:TT], -1.0, 1.0,
                                                                        ALU.mult, ALU.add), r=[sg], w=[sg])
                            C.op('dve', lambda: nc.vector.tensor_scalar(sg[:, 0:TT], sg[:, 0:TT],
                                                                        self.omlbf[:, col:col + 1], None, ALU.mult),
                                 r=[sg, self.omlbf], w=[sg])
                        r0 = prow + ch * 128
                        C.dma('sp', sg.b.name, pf[r0:r0 + 128, tok0:tok0 + TT], sg[:, 0:TT], r=[sg], w=[pf])
                for (c0, ncol, kind) in tm_blocks:
                    wt = self.wA[nA % 2]
                    nA += 1
                    self.wload(wt, wt[:, :, 0:ncol], mi, mi[l, :, c0:c0 + ncol].rearrange("(kc p) n -> p kc n", p=128))
                    for sub in range(TT // 128):
                        pp = self.ps[4 + nt % 3]
                        ts_ = tst[nt % 2]
                        nt += 1
                        t0 = tok0 + sub * 128
                        for kc in range(KC):
                            C.op('pe', lambda: nc.tensor.matmul(pp[:, 0:ncol], hT[:, kc, sub * 128:(sub + 1) * 128],
                                                                wt[:, kc, 0:ncol], start=(kc == 0), stop=(kc == KC - 1)),
                                 r=[wt, hT], w=[pp])
                        if kind == 'v':
                            C.op('act', lambda: nc.scalar.copy(ts_[:, 0:512], pp[:]), r=[pp], w=[ts_])
                            C.dma('sp', ts_.b.name, pt[t0:t0 + 128, PT['v']:PT['v'] + 512], ts_[:, 0:512], r=[ts_], w=[pt])
                        elif kind in ('f0', 'f1'):
                            d = int(kind[-1])
                            o = l * 1024 + d * 512
                            f, k = ts_[:, 0:512], ts_[:, 512:1024]
                            C.op('act', lambda: nc.scalar.activation(f, pp[:], AF.Sigmoid), r=[pp], w=[ts_])
                            C.op('dve', lambda: nc.vector.tensor_tensor(f, f, omlbr[:, d * 512:(d + 1) * 512], ALU.mult),
                                 r=[ts_, omlbr], w=[ts_])
                            C.op('dve', lambda: nc.vector.tensor_tensor(f, f, self.lbr[:, o:o + 512], ALU.add),
                                 r=[ts_, self.lbr], w=[ts_])
                            C.op('dve', lambda: nc.vector.tensor_scalar(k, f, -1.0, 1.0, ALU.mult, ALU.add),
                                 r=[ts_], w=[ts_])
                            C.op('dve', lambda: nc.vector.tensor_scalar(f, f, 1e-30, None, ALU.max), r=[ts_], w=[ts_])
                            C.op('act', lambda: nc.scalar.activation(f, f, AF.Ln), r=[ts_], w=[ts_])
                            C.dma('sp', ts_.b.name, pt[t0:t0 + 128, PT['logf'] + d * 512:PT['logf'] + (d + 1) * 512], f,
                                  r=[ts_], w=[pt])
                            C.dma('sp', ts_.b.name, pt[t0:t0 + 128, PT['k'] + d * 512:PT['k'] + (d + 1) * 512], k,
                                  r=[ts_], w=[pt])
                        else:
                            a, b = ts_[:, 0:8], ts_[:, 8:16]
                            C.op('dve', lambda: nc.vector.tensor_tensor(a, pp[:, 0:8], self.dtb[:, l * 8:l * 8 + 8], ALU.add),
                                 r=[pp, self.dtb], w=[ts_])
                            C.op('act', lambda: nc.scalar.activation(a, a, AF.Exp), r=[ts_], w=[ts_])
                            C.op('act', lambda: nc.scalar.activation(a, a, AF.Ln, bias=self.onec[:], scale=1.0),
                                 r=[ts_, self.onec], w=[ts_])
                            C.op('dve', lambda: nc.vector.tensor_tensor(a, a, self.negA[:, l * 8:l * 8 + 8], ALU.mult),
                                 r=[ts_, self.negA], w=[ts_])
                            C.op('act', lambda: nc.scalar.activation(b, pp[:, 8:16], AF.Sigmoid), r=[pp], w=[ts_])
                            C.dma('sp', ts_.b.name, pt[t0:t0 + 128, PT['lg']:PT['lg'] + 16], ts_[:, 0:16], r=[ts_], w=[pt])
            C.barrier()

    def mm(self, pt_, out_ap, lt, lhsT, rt, rhs, start=True, stop=True):
        nc = self.nc
        return self.C.op('pe', lambda: nc.tensor.matmul(out_ap, lhsT, rhs, start=start, stop=stop),
                         r=[lt, rt], w=[pt_])

    def vtt(self, ot, out, at, a, bt, b, op, eng='dve'):
        nc = self.nc
        e = nc.vector if eng == 'dve' else nc.gpsimd
        return self.C.op(eng, lambda: e.tensor_tensor(out, a, b, op), r=[at, bt], w=[ot])

    def act(self, ot, out, it, in_, func, bias=None, scale=1.0, extra=()):
        nc = self.nc
        if bias is None:
            return self.C.op('act', lambda: nc.scalar.activation(out, in_, func, scale=scale), r=[it], w=[ot])
        return self.C.op('act', lambda: nc.scalar.activation(out, in_, func, bias=bias, scale=scale),
                         r=[it] + list(extra), w=[ot])

    def stage_hgrn(self, l):
        C, nc = self.C, self.nc
        pf, pt, ob, br = self.S['pf'], self.S['pt'], self.S['ob'], self.S['br']
        PF, PT = self.PF, self.PT
        cst = self.cst
        seqs = self.seqs()
        with ExitStack() as st:
            Sst = {(si, h): self.sb(st, 'hS%d_%d' % (si, h), [128, 128]) for si in range(len(seqs)) for h in range(4)}
            NB = 3
            qT4 = [self.sb(st, 'hq%d' % i, [128, 4, 128]) for i in range(NB)]
            kT4 = [self.sb(st, 'hk%d' % i, [128, 4, 128]) for i in range(NB)]
            kt4 = [self.sb(st, 'hkt%d' % i, [128, 512]) for i in range(NB)]
            lf4 = [self.sb(st, 'hlf%d' % i, [128, 512]) for i in range(NB)]
            vt4 = [self.sb(st, 'hv%d' % i, [128, 512]) for i in range(NB)]
            ob4 = [self.sb(st, 'hob%d' % i, [128, 4, 128]) for i in range(NB)]
            gT4 = [self.sb(st, 'hg%d' % i, [128, 4, 128]) for i in range(NB)]
            oo4 = [self.sb(st, 'hoo%d' % i, [128, 4, 128]) for i in range(NB)]
            R = 4
            ebT = [self.sb(st, 'heb%d' % i, [128, 128]) for i in range(R)]
            enb = [self.sb(st, 'henb%d' % i, [128, 128]) for i in range(R)]
            qeT = [self.sb(st, 'hqe%d' % i, [128, 128]) for i in range(R)]
            keT = [self.sb(st, 'hke%d' % i, [128, 128]) for i in range(R)]
            elm = [self.sb(st, 'helm%d' % i, [128, 128]) for i in range(R)]
            kl4 = [self.sb(st, 'hkl%d' % i, [128, 4, 128]) for i in range(R)]
            scm = [self.sb(st, 'hscm%d' % i, [128, 128]) for i in range(R)]
            sqt = [self.sb(st, 'hsq%d' % i, [128, 128]) for i in range(R)]
            rst = [self.sb(st, 'hrs%d' % i, [128, 128]) for i in range(R)]
            hn = self.P['hgrn_norm']
            u = 0
            nld = 0
            for d in (1, 0):
                tri = 'hg_tri_f' if d == 0 else 'hg_tri_b'
                trx = 'hg_trx_f' if d == 0 else 'hg_trx_b'
                for si, (tok0, L, kind, idx) in enumerate(seqs):
                    for h in range(4):
                        S = Sst[(si, h)]
                        if kind == 's':
                            C.dma('sp', S.b.name, S[:], self.I['state_hgrn'][l, d, h], r=[self.I['state_hgrn']], w=[S])
                        else:
                            C.op('dve', lambda: nc.vector.memset(S[:], 0.0), w=[S])
                maxt = max(L // 128 for (_, L, _, _) in seqs)
                for step in range(maxt):
                    for si, (tok0, L, kind, idx) in enumerate(seqs):
                        ntile = L // 128
                        if step >= ntile:
                            continue
                        ti = step if d == 0 else ntile - 1 - step
                        tok = tok0 + ti * 128
                        b = nld % NB
                        nld += 1
                        q4, k4, kt, lf, vt = qT4[b], kT4[b], kt4[b], lf4[b], vt4[b]
                        C.dma('sp', q4.b.name, q4[:], pf[PF['hg_q']:PF['hg_q'] + 512, tok:tok + 128]
                              .rearrange("(h p) t -> p h t", p=128), r=[pf], w=[q4])
                        r0 = PF['hg_k'] + d * 512
                        C.dma('sp', k4.b.name, k4[:], pf[r0:r0 + 512, tok:tok + 128].rearrange("(h p) t -> p h t", p=128),
                              r=[pf], w=[k4])
                        c0 = PT['k'] + d * 512
                        C.dma('sp', kt.b.name, kt[:], pt[tok:tok + 128, c0:c0 + 512], r=[pt], w=[kt])
                        c0 = PT['logf'] + d * 512
                        C.dma('sp', lf.b.name, lf[:], pt[tok:tok + 128, c0:c0 + 512], r=[pt], w=[lf])
                        C.dma('sp', vt.b.name, vt[:], pt[tok:tok + 128, PT['v']:PT['v'] + 512], r=[pt], w=[vt])
                        if d == 0:
                            o4, g4 = ob4[b], gT4[b]
                            C.dma('sp', o4.b.name, o4[:], ob[0:512, tok:tok + 128].rearrange("(h p) t -> p h t", p=128),
                                  r=[ob], w=[o4])
                            C.dma('sp', g4.b.name, g4[:], pf[PF['hg_g']:PF['hg_g'] + 512, tok:tok + 128]
                                  .rearrange("(h p) t -> p h t", p=128), r=[pf], w=[g4])
                        oo = oo4[b]
                        for h in range(4):
                            S = Sst[(si, h)]
                            r = u % R
                            u += 1
                            hs = slice(h * 128, (h + 1) * 128)
                            p_b, p_l, p_s, p_o = self.pn(), self.pn(), self.pn(), self.pn()
                            self.mm(p_b, p_b[:], lf, lf[:, hs], cst, self.cc(tri))
                            self.mm(p_l, p_l[:], cst, self.cc(trx), lf, lf[:, hs])
                            self.act(ebT[r], ebT[r][:], p_b, p_b[:], AF.Exp)
                            self.act(enb[r], enb[r][:], p_b, p_b[:], AF.Exp, scale=-1.0)
                            self.vtt(qeT[r], qeT[r][:], q4, q4[:, h, :], ebT[r], ebT[r][:], ALU.mult)
                            self.vtt(keT[r], keT[r][:], k4, k4[:, h, :], enb[r], enb[r][:], ALU.mult, eng='pool')
                            self.act(elm[r], elm[r][:], p_l, p_l[:], AF.Exp)
                            self.vtt(elm[r], elm[r][:], kt, kt[:, hs], elm[r], elm[r][:], ALU.mult)
                            C.op('pool', lambda: nc.gpsimd.tensor_tensor(
                                kl4[r][:], elm[r][:].unsqueeze(1).to_broadcast([128, 4, 128]),
                                self.cc('hg_cm').rearrange("p (c s) -> p c s", c=4)[:, :, 0:1].to_broadcast([128, 4, 128]),
                                ALU.mult), r=[elm[r], cst], w=[kl4[r]])
                            self.mm(p_s, p_s[:], keT[r], keT[r][:], qeT[r], qeT[r][:])
                            self.vtt(scm[r], scm[r][:], p_s, p_s[:], cst, self.cc(tri), ALU.mult)
                            self.mm(p_o, p_o[:], vt, vt[:, hs], scm[r], scm[r][:], start=True, stop=False)
                            order = range(4) if d == 0 else range(3, -1, -1)
                            for n_, c in enumerate(order):
                                cs = slice(c * 32, (c + 1) * 32)
                                self.mm(p_o, p_o[:, cs], S, S[:], qeT[r], qeT[r][:, cs], start=False, stop=(n_ == 3))
                                p_u = self.pn()
                                self.mm(p_u, p_u[:], kl4[r], kl4[r][:, c, :], vt, vt[:, hs])
                                col = c * 32 + 31 if d == 0 else c * 32
                                C.op('dve', lambda: nc.vector.scalar_tensor_tensor(
                                    S[:], S[:], ebT[r][:, col:col + 1], p_u[:], ALU.mult, ALU.add),
                                    r=[S, ebT[r], p_u], w=[S])
                            if d == 1:
                                C.op('act', lambda: nc.scalar.copy(oo[:, h, :], p_o[:]), r=[p_o], w=[oo])
                            else:
                                osum = qeT[r]
                                self.vtt(osum, osum[:], p_o, p_o[:], o4, o4[:, h, :], ALU.add)
                                self.act(sqt[r], sqt[r][:], osum, osum[:], AF.Square)
                                p_m = self.pn()
                                self.mm(p_m, p_m[:], self.ones128, self.ones128[:], sqt[r], sqt[r][:])
                                self.rsqrt(rst[r], rst[r][:], p_m, p_m[:], self.eps128)
                                C.op('dve', lambda: nc.vector.scalar_tensor_tensor(
                                    osum[:], osum[:], hn[:, l * 4 + h:l * 4 + h + 1], rst[r][:], ALU.mult, ALU.mult),
                                    r=[osum, hn, rst[r]], w=[osum])
                                self.vtt(oo, oo[:, h, :], osum, osum[:], g4, g4[:, h, :], ALU.mult)
                            if kind == 'p' and step == ntile - 1:
                                dst = self.O['ns_hgrn']
                                C.dma('sp', S.b.name, dst[idx, l, d, h], S[:], r=[S], w=[dst])
                        tgt = ob if d == 1 else br
                        C.dma('sp', oo.b.name, tgt[0:512, tok:tok + 128].rearrange("(h p) t -> p h t", p=128), oo[:],
                              r=[oo], w=[tgt])
            C.barrier()

    def stage_gdn_pre(self, l):
        C, nc = self.C, self.nc
        pf = self.S['pf']
        PF = self.PF
        if 'gt' not in self.S:
            self.S['gt'] = self.dscr('gt', [self.NT, 1024])
        gt = self.S['gt']
        cw = self.P['gdn_conv']
        with ExitStack() as st:
            LM = max(self.LP, self.LS)
            xin = [self.sb(st, 'gx%d' % i, [128, LM]) for i in range(2)]
            xo = [self.sb(st, 'gxo%d' % i, [128, LM]) for i in range(2)]
            sq = [self.sb(st, 'gsq%d' % i, [128, 512]) for i in range(2)]
            rs = [self.sb(st, 'grs%d' % i, [128, 512]) for i in range(2)]
            tr = [self.sb(st, 'gtr%d' % i, [128, 128]) for i in range(3)]
            n = nq = ntr = 0
            for (tok0, L, kind, idx) in self.seqs():
                if kind == 's':
                    rows, W, dys = L // 64, 64, (0, 1, 2)
                else:
                    rows, W, dys = 1, L, (1,)
                for cc in range(12):
                    xi, xo_ = xin[n % 2], xo[n % 2]
                    n += 1
                    r0 = PF['gd_qkv'] + cc * 128
                    C.dma('sp', xi.b.name, xi[:, 0:L], pf[r0:r0 + 128, tok0:tok0 + L], r=[pf], w=[xi])
                    wc = lambda dy, dx: cw[:, l * 108 + (dy * 3 + dx) * 12 + cc:l * 108 + (dy * 3 + dx) * 12 + cc + 1]
                    C.op('dve', lambda: nc.vector.tensor_scalar(xo_[:, 0:L], xi[:, 0:L], wc(1, 1), None, ALU.mult),
                         r=[xi, cw], w=[xo_])
                    x3 = xi[:, 0:L].rearrange("p (r w) -> p r w", w=W)
                    o3 = xo_[:, 0:L].rearrange("p (r w) -> p r w", w=W)
                    for dy in dys:
                        for dx in range(3):
                            if dy == 1 and dx == 1:
                                continue
                            oy, ox = dy - 1, dx - 1
                            ra, rb = max(0, -oy), rows - max(0, oy)
                            ca, cb = max(0, -ox), W - max(0, ox)
                            if rb <= ra:
                                continue
                            C.op('dve', lambda: nc.vector.scalar_tensor_tensor(
                                o3[:, ra:rb, ca:cb], x3[:, ra + oy:rb + oy, ca + ox:cb + ox], wc(dy, dx),
                                o3[:, ra:rb, ca:cb], ALU.mult, ALU.add), r=[xi, cw, xo_], w=[xo_])
                    self.act(xo_, xo_[:, 0:L], xo_, xo_[:, 0:L], AF.Silu)
                    if cc < 8:
                        for t0 in range(0, L, 512):
                            tt = min(512, L - t0)
                            s_, r_ = sq[nq % 2], rs[nq % 2]
                            pp = self.ps[nq % 2]
                            nq += 1
                            self.act(s_, s_[:, 0:tt], xo_, xo_[:, t0:t0 + tt], AF.Square)
                            self.mm(pp, pp[:, 0:tt], self.cst, self.cc('ones'), s_, s_[:, 0:tt])
                            self.rsqrt(r_, r_[:, 0:tt], pp, pp[:, 0:tt], self.epsc)
                            self.vtt(xo_, xo_[:, t0:t0 + tt], xo_, xo_[:, t0:t0 + tt], r_, r_[:, 0:tt], ALU.mult)
                    C.dma('sp', xo_.b.name, pf[r0:r0 + 128, tok0:tok0 + L], xo_[:, 0:L], r=[xo_], w=[pf])
                    if cc >= 4:
                        for t0 in range(0, L, 128):
                            p_ = self.pn()
                            t_ = tr[ntr % 3]
                            ntr += 1
                            self.mm(p_, p_[:], xo_, xo_[:, t0:t0 + 128], self.cst, self.cc('ident'))
                            C.op('act', lambda: nc.scalar.copy(t_[:], p_[:]), r=[p_], w=[t_])
                            c0 = (cc - 4) * 128
                            C.dma('sp', t_.b.name, gt[tok0 + t0:tok0 + t0 + 128, c0:c0 + 128], t_[:], r=[t_], w=[gt])
            C.barrier()

    def finalize_branch(self, p_o, o4h, gh, ncol_t, ncol, eps_t, out_t, out_ap, scr, sqt_, rst_, onesm):
        C, nc = self.C, self.nc
        self.vtt(scr, scr[:], p_o, p_o[:], o4h[0], o4h[1], ALU.add)
        self.act(sqt_, sqt_[:], scr, scr[:], AF.Square)
        p_m = self.pn()
        self.mm(p_m, p_m[:], onesm, onesm[:], sqt_, sqt_[:])
        self.rsqrt(rst_, rst_[:], p_m, p_m[:], eps_t)
        C.op('dve', lambda: nc.vector.scalar_tensor_tensor(scr[:], scr[:], ncol, rst_[:], ALU.mult, ALU.mult),
             r=[scr, ncol_t, rst_], w=[scr])
        self.vtt(out_t, out_ap, scr, scr[:], gh[0], gh[1], ALU.mult)

    def stage_gdn(self, l):
        C, nc = self.C, self.nc
        pf, pt, gt, ob, br = self.S['pf'], self.S['pt'], self.S['gt'], self.S['ob'], self.S['br']
        PF, PT = self.PF, self.PT
        cst = self.cst
        seqs = self.seqs()
        ident = self.cc('ident')
        with ExitStack() as st:
            Sst = {(si, h): self.sb(st, 'gS%d_%d' % (si, h), [128, 128]) for si in range(len(seqs)) for h in range(4)}
            NB = 3
            qT4 = [self.sb(st, 'gq%d' % i, [128, 4, 128]) for i in range(NB)]
            kT4 = [self.sb(st, 'gk%d' % i, [128, 4, 128]) for i in range(NB)]
            kv4 = [self.sb(st, 'gkv%d' % i, [128, 1024]) for i in range(NB)]
            ab4 = [self.sb(st, 'gab%d' % i, [128, 16]) for i in range(NB)]
            nb4 = [self.sb(st, 'gnb%d' % i, [128, 8]) for i in range(NB)]
            ob4 = [self.sb(st, 'gob%d' % i, [128, 4, 128]) for i in range(NB)]
            gT4 = [self.sb(st, 'gg%d' % i, [128, 4, 128]) for i in range(NB)]
            oo4 = [self.sb(st, 'goo%d' % i, [128, 4, 128]) for i in range(NB)]
            R = 3
            names = ['lgb', 'lgbn', 'DT', 'DTs', 'EG', 'PT', 'Ua', 'Ub', 'La', 'Lb', 'Pa', 'Pb', 'keg', 'qd', 'kd', 'X',
                     'vn', 'scr', 'sq', 'rs']
            W = {nm: [self.sb(st, 'g%s%d' % (nm, i), [128, 128]) for i in range(R)] for nm in names}
            sc = [self.sb(st, 'gsc%d' % i, [128, 4]) for i in range(R)]
            gn = self.P['gdn_norm']
            u = nld = 0
            for d in (1, 0):
                tri = self.cc('gd_tri_f' if d == 0 else 'gd_tri_b')
                neg = self.cc('gd_neg_f' if d == 0 else 'gd_neg_b')
                stm = self.cc('gd_st_f' if d == 0 else 'gd_st_b')
                for si, (tok0, L, kind, idx) in enumerate(seqs):
                    for h in range(4):
                        S = Sst[(si, h)]
                        if kind == 's':
                            C.dma('sp', S.b.name, S[:], self.I['state_gdn'][l, d, h], r=[self.I['state_gdn']], w=[S])
                        else:
                            C.op('dve', lambda: nc.vector.memset(S[:], 0.0), w=[S])
                maxt = max(L // 128 for (_, L, _, _) in seqs)
                for step in range(maxt):
                    for si, (tok0, L, kind, idx) in enumerate(seqs):
                        ntile = L // 128
                        if step >= ntile:
                            continue
                        ti = step if d == 0 else ntile - 1 - step
                        tok = tok0 + ti * 128
                        b = nld % NB
                        nld += 1
                        q4, k4, kv, ab, nb_ = qT4[b], kT4[b], kv4[b], ab4[b], nb4[b]
                        r0 = PF['gd_qkv']
                        C.dma('sp', q4.b.name, q4[:], pf[r0:r0 + 512, tok:tok + 128].rearrange("(h p) t -> p h t", p=128),
                              r=[pf], w=[q4])
                        C.dma('sp', k4.b.name, k4[:], pf[r0 + 512:r0 + 1024, tok:tok + 128]
                              .rearrange("(h p) t -> p h t", p=128), r=[pf], w=[k4])
                        C.dma('sp', kv.b.name, kv[:], gt[tok:tok + 128, :], r=[gt], w=[kv])
                        C.dma('sp', ab.b.name, ab[:], pt[tok:tok + 128, PT['lg']:PT['lg'] + 16], r=[pt], w=[ab])
                        C.op('dve', lambda: nc.vector.tensor_scalar(nb_[:], ab[:, 8:16], -1.0, None, ALU.mult),
                             r=[ab], w=[nb_])
                        if d == 0:
                            o4, g4 = ob4[b], gT4[b]
                            C.dma('sp', o4.b.name, o4[:], ob[512:1024, tok:tok + 128].rearrange("(h p) t -> p h t", p=128),
                                  r=[ob], w=[o4])
                            C.dma('sp', g4.b.name, g4[:], pf[PF['gd_z']:PF['gd_z'] + 512, tok:tok + 128]
                                  .rearrange("(h p) t -> p h t", p=128), r=[pf], w=[g4])
                        oo = oo4[b]
                        for h in range(4):
                            S = Sst[(si, h)]
                            r = u % R
                            u += 1
                            w = {nm: W[nm][r] for nm in names}
                            hs = slice(h * 128, (h + 1) * 128)
                            lgc = ab[:, d * 4 + h:d * 4 + h + 1]
                            bc = ab[:, 8 + d * 4 + h:8 + d * 4 + h + 1]
                            nbc = nb_[:, d * 4 + h:d * 4 + h + 1]
                            kT, qT = k4[:, h, :], q4[:, h, :]
                            ktok, vtok = kv[:, hs], kv[:, 512 + h * 128:512 + (h + 1) * 128]
                            C.op('dve', lambda: nc.vector.tensor_scalar(w['lgb'][:], self.cc('ones'), lgc, None, ALU.mult),
                                 r=[cst, ab], w=[w['lgb']])
                            C.op('pool', lambda: nc.gpsimd.tensor_scalar(w['lgbn'][:], w['lgb'][:], -1.0, None, ALU.mult),
                                 r=[w['lgb']], w=[w['lgbn']])
                            p_d, p_g, p_c, p_kk, p_qk = self.pn(), self.pn(), self.pn(), self.pn(), self.pn()
                            self.mm(p_d, p_d[:], w['lgb'], w['lgb'][:], cst, tri, True, False)
                            self.mm(p_d, p_d[:], cst, tri, w['lgbn'], w['lgbn'][:], False, False)
                            self.mm(p_d, p_d[:], cst, ident, cst, neg, False, True)
                            self.mm(p_g, p_g[:], w['lgb'], w['lgb'][:], cst, tri)
                            self.mm(p_c, p_c[:, 0:1], cst, tri, ab, lgc)
                            self.mm(p_c, p_c[:, 1:2], cst, self.cc('ones'), ab, lgc)
                            self.mm(p_kk, p_kk[:], k4, kT, k4, kT)
                            self.mm(p_qk, p_qk[:], k4, kT, q4, qT)
                            self.act(w['DT'], w['DT'][:], p_d, p_d[:], AF.Exp)
                            self.act(w['EG'], w['EG'][:], p_g, p_g[:], AF.Exp)
                            s_ = sc[r]
                            C.op('dve', lambda: nc.vector.tensor_copy(s_[:, 0:1], p_c[:, 1:2]), r=[p_c], w=[s_])
                            self.act(s_, s_[:, 1:2], p_c, p_c[:, 1:2], AF.Exp)
                            self.act(s_, s_[:, 2:3], p_c, p_c[:, 0:1], AF.Exp, bias=s_[:, 0:1], scale=-1.0, extra=[s_])
                            self.vtt(w['DTs'], w['DTs'][:], w['DT'], w['DT'][:], cst, stm, ALU.mult, eng='pool')
                            U, Lm, P_ = w['Ua'], w['La'], w['Pa']
                            U2, L2, P2 = w['Ub'], w['Lb'], w['Pb']
                            C.op('dve', lambda: nc.vector.scalar_tensor_tensor(U[:], p_kk[:], nbc, w['DTs'][:],
                                                                               ALU.mult, ALU.mult),
                                 r=[p_kk, nb_, w['DTs']], w=[U])
                            self.vtt(w['PT'], w['PT'][:], p_qk, p_qk[:], w['DT'], w['DT'][:], ALU.mult)
                            p_t = self.pn()
                            self.mm(p_t, p_t[:], U, U[:], cst, ident)
                            C.op('act', lambda: nc.scalar.copy(Lm[:], p_t[:]), r=[p_t], w=[Lm])
                            self.vtt(P_, P_[:], U, U[:], cst, ident, ALU.add, eng='pool')
                            for n_ in range(6):
                                p_u, p_l, p_p = self.pn(), self.pn(), self.pn()
                                self.mm(p_u, p_u[:], Lm, Lm[:], U, U[:])
                                self.mm(p_l, p_l[:], U, U[:], Lm, Lm[:])
                                C.op('act', lambda: nc.scalar.copy(U2[:], p_u[:]), r=[p_u], w=[U2])
                                C.op('dve', lambda: nc.vector.tensor_copy(L2[:], p_l[:]), r=[p_l], w=[L2])
                                self.mm(p_p, p_p[:], L2, L2[:], P_, P_[:])
                                self.vtt(P2, P2[:], p_p, p_p[:], P_, P_[:], ALU.add)
                                U, U2, Lm, L2, P_, P2 = U2, U, L2, Lm, P2, P_
                            self.vtt(w['keg'], w['keg'][:], k4, kT, w['EG'], w['EG'][:], ALU.mult, eng='pool')
                            self.vtt(w['qd'], w['qd'][:], q4, qT, w['EG'], w['EG'][:], ALU.mult, eng='pool')
                            C.op('dve', lambda: nc.vector.tensor_scalar(w['kd'][:], ktok, s_[:, 2:3], None, ALU.mult),
                                 r=[kv, s_], w=[w['kd']])
                            p_x, p_v, p_o, p_k = self.pn(), self.pn(), self.pn(), self.pn()
                            self.mm(p_x, p_x[:], w['keg'], w['keg'][:], S, S[:])
                            self.vtt(w['X'], w['X'][:], kv, vtok, p_x, p_x[:], ALU.subtract)
                            self.mm(p_v, p_v[:], P_, P_[:], w['X'], w['X'][:])
                            C.op('dve', lambda: nc.vector.tensor_scalar(w['vn'][:], p_v[:], bc, None, ALU.mult),
                                 r=[p_v, ab], w=[w['vn']])
                            self.mm(p_o, p_o[:], S, S[:], w['qd'], w['qd'][:], True, False)
                            self.mm(p_o, p_o[:], w['vn'], w['vn'][:], w['PT'], w['PT'][:], False, True)
                            self.mm(p_k, p_k[:], w['kd'], w['kd'][:], w['vn'], w['vn'][:])
                            C.op('dve', lambda: nc.vector.scalar_tensor_tensor(S[:], S[:], s_[:, 1:2], p_k[:],
                                                                               ALU.mult, ALU.add),
                                 r=[S, s_, p_k], w=[S])
                            if d == 1:
                                C.op('act', lambda: nc.scalar.copy(oo[:, h, :], p_o[:]), r=[p_o], w=[oo])
                            else:
                                self.finalize_branch(p_o, (o4, o4[:, h, :]), (g4, g4[:, h, :]), gn, gn[:, l:l + 1],
                                                     self.eps128, oo, oo[:, h, :], w['scr'], w['sq'], w['rs'],
                                                     self.ones128)
                            if kind == 'p' and step == ntile - 1:
                                dst = self.O['ns_gdn']
                                C.dma('sp', S.b.name, dst[idx, l, d, h], S[:], r=[S], w=[dst])
                        tgt = ob if d == 1 else br
                        C.dma('sp', oo.b.name, tgt[512:1024, tok:tok + 128].rearrange("(h p) t -> p h t", p=128), oo[:],
                              r=[oo], w=[tgt])
            C.barrier()

    def stage_rwkv(self, l):
        C, nc = self.C, self.nc
        pf, ob, br = self.S['pf'], self.S['ob'], self.S['br']
        if 'bb' not in self.S:
            self.S['bb'] = self.dscr('bb', [512, self.NT])
        bb = self.S['bb']
        PF = self.PF
        cst = self.cst
        ident = self.cc('ident')
        bones = self.cc('bones64')
        blk4 = self.cc('rw_blk').rearrange("p (c h s) -> p c h s", c=2, h=2)
        seqs = self.seqs()
        I = self.I
        CW = 0.6065306597126334
        P = self.P
        with ExitStack() as st:
            w2t = self.sb(st, 'rw2', [128, 512])
            a2t = self.sb(st, 'ra2', [128, 512])
            g2t = self.sb(st, 'rg2', [128, 512])
            w0r = self.sb(st, 'rw0r', [128, 1024])
            omka = self.sb(st, 'romka', [128, 4])
            C.dma('sp', 'rw2', w2t[:], I['rwkv_w2'][l].rearrange("d r c -> (d r) c"), r=[I['rwkv_w2']], w=[w2t])
            C.dma('sp', 'ra2', a2t[:], I['rwkv_a2'][l].rearrange("d r c -> (d r) c"), r=[I['rwkv_a2']], w=[a2t])
            C.dma('sp', 'rg2', g2t[:], I['rwkv_g2'][l], r=[I['rwkv_g2']], w=[g2t])
            C.dma('sp', 'rw0r', w0r[:], I['rwkv_w0'][l].rearrange("d c -> (d c)").partition_broadcast(128),
                  r=[I['rwkv_w0']], w=[w0r])
            ka = P['rwkv_ka']
            C.op('dve', lambda: nc.vector.tensor_scalar(omka[:], ka[:, l * 4:l * 4 + 4], -1.0, 1.0, ALU.mult, ALU.add),
                 r=[ka], w=[omka])
            Zst = {(si, p): self.sb(st, 'rZ%d_%d' % (si, p), [128, 128]) for si in range(len(seqs)) for p in range(4)}
            NB = 2
            zstg = [self.sb(st, 'rzs%d' % i, [128, 128]) for i in range(2)]
            r4 = [self.sb(st, 'rr%d' % i, [128, 4, 128]) for i in range(NB)]
            k4 = [self.sb(st, 'rk%d' % i, [128, 4, 128]) for i in range(NB)]
            v4 = [self.sb(st, 'rv%d' % i, [128, 4, 128]) for i in range(NB)]
            wl = [self.sb(st, 'rwl%d' % i, [128, 128]) for i in range(NB)]
            al = [self.sb(st, 'ral%d' % i, [128, 128]) for i in range(NB)]
            gd = [self.sb(st, 'rgd%d' % i, [128, 128]) for i in range(NB)]
            yb4 = [self.sb(st, 'ryb%d' % i, [128, 4, 128]) for i in range(NB)]
            bb4 = [self.sb(st, 'rbb%d' % i, [128, 4, 128]) for i in range(NB)]
            oo4 = [self.sb(st, 'roo%d' % i, [128, 4, 128]) for i in range(NB)]
            pb4 = [self.sb(st, 'rpb%d' % i, [128, 4, 128]) for i in range(NB)]
            R = 2
            n128 = ['kk', 'sq', 'rs', 'sz', 'Ep', 'Em', 'Ex', 'El', 'icl', 't1', 'kd', 'bT', 'rt', 'og', 'ys', 'yc', 'pr',
                    'X', 'Ub', 'M0', 'Aak', 'Ua', 'Ub2', 'La', 'Lb', 'Pa', 'Pb', 'Bbd', 'Kbd', 'Vbd0', 'Vbd1']
            W = {nm: [self.sb(st, 'r%s%d' % (nm, i), [128, 128]) for i in range(R)] for nm in n128}
            n256 = ['ve', 'at2', 'ate', 'bte', 'kte', 'Bhe', 'Khe', 'tmp']
            W2 = {nm: [self.sb(st, 'r%s%d' % (nm, i), [128, 256]) for i in range(R)] for nm in n256}
            n64 = ['Arb', 'Ark']
            W3 = {nm: [self.sb(st, 'r%s%d' % (nm, i), [128, 64]) for i in range(R)] for nm in n64}
            u = nld = 0

            def expand(dst, src_t, src_ap, eng='pool'):
                e = nc.gpsimd if eng == 'pool' else nc.vector
                C.op(eng, lambda: e.tensor_tensor(
                    dst[:].rearrange("p (c h s) -> p c h s", c=2, h=2),
                    src_ap.rearrange("p (c s) -> p c s", c=2).unsqueeze(2).to_broadcast([128, 2, 2, 64]),
                    blk4, ALU.mult), r=[src_t, cst], w=[dst])

            def ch(t2, c):
                return t2[:].rearrange("p (c m) -> p c m", c=2)[:, c, :]

            for d in (1, 0):
                f = (d == 0)
                tri = self.cc('rw_tri_f' if f else 'rw_tri_b')
                trx = self.cc('rw_trx_f' if f else 'rw_trx_b')
                tra = self.cc('rw_trx_b' if f else 'rw_trx_f')
                bst = self.cc('rw_bst_f' if f else 'rw_bst_b')
                sin = self.cc('rw_sin_f' if f else 'rw_sin_b')
                for si, (tok0, L, kind, idx) in enumerate(seqs):
                    for p in range(4):
                        Z = Zst[(si, p)]
                        if kind == 's':
                            zs = zstg[(si + p) % 2]
                            C.op('dve', lambda: nc.vector.memset(zs[:], 0.0), w=[zs])
                            for hp in range(2):
                                C.dma('sp', zs.b.name, zs[hp * 64:(hp + 1) * 64, hp * 64:(hp + 1) * 64],
                                      I['state_rwkv'][l, d, p * 2 + hp], r=[I['state_rwkv']], w=[zs])
                            p_ = self.pn()
                            self.mm(p_, p_[:], zs, zs[:], cst, ident)
                            C.op('dve', lambda: nc.vector.tensor_copy(Z[:], p_[:]), r=[p_], w=[Z])
                        else:
                            C.op('dve', lambda: nc.vector.memset(Z[:], 0.0), w=[Z])
                maxt = max(L // 128 for (_, L, _, _) in seqs)
                for step in range(maxt):
                    for si, (tok0, L, kind, idx) in enumerate(seqs):
                        ntile = L // 128
                        if step >= ntile:
                            continue
                        ti = step if f else ntile - 1 - step
                        tok = tok0 + ti * 128
                        b = nld % NB
                        nld += 1
                        rr, kk_, vv, wl_, al_, gd_ = r4[b], k4[b], v4[b], wl[b], al[b], gd[b]
                        r0 = PF['rw_rkv']
                        for j_, t_ in enumerate((rr, kk_, vv)):
                            C.dma('sp', t_.b.name, t_[:], pf[r0 + j_ * 512:r0 + (j_ + 1) * 512, tok:tok + 128]
                                  .rearrange("(h p) t -> p h t", p=128), r=[pf], w=[t_])
                        C.dma('sp', wl_.b.name, wl_[:], pf[PF['rw_wd']:PF['rw_wd'] + 128, tok:tok + 128], r=[pf], w=[wl_])
                        C.dma('sp', al_.b.name, al_[:], pf[PF['rw_ad']:PF['rw_ad'] + 128, tok:tok + 128], r=[pf], w=[al_])
                        if f:
                            C.dma('sp', gd_.b.name, gd_[:], pf[PF['rw_gd']:PF['rw_gd'] + 128, tok:tok + 128],
                                  r=[pf], w=[gd_])
                            yb, bbt = yb4[b], bb4[b]
                            C.dma('sp', yb.b.name, yb[:], ob[1024:1536, tok:tok + 128].rearrange("(h p) t -> p h t", p=128),
                                  r=[ob], w=[yb])
                            C.dma('sp', bbt.b.name, bbt[:], bb[:, tok:tok + 128].rearrange("(h p) t -> p h t", p=128),
                                  r=[bb], w=[bbt])
                        oo, pbo = oo4[b], pb4[b]
                        for p in range(4):
                            Z = Zst[(si, p)]
                            rix = u % R
                            u += 1
                            w = {nm: W[nm][rix] for nm in n128}
                            w.update({nm: W2[nm][rix] for nm in n256})
                            w.update({nm: W3[nm][rix] for nm in n64})
                            ps_ = slice(p * 128, (p + 1) * 128)
                            rT, kT, vT = rr[:, p, :], kk_[:, p, :], vv[:, p, :]
                            col = lambda nm: P[nm][:, l * 4 + p:l * 4 + p + 1]
                            C.op('dve', lambda: nc.vector.tensor_scalar(w['kk'][:], kT, col('rwkv_kk'), None, ALU.mult),
                                 r=[kk_, P['rwkv_kk']], w=[w['kk']])
                            self.act(w['sq'], w['sq'][:], w['kk'], w['kk'][:], AF.Square)
                            p_ = self.pn()
                            self.mm(p_, p_[:], cst, bones, w['sq'], w['sq'][:])
                            self.rsqrt(w['rs'], w['rs'][:], p_, p_[:], self.epsc)
                            self.vtt(w['kk'], w['kk'][:], w['kk'], w['kk'][:], w['rs'], w['rs'][:], ALU.mult, eng='pool')
                            expand(w['ve'], vv, vT)
                            for c in range(2):
                                p_ = self.pn()
                                self.mm(p_, p_[:], w['ve'], ch(w['ve'], c), cst, ident)
                                vb = w['Vbd%d' % c]
                                C.op('act', lambda: nc.scalar.copy(vb[:], p_[:]), r=[p_], w=[vb])
                            if RWL < 2:
                                continue
                            dsl = slice(d * 64, (d + 1) * 64)
                            p_z = self.pn()
                            self.mm(p_z, p_z[:], wl_, wl_[dsl, :], w2t, w2t[dsl, ps_])
                            self.vtt(w['sz'], w['sz'][:], p_z, p_z[:], w0r, w0r[:, d * 512 + p * 128:d * 512 + (p + 1) * 128],
                                     ALU.add)
                            self.act(w['sz'], w['sz'][:], w['sz'], w['sz'][:], AF.Sigmoid)
                            p_c, p_x_, p_a = self.pn(), self.pn(), self.pn()
                            self.mm(p_c, p_c[:], w['sz'], w['sz'][:], cst, tri)
                            self.mm(p_x_, p_x_[:], w['sz'], w['sz'][:], cst, trx)
                            self.mm(p_a, p_a[:], w['sz'], w['sz'][:], cst, tra)
                            self.act(w['Ep'], w['Ep'][:], p_c, p_c[:], AF.Exp, scale=-CW)
                            self.act(w['Em'], w['Em'][:], p_c, p_c[:], AF.Exp, scale=CW)
                            self.act(w['Ex'], w['Ex'][:], p_x_, p_x_[:], AF.Exp, scale=-CW)
                            self.act(w['El'], w['El'][:], p_a, p_a[:], AF.Exp, scale=-CW)
                            p_i = self.pn()
                            self.mm(p_i, p_i[:], a2t, a2t[dsl, ps_], al_, al_[dsl, :])
                            a0 = P['rwkv_a0']
                            self.act(w['icl'], w['icl'][:], p_i, p_i[:], AF.Sigmoid,
                                     bias=a0[:, l * 8 + d * 4 + p:l * 8 + d * 4 + p + 1], extra=[a0])
                            C.op('dve', lambda: nc.vector.tensor_scalar(w['t1'][:], w['icl'][:], col('rwkv_ka'),
                                                                        omka[:, p:p + 1], ALU.mult, ALU.add),
                                 r=[w['icl'], ka, omka], w=[w['t1']])
                            self.vtt(w['kd'], w['kd'][:], kk_, kT, w['t1'], w['t1'][:], ALU.mult, eng='pool')
                            self.vtt(w['bT'], w['bT'][:], w['kk'], w['kk'][:], w['icl'], w['icl'][:], ALU.mult, eng='pool')
                            C.op('dve', lambda: nc.vector.scalar_tensor_tensor(w['pr'][:], rr[:, p, :], col('rwkv_rk'),
                                                                               w['kd'][:], ALU.mult, ALU.mult),
                                 r=[rr, P['rwkv_rk'], w['kd']], w=[w['pr']])
                            if RWL < 3:
                                continue
                            self.vtt(w['rt'], w['rt'][:], rr, rT, w['Ep'], w['Ep'][:], ALU.mult)
                            tmp = w['tmp']
                            C.op('dve', lambda: nc.vector.scalar_tensor_tensor(w['t1'][:], w['kk'][:], -1.0, w['Ex'][:],
                                                                               ALU.mult, ALU.mult),
                                 r=[w['kk'], w['Ex']], w=[w['t1']])
                            expand(w['ate'], w['t1'], w['t1'][:])
                            C.op('pool', lambda: nc.gpsimd.tensor_copy(
                                w['at2'][:].rearrange("p (c h s) -> p c h s", c=2, h=2),
                                w['t1'][:].rearrange("p (c s) -> p c s", c=2).unsqueeze(2).to_broadcast([128, 2, 2, 64])),
                                r=[w['t1']], w=[w['at2']])
                            self.vtt(w['sq'], w['sq'][:], w['bT'], w['bT'][:], w['Em'], w['Em'][:], ALU.mult)
                            expand(w['bte'], w['sq'], w['sq'][:])
                            self.vtt(w['rs'], w['rs'][:], w['kd'], w['kd'][:], w['Em'], w['Em'][:], ALU.mult)
                            expand(w['kte'], w['rs'], w['rs'][:])
                            self.vtt(w['ys'], w['ys'][:], w['bT'], w['bT'][:], w['El'], w['El'][:], ALU.mult)
                            expand(w['Bhe'], w['ys'], w['ys'][:])
                            self.vtt(w['yc'], w['yc'][:], w['kd'], w['kd'][:], w['El'], w['El'][:], ALU.mult)
                            expand(w['Khe'], w['yc'], w['yc'][:])
                            if RWL < 4:
                                continue
                            p_ys = (self.pnl(), self.pnl())
                            for c in ((0, 1) if f else (1, 0)):
                                p_y = p_ys[c]
                                cs_ = slice(c * 64, (c + 1) * 64)
                                vb = w['Vbd%d' % c]
                                p_ab, p_ak, p_rb, p_rk = self.pn(), self.pn(), self.pn(), self.pn()
                                self.mm(p_ab, p_ab[:], w['bte'], ch(w['bte'], c), w['at2'], ch(w['at2'], c))
                                self.mm(p_ak, p_ak[:], w['kte'], ch(w['kte'], c), w['at2'], ch(w['at2'], c))
                                self.mm(p_rb, p_rb[:, 0:64], w['bte'], ch(w['bte'], c), w['rt'], w['rt'][:, cs_])
                                self.mm(p_rk, p_rk[:, 0:64], w['kte'], ch(w['kte'], c), w['rt'], w['rt'][:, cs_])
                                U, Lm, P_ = w['Ua'], w['La'], w['Pa']
                                U2, L2, P2 = w['Ub2'], w['Lb'], w['Pb']
                                self.vtt(U, U[:], p_ab, p_ab[:], cst, bst, ALU.mult)
                                self.vtt(w['Aak'], w['Aak'][:], p_ak, p_ak[:], cst, bst, ALU.mult)
                                self.vtt(w['Arb'], w['Arb'][:], p_rb, p_rb[:, 0:64], cst, sin, ALU.mult)
                                self.vtt(w['Ark'], w['Ark'][:], p_rk, p_rk[:, 0:64], cst, sin, ALU.mult)
                                if RWL < 5:
                                    continue
                                p_t = self.pn()
                                self.mm(p_t, p_t[:], U, U[:], cst, ident)
                                C.op('act', lambda: nc.scalar.copy(Lm[:], p_t[:]), r=[p_t], w=[Lm])
                                self.vtt(P_, P_[:], U, U[:], cst, ident, ALU.add, eng='pool')
                                for n_ in range(5):
                                    p_u, p_l, p_p = self.pn(), self.pn(), self.pn()
                                    self.mm(p_u, p_u[:], Lm, Lm[:], U, U[:])
                                    self.mm(p_l, p_l[:], U, U[:], Lm, Lm[:])
                                    C.op('act', lambda: nc.scalar.copy(U2[:], p_u[:]), r=[p_u], w=[U2])
                                    C.op('dve', lambda: nc.vector.tensor_copy(L2[:], p_l[:]), r=[p_l], w=[L2])
                                    self.mm(p_p, p_p[:], L2, L2[:], P_, P_[:])
                                    self.vtt(P2, P2[:], p_p, p_p[:], P_, P_[:], ALU.add)
                                    U, U2, Lm, L2, P_, P2 = U2, U, L2, Lm, P2, P_
                                if RWL < 6:
                                    continue
                                p_b1, p_k1 = self.pn(), self.pn()
                                self.mm(p_b1, p_b1[:], w['Bhe'], ch(w['Bhe'], c), cst, ident)
                                self.mm(p_k1, p_k1[:], w['Khe'], ch(w['Khe'], c), cst, ident)
                                C.op('act', lambda: nc.scalar.copy(w['Bbd'][:], p_b1[:]), r=[p_b1], w=[w['Bbd']])
                                C.op('dve', lambda: nc.vector.tensor_copy(w['Kbd'][:], p_k1[:]), r=[p_k1], w=[w['Kbd']])
                                if RWL < 7:
                                    continue
                                p_x, p_u2, p_zz = self.pn(), self.pn(), self.pn()
                                self.mm(p_x, p_x[:], w['ate'], ch(w['ate'], c), Z, Z[:], True, False)
                                self.mm(p_x, p_x[:], w['Aak'], w['Aak'][:], vb, vb[:], False, True)
                                C.op('act', lambda: nc.scalar.copy(w['X'][:], p_x[:]), r=[p_x], w=[w['X']])
                                self.mm(p_u2, p_u2[:], P_, P_[:], w['X'], w['X'][:])
                                C.op('dve', lambda: nc.vector.tensor_copy(w['Ub'][:], p_u2[:]), r=[p_u2], w=[w['Ub']])
                                if RWL < 8:
                                    continue
                                RWV = int(_os.environ.get('RWV', '0'))
                                if RWV == 0:
                                    self.mm(p_y, p_y[:, 0:64], Z, Z[:], w['rt'], w['rt'][:, cs_], True, False)
                                    self.mm(p_y, p_y[:, 0:64], w['Ub'], w['Ub'][:], w['Arb'], w['Arb'][:], False, False)
                                    self.mm(p_y, p_y[:, 0:64], vb, vb[:], w['Ark'], w['Ark'][:], False, True)
                                elif RWV == 1:
                                    self.mm(p_y, p_y[:, 0:64], Z, Z[:], w['rt'], w['rt'][:, cs_], True, True)
                                elif RWV == 2:
                                    self.mm(p_y, p_y[:, 0:64], w['Ub'], w['Ub'][:], w['Arb'], w['Arb'][:], True, True)
                                elif RWV == 3:
                                    self.mm(p_y, p_y[:, 0:64], vb, vb[:], w['Ark'], w['Ark'][:], True, True)
                                if RWL < 9:
                                    continue
                                self.mm(p_zz, p_zz[:], w['Bbd'], w['Bbd'][:], w['Ub'], w['Ub'][:], True, False)
                                self.mm(p_zz, p_zz[:], w['Kbd'], w['Kbd'][:], vb, vb[:], False, True)
                                gcol = c * 64 + 63 if f else c * 64
                                C.op('dve', lambda: nc.vector.scalar_tensor_tensor(
                                    Z[:], Z[:], w['Ep'][:, gcol:gcol + 1], p_zz[:], ALU.mult, ALU.add),
                                    r=[Z, w['Ep'], p_zz], w=[Z])
                            if RWL < 10:
                                continue
                            if not f:
                                for c in range(2):
                                    C.op('act', lambda: nc.scalar.copy(oo[:, p, c * 64:(c + 1) * 64], p_ys[c][:, 0:64]),
                                         r=[p_ys[c]], w=[oo])
                                C.op('pool', lambda: nc.gpsimd.tensor_copy(pbo[:, p, :], w['pr'][:]), r=[w['pr']], w=[pbo])
                            else:
                                p_g = self.pn()
                                self.mm(p_g, p_g[:], g2t, g2t[:, ps_], gd_, gd_[:])
                                C.op('act', lambda: nc.scalar.copy(w['og'][:], p_g[:]), r=[p_g], w=[w['og']])
                                for c in range(2):
                                    self.vtt(w['ys'], w['ys'][:, c * 64:(c + 1) * 64], p_ys[c], p_ys[c][:, 0:64], yb,
                                             yb[:, p, c * 64:(c + 1) * 64], ALU.add)
                                p_m = self.pn()
                                self.mm(p_m, p_m[:], cst, bones, w['ys'], w['ys'][:])
                                C.op('dve', lambda: nc.vector.scalar_tensor_tensor(
                                    w['yc'][:], p_m[:], -1.0 / 64, w['ys'][:], ALU.mult, ALU.add),
                                    r=[p_m, w['ys']], w=[w['yc']])
                                self.act(w['sq'], w['sq'][:], w['yc'], w['yc'][:], AF.Square)
                                p_v = self.pn()
                                self.mm(p_v, p_v[:], cst, bones, w['sq'], w['sq'][:])
                                self.rsqrt(w['rs'], w['rs'][:], p_v, p_v[:], self.epsgn, scale=1.0 / 64)
                                C.op('dve', lambda: nc.vector.scalar_tensor_tensor(
                                    w['yc'][:], w['yc'][:], col('rwkv_ln_g'), w['rs'][:], ALU.mult, ALU.mult),
                                    r=[w['yc'], P['rwkv_ln_g'], w['rs']], w=[w['yc']])
                                self.vtt(w['pr'], w['pr'][:], w['pr'], w['pr'][:], bbt, bbt[:, p, :], ALU.add, eng='pool')
                                p_bn = self.pn()
                                self.mm(p_bn, p_bn[:], cst, bones, w['pr'], w['pr'][:])
                                self.vtt(w['ys'], w['ys'][:], p_bn, p_bn[:], vv, vT, ALU.mult)
                                C.op('dve', lambda: nc.vector.scalar_tensor_tensor(
                                    w['yc'][:], w['yc'][:], col('rwkv_ln_b'), w['ys'][:], ALU.add, ALU.add),
                                    r=[w['yc'], P['rwkv_ln_b'], w['ys']], w=[w['yc']])
                                self.vtt(oo, oo[:, p, :], w['yc'], w['yc'][:], w['og'], w['og'][:], ALU.mult, eng='pool')
                            if kind == 'p' and step == ntile - 1:
                                dst = self.O['ns_rwkv']
                                zs = zstg[(si + p) % 2]
                                p_ = self.pn()
                                self.mm(p_, p_[:], Z, Z[:], cst, ident)
                                C.op('dve', lambda: nc.vector.tensor_copy(zs[:], p_[:]), r=[p_], w=[zs])
                                for hp in range(2):
                                    C.dma('sp', zs.b.name, dst[idx, l, d, p * 2 + hp],
                                          zs[hp * 64:(hp + 1) * 64, hp * 64:(hp + 1) * 64], r=[zs], w=[dst])
                        if not f:
                            C.dma('sp', oo.b.name, ob[1024:1536, tok:tok + 128].rearrange("(h p) t -> p h t", p=128), oo[:],
                                  r=[oo], w=[ob])
                            C.dma('sp', pbo.b.name, bb[:, tok:tok + 128].rearrange("(h p) t -> p h t", p=128), pbo[:],
                                  r=[pbo], w=[bb])
                        else:
                            C.dma('sp', oo.b.name, br[1024:1536, tok:tok + 128].rearrange("(h p) t -> p h t", p=128), oo[:],
                                  r=[oo], w=[br])
            C.barrier()

    def stage_merge(self, l):
        C, nc = self.C, self.nc
        pf, br = self.S['pf'], self.S['br']
        PF = self.PF
        wb, wo = self.I['mix_branch'], self.I['mix_out']
        with ExitStack() as st:
            xt = self.sb(st, 'xt', [128, KC, 512])
            brb = self.sb(st, 'brb', [128, 12, 512], BF16)
            gt_ = self.sb(st, 'mg', [128, KC, 512])
            mg = self.sb(st, 'mgd', [128, KC, 512])
            mT = self.sb(st, 'mT', [128, KC, 512], BF16)
            tm = [self.sb(st, 'mtm%d' % i, [128, 512]) for i in range(2)]
            nA = n = 0
            xs = self.S['xT']
            for (tok0, TT, ci) in self.token_tiles():
                C.dma('sp', 'xt', xt[:, :, 0:TT], xs[:, tok0:tok0 + TT].rearrange("(kc p) t -> p kc t", p=128),
                      r=[xs], w=[xt])
                C.dma('pool', 'brb', brb[:, :, 0:TT], br[:, tok0:tok0 + TT].rearrange("(c p) t -> p c t", p=128),
                      r=[br], w=[brb])
                for bi in range(3):
                    wt = self.wA[nA % 2]
                    nA += 1
                    wv = wt[:].rearrange("p a b -> p (a b)").rearrange("p (k n) -> p k n", k=4)
                    self.wload(wt, wv, wb, wb[l, bi].rearrange("(kc p) n -> p kc n", p=128))
                    r0 = PF['merge'] + bi * D
                    C.dma('sp', 'mg', gt_[:, :, 0:TT], pf[r0:r0 + D, tok0:tok0 + TT].rearrange("(c p) t -> p c t", p=128),
                          r=[pf], w=[gt_])
                    for oc in range(KC):
                        pp = self.ps[n % 4]
                        n += 1
                        for kc in range(4):
                            C.op('pe', lambda: nc.tensor.matmul(pp[:, 0:TT], wv[:, kc, oc * 128:(oc + 1) * 128],
                                                                brb[:, bi * 4 + kc, 0:TT], start=(kc == 0), stop=(kc == 3)),
                                 r=[wt, brb], w=[pp])
                        if bi == 0:
                            self.vtt(mg, mg[:, oc, 0:TT], pp, pp[:, 0:TT], gt_, gt_[:, oc, 0:TT], ALU.mult)
                        else:
                            t_ = tm[n % 2]
                            self.vtt(t_, t_[:, 0:TT], pp, pp[:, 0:TT], gt_, gt_[:, oc, 0:TT], ALU.mult)
                            self.vtt(mg, mg[:, oc, 0:TT], mg, mg[:, oc, 0:TT], t_, t_[:, 0:TT], ALU.add, eng='pool')
                C.op('act', lambda: nc.scalar.copy(mT[:, :, 0:TT], mg[:, :, 0:TT]), r=[mg], w=[mT])
                for ob_ in range(4):
                    wt = self.wA[nA % 2]
                    nA += 1
                    self.wload(wt, wt[:], wo, wo[l, :, ob_ * 512:(ob_ + 1) * 512].rearrange("(kc p) n -> p kc n", p=128))
                    for oc in range(4):
                        pp = self.ps[4 + oc]
                        for kc in range(KC):
                            C.op('pe', lambda: nc.tensor.matmul(pp[:, 0:TT], wt[:, kc, oc * 128:(oc + 1) * 128],
                                                                mT[:, kc, 0:TT], start=(kc == 0), stop=(kc == KC - 1)),
                                 r=[wt, mT], w=[pp])
                        ko = ob_ * 4 + oc
                        C.op('dve', lambda: nc.vector.scalar_tensor_tensor(
                            xt[:, ko, 0:TT], pp[:, 0:TT], self.gat[:, 1, ko, ci:ci + 1], xt[:, ko, 0:TT],
                            ALU.mult, ALU.add), r=[pp, self.gat, xt], w=[xt])
                self.store_x(tok0, TT, xt)
            C.barrier()

    def stage_mix(self, l):
        parts = self.mix_parts
        self.stage_mixproj(l)
        if 'hgrn' in parts:
            self.stage_hgrn(l)
        if 'gdn' in parts:
            self.stage_gdn_pre(l)
            self.stage_gdn(l)
        if 'rwkv' in parts:
            self.stage_rwkv(l)
        if 'merge' in parts:
            self.stage_merge(l)

    def stage_final(self):
        C, nc = self.C, self.nc
        fn = self.P['final_norm']
        with ExitStack() as st:
            xt = self.sb(st, 'xt', [128, KC, 512])
            sq = [self.sb(st, 'sq%d' % i, [128, 512]) for i in range(2)]
            rstd = self.sb(st, 'rstd', [128, 512])
            yo = [self.sb(st, 'yo%d' % i, [128, D]) for i in range(2)]
            n = 0
            for (tok0, TT, ci) in self.token_tiles():
                self.load_norm(tok0, TT, 0, ci, xt, None, sq, rstd, None)
                for kc in range(KC):
                    C.op('dve', lambda: nc.vector.scalar_tensor_tensor(
                        xt[:, kc, 0:TT], xt[:, kc, 0:TT], fn[:, kc:kc + 1], rstd[:, 0:TT], ALU.mult, ALU.mult),
                        r=[xt, fn, rstd], w=[xt])
                for sub in range(TT // 128):
                    y = yo[n % 2]
                    n += 1
                    for g in range(4):
                        ps = self.ps[g]
                        for q in range(4):
                            kc = g * 4 + q
                            C.op('pe', lambda: nc.tensor.matmul(ps[:, q * 128:(q + 1) * 128],
                                                                xt[:, kc, sub * 128:(sub + 1) * 128], self.cc('ident'),
                                                                start=True, stop=True), r=[xt, self.cst], w=[ps])
                        if g % 2 == 0:
                            C.op('dve', lambda: nc.vector.tensor_copy(y[:, g * 512:(g + 1) * 512], ps[:]), r=[ps], w=[y])
                        else:
                            C.op('act', lambda: nc.scalar.copy(y[:, g * 512:(g + 1) * 512], ps[:]), r=[ps], w=[y])
                    t0 = tok0 + sub * 128
                    if t0 < self.NTP:
                        dst_t, dst = self.O['y_prompt'], self.O['y_prompt'][t0:t0 + 128, :]
                    else:
                        dst_t, dst = self.O['y_sample'], self.O['y_sample'][t0 - self.NTP:t0 - self.NTP + 128, :]
                    C.dma('sp', y.b.name, dst, y[:], r=[y], w=[dst_t])
            C.barrier()

    def build(self):
        self.setup()
        self.wA = [self.sb(self.es, 'wA%d' % i, [128, KC, 512], BF16) for i in range(2)]
        if self.do_mix:
            self.setup_mix()
        self.stage_x0()
        for l in range(self.depth):
            self.mod_layer(l, None)
            self.stage_ffn(l, 0)
            if self.do_mix:
                self.stage_mix(l)
            self.stage_ffn(l, 2)
        self.stage_final()
        self.C.barrier()
        self.es.close()
        return self.nc


WEIGHT_KEYS = ['mod_w', 'mod_b', 'norm_g', 'ffn_up', 'ffn_down', 'mix_in', 'hgrn_lb', 'hgrn_norm', 'gdn_conv',
               'gdn_a_log', 'gdn_dt_bias', 'gdn_norm', 'rwkv_w0', 'rwkv_w2', 'rwkv_a0', 'rwkv_a2', 'rwkv_g2',
               'rwkv_kk', 'rwkv_ka', 'rwkv_ln_g', 'rwkv_ln_b', 'mix_branch', 'mix_out', 'final_norm']


def make_in_maps(inp, n_cores, NP, wdepth=DEPTH):
    f = lambda a: np.ascontiguousarray(np.asarray(a, dtype=np.float32))
    shared = {k: f(inp[k]) for k in WEIGHT_KEYS}
    shared['rwkv_rk'] = f(inp['rwkv_rk']).reshape(DEPTH, BW)
    shared['consts'] = CONST_ARR
    for k in ('mod_w', 'ffn_up', 'ffn_down', 'mix_in', 'mix_branch', 'mix_out'):
        shared[k] = shared[k][:wdepth]
    xp, xs = f(inp['x_prompt']), f(inp['x_sample'])
    maps = []
    for i in range(n_cores):
        m = dict(shared)
        m['x_prompt'] = np.ascontiguousarray(xp[i * NP:(i + 1) * NP].reshape(-1, D))
        m['x_sample'] = np.ascontiguousarray(xs[i])
        m['state_hgrn'] = f(inp['state_hgrn'][i])
        m['state_gdn'] = f(inp['state_gdn'][i])
        m['state_rwkv'] = f(inp['state_rwkv'][i])
        m['cond'] = np.ascontiguousarray(np.stack([f(inp['c_ctx']), f(inp['c'])[i]], axis=0))
        maps.append(m)
    return maps


def kernel(**inp):
    n = 8
    NP = inp['x_prompt'].shape[0] // n
    LP = inp['x_prompt'].shape[1]
    LS = inp['x_sample'].shape[1]
    b = Builder(NP=NP, LP=LP, LS=LS)
    nc = b.build()
    maps = make_in_maps(inp, n, NP)
    res = run_bass_kernel_spmd(nc, maps, core_ids=list(range(n)))
    R = res.results
    yp = np.concatenate([r['y_prompt'].reshape(NP, LP, D) for r in R], axis=0)
    ys = np.stack([r['y_sample'] for r in R], axis=0)
    hg = np.concatenate([r['ns_hgrn'] for r in R], axis=0)
    gd = np.concatenate([r['ns_gdn'] for r in R], axis=0)
    rw = np.concatenate([r['ns_rwkv'] for r in R], axis=0)
    return (yp.astype(np.float32), ys.astype(np.float32), hg.astype(np.float32), gd.astype(np.float32),
            rw.astype(np.float32))
```

```python
import numpy as np
from contextlib import ExitStack
import concourse.bass as bass
import concourse.mybir as mybir
from concourse.bass_utils import run_bass_kernel_spmd

F32 = mybir.dt.float32
BF16 = mybir.dt.bfloat16
AF = mybir.ActivationFunctionType
ALU = mybir.AluOpType
import os as _os
RWL = int(_os.environ.get('RWL', '99'))

D = 2048
KC = D // 128
DFF = 5632
DEPTH = 4
NMOD = 9
BW = 512
EPS = 1e-6
IN_SPLITS = (('hg_q', 512), ('hg_i', 512), ('hg_f', 1024), ('hg_g', 512), ('gd_qkv', 1536), ('gd_z', 512),
             ('gd_a', 8), ('gd_b', 8), ('rw_rkv', 1536), ('rw_wd', 128), ('rw_ad', 128), ('rw_gd', 128),
             ('merge', 6144))
OFF = {}
_o = 0
for _n, _s in IN_SPLITS:
    OFF[_n] = _o
    _o += _s
INW = _o


class Buf:
    __slots__ = ('name', 'w', 'r')

    def __init__(self, name):
        self.name = name
        self.w = {}
        self.r = {}


class T:
    def __init__(self, h, name, buf=None, excl=False):
        self.h = h
        self.b = buf if buf is not None else Buf(name)
        self.excl = excl

    def __getitem__(self, k):
        return self.h[k]


class Ctx:
    def __init__(self, nc, es):
        self.nc = nc
        self.es = es
        self.eng = {'pe': nc.tensor, 'dve': nc.vector, 'act': nc.scalar, 'pool': nc.gpsimd, 'sp': nc.sync}
        self.sem = {k: es.enter_context(nc.semaphore('e_' + k)) for k in self.eng}
        self.cnt = {k: 0 for k in self.eng}
        self.seen = {k: {} for k in self.eng}
        self.dsem = {}
        self.dtot = {}
        self.gmap = {}
        self.nins = 0

    def _wait(self, e, need):
        for k, v in need.items():
            if k == 'pe' and e == 'pe':
                continue
            if k[0] == 'd' and k[1] == ':':
                v = self.dtot[k]
                sem = self.dsem[k]
            else:
                sem = self.sem[k]
            if self.seen[e].get(k, 0) >= v:
                continue
            self.eng[e].wait_ge(sem, v)
            self.seen[e][k] = v

    @staticmethod
    def _deps(r, w):
        need = {}
        for t in r:
            for k, v in t.b.w.items():
                if need.get(k, 0) < v:
                    need[k] = v
        for t in w:
            for k, v in t.b.w.items():
                if need.get(k, 0) < v:
                    need[k] = v
            for k, v in t.b.r.items():
                if need.get(k, 0) < v:
                    need[k] = v
        return need

    @staticmethod
    def _record(ev, r, w):
        k, v = ev
        for t in r:
            t.b.r[k] = v
        for t in w:
            t.b.w = {k: v}
            t.b.r = {}

    def op(self, e, fn, r=(), w=()):
        ex = [t for t in r if t.excl]
        if ex:
            w = list(w) + ex
        self._wait(e, self._deps(r, w))
        ins = fn()
        self.cnt[e] += 1
        ins.then_inc(self.sem[e], 1)
        self._record((e, self.cnt[e]), r, w)
        self.nins += 1
        return ins

    MAXDSEM = 40

    def dma(self, q, g, out, in_, r=(), w=(), **kw):
        if g not in self.gmap:
            if len(self.dsem) < self.MAXDSEM:
                key = 'd:%d' % len(self.dsem)
                self.dsem[key] = self.es.enter_context(self.nc.semaphore('d_%d' % len(self.dsem)))
                self.dtot[key] = 0
            else:
                key = 'd:%d' % (8 + (len(self.gmap) % (self.MAXDSEM - 8)))
            self.gmap[g] = key
        key = self.gmap[g]
        self._wait(q, self._deps(r, w))
        ins = self.eng[q].dma_start(out=out, in_=in_, **kw)
        self.dtot[key] += 16
        ins.then_inc(self.dsem[key], 16)
        self._record((key, self.dtot[key]), r, w)
        self.nins += 1
        return ins

    def barrier(self):
        need = {k: v for k, v in self.cnt.items() if v > 0}
        need.update({k: v for k, v in self.dtot.items() if v > 0})
        for e in self.eng:
            self._wait(e, dict(need))


def host_consts():
    c = {}
    c['ident'] = np.eye(128, dtype=np.float32)
    c['ones'] = np.ones((128, 128), np.float32)
    i = np.arange(128)
    s, t = i[:, None], i[None, :]
    c['bones64'] = ((s // 64) == (t // 64)).astype(np.float32)
    same32 = (s // 32) == (t // 32)
    c['hg_tri_f'] = (same32 & (s <= t)).astype(np.float32)
    c['hg_tri_b'] = (same32 & (s >= t)).astype(np.float32)
    c['hg_trx_f'] = (same32 & (s > t)).astype(np.float32)
    c['hg_trx_b'] = (same32 & (s < t)).astype(np.float32)
    c['hg_cm'] = np.repeat(((i[:, None] // 32) == np.arange(4)[None, :]).astype(np.float32), 32, axis=1)
    c['gd_tri_f'] = (s <= t).astype(np.float32)
    c['gd_tri_b'] = (s >= t).astype(np.float32)
    c['gd_neg_f'] = np.where(s <= t, 0.0, -30000.0).astype(np.float32)
    c['gd_neg_b'] = np.where(s >= t, 0.0, -30000.0).astype(np.float32)
    c['gd_st_f'] = (s < t).astype(np.float32)
    c['gd_st_b'] = (s > t).astype(np.float32)
    same64 = (s // 64) == (t // 64)
    c['rw_tri_f'] = (same64 & (s <= t)).astype(np.float32)
    c['rw_tri_b'] = (same64 & (s >= t)).astype(np.float32)
    c['rw_trx_f'] = (same64 & (s < t)).astype(np.float32)
    c['rw_trx_b'] = (same64 & (s > t)).astype(np.float32)
    ss, tt = s % 64, t % 64
    c['rw_bst_f'] = (same64 & (ss < tt)).astype(np.float32)
    c['rw_bst_b'] = (same64 & (ss > tt)).astype(np.float32)
    t64 = np.arange(64)[None, :]
    c['rw_sin_f'] = ((s % 64) <= t64).astype(np.float32)
    c['rw_sin_b'] = ((s % 64) >= t64).astype(np.float32)
    blk = np.zeros((128, 2, 2, 64), np.float32)
    blk[:64, :, 0, :] = 1.0
    blk[64:, :, 1, :] = 1.0
    c['rw_blk'] = blk.reshape(128, 256)
    names, cols, off = [], {}, 0
    arrs = []
    for k, v in c.items():
        cols[k] = (off, v.shape[1])
        off += v.shape[1]
        arrs.append(v)
    return np.concatenate(arrs, axis=1), cols


CONST_ARR, CONST_COLS = host_consts()


class Builder:
    def __init__(self, NP=4, LP=256, LS=2048, depth=DEPTH, do_mix=True, debug_outs=(),
                 mix_parts=('hgrn', 'gdn', 'rwkv', 'merge'), wdepth=DEPTH):
        self.wdepth = wdepth
        self.NP, self.LP, self.LS, self.depth, self.do_mix = NP, LP, LS, depth, do_mix
        self.NTP = NP * LP
        self.NT = self.NTP + LS
        self.debug_outs = debug_outs
        self.mix_parts = mix_parts
        self.nc = bass.Bass("TRN2", target_bir_lowering=False)
        self.es = ExitStack()
        self.C = Ctx(self.nc, self.es)
        self.declare()

    def sb(self, st, name, shape, dt=F32):
        self.uid = getattr(self, 'uid', 0) + 1
        return T(st.enter_context(self.nc.sbuf_tensor('%s_u%d' % (name, self.uid), list(shape), dt)), name)

    def din(self, name, shape, dt=F32):
        return T(self.nc.dram_tensor(name, list(shape), dt, kind="ExternalInput").ap(), name)

    def dout(self, name, shape, dt=F32):
        return T(self.nc.dram_tensor(name, list(shape), dt, kind="ExternalOutput").ap(), name)

    def dscr(self, name, shape, dt=F32):
        kind = "ExternalOutput" if name in self.debug_outs else None
        if kind:
            return T(self.nc.dram_tensor(name, list(shape), dt, kind=kind).ap(), name)
        return T(self.nc.dram_tensor(name, list(shape), dt).ap(), name)

    def declare(self):
        NP, LP, LS, NT, dp = self.NP, self.LP, self.LS, self.NT, self.depth
        I = {}
        I['x_prompt'] = self.din('x_prompt', [NP * LP, D])
        I['x_sample'] = self.din('x_sample', [LS, D])
        I['state_hgrn'] = self.din('state_hgrn', [DEPTH, 2, 4, 128, 128])
        I['state_gdn'] = self.din('state_gdn', [DEPTH, 2, 4, 128, 128])
        I['state_rwkv'] = self.din('state_rwkv', [DEPTH, 2, 8, 64, 64])
        I['cond'] = self.din('cond', [2, D])
        I['mod_w'] = self.din('mod_w', [self.wdepth, D, NMOD * D])
        I['mod_b'] = self.din('mod_b', [DEPTH, NMOD * D])
        I['norm_g'] = self.din('norm_g', [DEPTH, 3, D])
        I['ffn_up'] = self.din('ffn_up', [self.wdepth, 2, D, 2 * DFF])
        I['ffn_down'] = self.din('ffn_down', [self.wdepth, 2, DFF, D])
        I['mix_in'] = self.din('mix_in', [self.wdepth, D, INW])
        I['hgrn_lb'] = self.din('hgrn_lb', [DEPTH, 2, BW])
        I['hgrn_norm'] = self.din('hgrn_norm', [DEPTH, BW])
        I['gdn_conv'] = self.din('gdn_conv', [DEPTH, 3, 3, 1536])
        I['gdn_a_log'] = self.din('gdn_a_log', [DEPTH, 2, 4])
        I['gdn_dt_bias'] = self.din('gdn_dt_bias', [DEPTH, 2, 4])
        I['gdn_norm'] = self.din('gdn_norm', [DEPTH, 128])
        I['rwkv_w0'] = self.din('rwkv_w0', [DEPTH, 2, BW])
        I['rwkv_w2'] = self.din('rwkv_w2', [DEPTH, 2, 64, BW])
        I['rwkv_a0'] = self.din('rwkv_a0', [DEPTH, 2, BW])
        I['rwkv_a2'] = self.din('rwkv_a2', [DEPTH, 2, 64, BW])
        I['rwkv_g2'] = self.din('rwkv_g2', [DEPTH, 128, BW])
        I['rwkv_kk'] = self.din('rwkv_kk', [DEPTH, BW])
        I['rwkv_ka'] = self.din('rwkv_ka', [DEPTH, BW])
        I['rwkv_rk'] = self.din('rwkv_rk', [DEPTH, BW])
        I['rwkv_ln_g'] = self.din('rwkv_ln_g', [DEPTH, BW])
        I['rwkv_ln_b'] = self.din('rwkv_ln_b', [DEPTH, BW])
        I['mix_branch'] = self.din('mix_branch', [self.wdepth, 3, BW, D])
        I['mix_out'] = self.din('mix_out', [self.wdepth, D, D])
        I['final_norm'] = self.din('final_norm', [D])
        I['consts'] = self.din('consts', list(CONST_ARR.shape))
        self.I = I
        O = {}
        O['y_prompt'] = self.dout('y_prompt', [NP * LP, D])
        O['y_sample'] = self.dout('y_sample', [LS, D])
        O['ns_hgrn'] = self.dout('ns_hgrn', [NP, DEPTH, 2, 4, 128, 128])
        O['ns_gdn'] = self.dout('ns_gdn', [NP, DEPTH, 2, 4, 128, 128])
        O['ns_rwkv'] = self.dout('ns_rwkv', [NP, DEPTH, 2, 8, 64, 64])
        self.O = O
        S = {}
        S['xT'] = self.dscr('xT', [D, NT])
        self.S = S

    def setup(self):
        C, nc, es = self.C, self.nc, self.es
        self.cst = self.sb(es, 'cst', CONST_ARR.shape)
        C.dma('sp', 'cst', self.cst[:], self.I['consts'][:, :], r=[self.I['consts']], w=[self.cst])
        self.ps = [T(es.enter_context(nc.psum_tensor('ps%d' % i, [128, 512], F32)), 'ps%d' % i, excl=True)
                   for i in range(8)]
        self.onesD = self.sb(es, 'onesD', [128, 128])
        C.op('dve', lambda: nc.vector.tensor_scalar(self.onesD[:], self.cc('ones'), 1.0 / D, None, ALU.mult),
             r=[self.cst], w=[self.onesD])
        self.ones128 = self.sb(es, 'ones128', [128, 128])
        C.op('dve', lambda: nc.vector.tensor_scalar(self.ones128[:], self.cc('ones'), 1.0 / 128, None, ALU.mult),
             r=[self.cst], w=[self.ones128])
        self.epsc = self.sb(es, 'epsc', [128, 1])
        C.op('dve', lambda: nc.vector.memset(self.epsc[:], EPS), w=[self.epsc])
        self.stg = self.sb(es, 'fmstg', [128, 128])
        self.P = {}
        dp = DEPTH
        self.P['norm_g'] = self.load_fm('norm_g', self.I['norm_g'], "l j (c p) -> (l j c) p", dp * 3 * KC)
        self.P['final_norm'] = self.load_fm('final_norm', self.I['final_norm'], "(c p) -> c p", KC)
        self.P['mod_b'] = self.load_fm('mod_b', self.I['mod_b'], "l (c p) -> (l c) p", dp * 144)
        cf = self.load_fm('condf', self.I['cond'], "r (c p) -> (r c) p", 2 * KC)
        self.cond = self.sb(es, 'condb', [128, KC, 2], BF16)
        C.op('act', lambda: nc.scalar.activation(self.cond[:].rearrange("p c r -> p r c"),
                                                 cf[:].rearrange("p (r c) -> p r c", r=2), AF.Silu),
             r=[cf], w=[self.cond])
        self.modFM = self.sb(es, 'modFM', [128, 144, 2])
        self.scl = self.sb(es, 'scl', [128, 3, KC, 2])
        self.gat = self.sb(es, 'gat', [128, 3, KC, 2])

    def rsqrt(self, out_t, out_ap, in_t, in_ap, eps_t, scale=1.0):
        C, nc = self.C, self.nc
        npart = out_ap.shape[0]
        C.op('act', lambda: nc.scalar.activation(out_ap, in_ap, AF.Ln, bias=eps_t[0:npart, :], scale=scale),
             r=[in_t, eps_t], w=[out_t])
        C.op('act', lambda: nc.scalar.activation(out_ap, out_ap, AF.Exp, scale=-0.5), r=[out_t], w=[out_t])

    def cc(self, name, c0=0, n=None):
        o, w = CONST_COLS[name]
        if n is None:
            n = w
        return self.cst[:, o + c0:o + c0 + n]

    def load_fm(self, name, src, pattern, nrows):
        C, nc = self.C, self.nc
        dst = self.sb(self.es, 'P_' + name, [128, nrows])
        view = src[:].rearrange(pattern, p=128)
        for r0 in range(0, nrows, 128):
            n = min(128, nrows - r0)
            C.dma('sp', 'fmstg', self.stg[0:n, :], view[r0:r0 + n, :], r=[src], w=[self.stg])
            ps = self.ps[0]
            C.op('pe', lambda: nc.tensor.matmul(ps[:, 0:n], self.stg[0:n, :], self.cc('ident')[0:n, 0:n],
                                                start=True, stop=True), r=[self.stg, self.cst], w=[ps])
            C.op('dve', lambda: nc.vector.tensor_copy(dst[:, r0:r0 + n], ps[:, 0:n]), r=[ps], w=[dst])
        return dst

    def wload(self, wt, dst_ap, src_t, src_ap):
        self.C.dma('pool', wt.b.name, dst_ap, src_ap, r=[src_t], w=[wt])

    def mod_layer(self, l, st):
        C, nc = self.C, self.nc
        ps = self.ps[1]
        mw = self.I['mod_w']
        for nb in range(36):
            wt = self.wA[nb % 2]
            self.wload(wt, wt[:], mw, mw[l, :, nb * 512:(nb + 1) * 512].rearrange("(kc p) n -> p kc n", p=128))
            for c4 in range(4):
                ch = nb * 4 + c4
                for kc in range(KC):
                    C.op('pe', lambda: nc.tensor.matmul(ps[:, ch * 2:ch * 2 + 2], wt[:, kc, c4 * 128:(c4 + 1) * 128],
                                                        self.cond[:, kc, :], start=(kc == 0), stop=(kc == KC - 1)),
                         r=[wt, self.cond], w=[ps])
        mb = self.P['mod_b']
        C.op('dve', lambda: nc.vector.tensor_tensor(
            self.modFM[:], ps[:, 0:288].rearrange("p (c r) -> p c r", r=2),
            mb[:, l * 144:(l + 1) * 144].unsqueeze(2).to_broadcast([128, 144, 2]), ALU.add),
            r=[ps, mb], w=[self.modFM])
        ng = self.P['norm_g']
        for j in range(3):
            sc = self.modFM[:, (3 * j + 1) * KC:(3 * j + 2) * KC, :]
            gt = self.modFM[:, (3 * j + 2) * KC:(3 * j + 3) * KC, :]
            ngj = ng[:, (l * 3 + j) * KC:(l * 3 + j + 1) * KC].unsqueeze(2).to_broadcast([128, KC, 2])
            C.op('dve', lambda: nc.vector.scalar_tensor_tensor(self.scl[:, j], sc, 1.0, ngj, ALU.add, ALU.mult),
                 r=[self.modFM, ng], w=[self.scl])
            C.op('dve', lambda: nc.vector.tensor_scalar(self.gat[:, j], gt, 0.5 if j != 1 else 1.0, None, ALU.mult),
                 r=[self.modFM], w=[self.gat])

    def shift(self, j, kc, ci):
        return self.modFM[:, 3 * j * KC + kc, ci:ci + 1]

    def load_norm(self, tok0, TT, j, ci, xt, hT, sq, rstd, tmp, hoff=0):
        C, nc = self.C, self.nc
        xs = self.S['xT']
        C.dma('sp', 'xt', xt[:, :, 0:TT], xs[:, tok0:tok0 + TT].rearrange("(kc p) t -> p kc t", p=128),
              r=[xs], w=[xt])
        ps = self.ps[7]
        for kc in range(KC):
            s = sq[kc % 2]
            C.op('act', lambda: nc.scalar.activation(s[:, 0:TT], xt[:, kc, 0:TT], AF.Square), r=[xt], w=[s])
            C.op('pe', lambda: nc.tensor.matmul(ps[:, 0:TT], self.onesD[:], s[:, 0:TT], start=(kc == 0),
                                                stop=(kc == KC - 1)), r=[self.onesD, s], w=[ps])
        self.rsqrt(rstd, rstd[:, 0:TT], ps, ps[:, 0:TT], self.epsc)
        if hT is None:
            return
        for kc in range(KC):
            tm = tmp[kc % 2]
            C.op('dve', lambda: nc.vector.scalar_tensor_tensor(tm[:, 0:TT], xt[:, kc, 0:TT], self.scl[:, j, kc, ci:ci + 1],
                                                               rstd[:, 0:TT], ALU.mult, ALU.mult),
                 r=[xt, self.scl, rstd], w=[tm])
            C.op('act', lambda: nc.scalar.activation(hT[:, kc, hoff:hoff + TT], tm[:, 0:TT], AF.Identity,
                                                     bias=self.shift(j, kc, ci), scale=1.0),
                 r=[tm, self.modFM], w=[hT])

    def store_x(self, tok0, TT, xt):
        xs = self.S['xT']
        self.C.dma('sp', 'xt', xs[:, tok0:tok0 + TT].rearrange("(kc p) t -> p kc t", p=128), xt[:, :, 0:TT],
                   r=[xt], w=[xs])

    def token_tiles(self):
        out = []
        for t0 in range(0, self.NTP, 512):
            out.append((t0, min(512, self.NTP - t0), 0))
        for t0 in range(self.NTP, self.NT, 512):
            out.append((t0, min(512, self.NT - t0), 1))
        return out

    def stage_x0(self):
        C, nc = self.C, self.nc
        with ExitStack() as st:
            xr = [self.sb(st, 'x0r%d' % i, [128, D]) for i in range(2)]
            xo = [self.sb(st, 'x0o%d' % i, [128, KC, 128]) for i in range(2)]
            xs = self.S['xT']
            for ti in range(self.NT // 128):
                tok0 = ti * 128
                if tok0 < self.NTP:
                    src_t, src = self.I['x_prompt'], self.I['x_prompt'][tok0:tok0 + 128, :]
                else:
                    src_t, src = self.I['x_sample'], self.I['x_sample'][tok0 - self.NTP:tok0 - self.NTP + 128, :]
                a, o = xr[ti % 2], xo[ti % 2]
                C.dma('sp', a.b.name, a[:], src, r=[src_t], w=[a])
                for g in range(4):
                    ps = self.ps[(ti % 2) * 4 + g]
                    for q in range(4):
                        kc = g * 4 + q
                        C.op('pe', lambda: nc.tensor.matmul(ps[:, q * 128:(q + 1) * 128], a[:, kc * 128:(kc + 1) * 128],
                                                            self.cc('ident'), start=True, stop=True),
                             r=[a, self.cst], w=[ps])
                    eng = 'dve' if g % 2 == 0 else 'act'
                    if eng == 'dve':
                        C.op('dve', lambda: nc.vector.tensor_copy(o[:, g * 4:(g + 1) * 4, :],
                                                                  ps[:].rearrange("p (q t) -> p q t", q=4)),
                             r=[ps], w=[o])
                    else:
                        C.op('act', lambda: nc.scalar.copy(o[:, g * 4:(g + 1) * 4, :],
                                                           ps[:].rearrange("p (q t) -> p q t", q=4)),
                             r=[ps], w=[o])
                C.dma('sp', o.b.name, xs[:, tok0:tok0 + 128].rearrange("(kc p) t -> p kc t", p=128), o[:],
                      r=[o], w=[xs])
            C.barrier()

    def stage_ffn(self, l, j):
        C, nc = self.C, self.nc
        fi = 0 if j == 0 else 1
        up, dn = self.I['ffn_up'], self.I['ffn_down']
        NT = self.NT
        if 'actT' not in self.S:
            self.S['actT'] = self.dscr('actT', [DFF, NT], BF16)
        actT = self.S['actT']
        xs = self.S['xT']
        tiles = self.token_tiles()
        with ExitStack() as st:
            hA = self.sb(st, 'hA', [128, KC, NT], BF16)
            with ExitStack() as st1:
                xt = self.sb(st1, 'xt', [128, KC, 512])
                sq = [self.sb(st1, 'sq%d' % i, [128, 512]) for i in range(2)]
                tmp = [self.sb(st1, 'tmp%d' % i, [128, 512]) for i in range(2)]
                rstd = self.sb(st1, 'rstd', [128, 512])
                for (tok0, TT, ci) in tiles:
                    self.load_norm(tok0, TT, j, ci, xt, hA, sq, rstd, tmp, hoff=tok0)
                C.barrier()
            with ExitStack() as st2:
                sa = [self.sb(st2, 'sa%d' % i, [128, 512]) for i in range(2)]
                ast = [self.sb(st2, 'ast%d' % i, [128, 2, 512], BF16) for i in range(3)]
                n = na = 0
                for fb in range(22):
                    wt = self.wA[fb % 2]
                    self.wload(wt, wt[:, :, 0:256], up,
                               up[l, fi, :, fb * 256:(fb + 1) * 256].rearrange("(kc p) n -> p kc n", p=128))
                    self.wload(wt, wt[:, :, 256:512], up,
                               up[l, fi, :, DFF + fb * 256:DFF + (fb + 1) * 256].rearrange("(kc p) n -> p kc n", p=128))
                    for (tok0, TT, ci) in tiles:
                        a_ = ast[na % 3]
                        na += 1
                        for c2 in range(2):
                            k = n % 2
                            n += 1
                            pa, pb = self.ps[k * 2], self.ps[k * 2 + 1]
                            for half, pp in ((0, pa), (1, pb)):
                                for kc in range(KC):
                                    C.op('pe', lambda: nc.tensor.matmul(
                                        pp[:, 0:TT], wt[:, kc, half * 256 + c2 * 128:half * 256 + (c2 + 1) * 128],
                                        hA[:, kc, tok0:tok0 + TT], start=(kc == 0), stop=(kc == KC - 1)),
                                        r=[wt, hA], w=[pp])
                            s_ = sa[k]
                            C.op('act', lambda: nc.scalar.activation(s_[:, 0:TT], pa[:, 0:TT], AF.Silu), r=[pa], w=[s_])
                            C.op('dve', lambda: nc.vector.tensor_tensor(a_[:, c2, 0:TT], s_[:, 0:TT], pb[:, 0:TT], ALU.mult),
                                 r=[s_, pb], w=[a_])
                        C.dma('sp', a_.b.name, actT[fb * 256:(fb + 1) * 256, tok0:tok0 + TT]
                              .rearrange("(c p) t -> p c t", p=128), a_[:, :, 0:TT], r=[a_], w=[actT])
                C.barrier()
        with ExitStack() as st:
            wD = self.sb(st, 'wD', [128, 44, 512], BF16)
            at = [self.sb(st, 'at%d' % i, [128, 44, 512], BF16) for i in range(2)]
            xc = [self.sb(st, 'xc%d' % i, [128, 512]) for i in range(3)]
            n = nx = 0
            for ob in range(4):
                for qk in range(4):
                    self.wload(wD, wD[:, qk * 11:(qk + 1) * 11, :], dn,
                               dn[l, fi, qk * 1408:(qk + 1) * 1408, ob * 512:(ob + 1) * 512]
                               .rearrange("(kc p) n -> p kc n", p=128))
                for (tok0, TT, ci) in tiles:
                    a_ = at[n % 2]
                    n += 1
                    for hh in range(2):
                        C.dma('sp', a_.b.name, a_[:, hh * 22:(hh + 1) * 22, 0:TT],
                              actT[hh * 2816:(hh + 1) * 2816, tok0:tok0 + TT].rearrange("(c p) t -> p c t", p=128),
                              r=[actT], w=[a_])
                    for oc in range(4):
                        pp = self.ps[4 + oc]
                        kc = ob * 4 + oc
                        x_ = xc[nx % 3]
                        nx += 1
                        C.dma('sp', x_.b.name, x_[:, 0:TT], xs[kc * 128:(kc + 1) * 128, tok0:tok0 + TT], r=[xs], w=[x_])
                        for kk in range(44):
                            C.op('pe', lambda: nc.tensor.matmul(pp[:, 0:TT], wD[:, kk, oc * 128:(oc + 1) * 128],
                                                                a_[:, kk, 0:TT], start=(kk == 0), stop=(kk == 43)),
                                 r=[wD, a_], w=[pp])
                        C.op('dve', lambda: nc.vector.scalar_tensor_tensor(
                            x_[:, 0:TT], pp[:, 0:TT], self.gat[:, j, kc, ci:ci + 1], x_[:, 0:TT], ALU.mult, ALU.add),
                            r=[pp, self.gat, x_], w=[x_])
                        C.dma('sp', x_.b.name, xs[kc * 128:(kc + 1) * 128, tok0:tok0 + TT], x_[:, 0:TT], r=[x_], w=[xs])
            C.barrier()

    PF = {'hg_q': 0, 'hg_k': 512, 'hg_g': 1536, 'gd_qkv': 2048, 'gd_z': 3584, 'rw_rkv': 4096, 'rw_wd': 5632,
          'rw_ad': 5760, 'rw_gd': 5888, 'merge': 6016}
    PF_ROWS = 12160
    PT = {'v': 0, 'logf': 512, 'k': 1536, 'lg': 2560, 'beta': 2568}
    PT_COLS = 2576

    def seqs(self):
        out = [(i * self.LP, self.LP, 'p', i) for i in range(self.NP)]
        out.append((self.NTP, self.LS, 's', 0))
        return out

    def softmax_cum(self, x, e, L, R):
        C, nc = self.C, self.nc
        C.op('act', lambda: nc.scalar.activation(e[:], x[:], AF.Exp), r=[x], w=[e])
        C.op('dve', lambda: nc.vector.tensor_tensor(x[:, 0], e[:, 0], e[:, 1], ALU.add), r=[e], w=[x])
        for l in range(2, L):
            C.op('dve', lambda: nc.vector.tensor_tensor(x[:, 0], x[:, 0], e[:, l], ALU.add), r=[e, x], w=[x])
        C.op('dve', lambda: nc.vector.reciprocal(x[:, 0], x[:, 0]), r=[x], w=[x])
        for l in range(1, L):
            C.op('dve', lambda: nc.vector.tensor_tensor(e[:, l], e[:, l], x[:, 0], ALU.mult), r=[e, x], w=[e])
        C.op('dve', lambda: nc.vector.memset(x[:, 0], 0.0), w=[x])
        for l in range(1, L):
            C.op('dve', lambda: nc.vector.tensor_tensor(x[:, l], x[:, l - 1], e[:, l], ALU.add), r=[e, x], w=[x])

    def bcast_load(self, name, src_t, flat_ap, n):
        t = self.sb(self.es, name, [128, n])
        self.C.dma('sp', name, t[:], flat_ap.partition_broadcast(128), r=[src_t], w=[t])
        return t

    def setup_mix(self):
        C, nc, es, I = self.C, self.nc, self.es, self.I
        dp = DEPTH
        lbf = self.load_fm('lbf', I['hgrn_lb'], "l d (c p) -> (l d c) p", dp * 8)
        ef = self.sb(es, 'lbf_e', [128, dp * 8])
        self.softmax_cum(T(lbf[:].rearrange("p (l r) -> p l r", l=dp), 'lbfv', buf=lbf.b),
                         T(ef[:].rearrange("p (l r) -> p l r", l=dp), 'lbfe', buf=ef.b), dp, 8)
        self.lbf = lbf
        self.omlbf = self.sb(es, 'omlbf', [128, dp * 8])
        C.op('dve', lambda: nc.vector.tensor_scalar(self.omlbf[:], lbf[:], -1.0, 1.0, ALU.mult, ALU.add),
             r=[lbf], w=[self.omlbf])
        lbr = self.bcast_load('lbr', I['hgrn_lb'], I['hgrn_lb'][:].rearrange("l d c -> (l d c)"), dp * 1024)
        with ExitStack() as st_:
            er = self.sb(st_, 'lbr_e', [128, dp * 1024])
            self.softmax_cum(T(lbr[:].rearrange("p (l r) -> p l r", l=dp), 'lbrv', buf=lbr.b),
                             T(er[:].rearrange("p (l r) -> p l r", l=dp), 'lbre', buf=er.b), dp, 1024)
            C.barrier()
        self.lbr = lbr
        self.negA = self.bcast_load('negA', I['gdn_a_log'], I['gdn_a_log'][:].rearrange("l d h -> (l d h)"), dp * 8)
        C.op('act', lambda: nc.scalar.activation(self.negA[:], self.negA[:], AF.Exp), r=[self.negA], w=[self.negA])
        C.op('dve', lambda: nc.vector.tensor_scalar(self.negA[:], self.negA[:], -1.0, None, ALU.mult),
             r=[self.negA], w=[self.negA])
        self.dtb = self.bcast_load('dtb', I['gdn_dt_bias'], I['gdn_dt_bias'][:].rearrange("l d h -> (l d h)"), dp * 8)
        self.onec = self.sb(es, 'onec', [128, 1])
        C.op('dve', lambda: nc.vector.memset(self.onec[:], 1.0), w=[self.onec])
        self.eps128 = self.sb(es, 'eps128', [128, 1])
        C.op('dve', lambda: nc.vector.memset(self.eps128[:], 128.0 * EPS), w=[self.eps128])
        self.epsgn = self.sb(es, 'epsgn', [128, 1])
        C.op('dve', lambda: nc.vector.memset(self.epsgn[:], 64e-5), w=[self.epsgn])
        P = self.P
        P['hgrn_norm'] = self.load_fm('hgrn_norm', I['hgrn_norm'], "l (c p) -> (l c) p", dp * 4)
        P['gdn_conv'] = self.load_fm('gdn_conv', I['gdn_conv'], "l a b (c p) -> (l a b c) p", dp * 108)
        P['gdn_norm'] = self.load_fm('gdn_norm', I['gdn_norm'], "l p -> l p", dp)
        P['rwkv_a0'] = self.load_fm('rwkv_a0', I['rwkv_a0'], "l d (c p) -> (l d c) p", dp * 8)
        for k in ('rwkv_kk', 'rwkv_ka', 'rwkv_rk', 'rwkv_ln_g', 'rwkv_ln_b'):
            P[k] = self.load_fm(k, I[k], "l (c p) -> (l c) p", dp * 4)
        self.pq = [T(self.ps[i][:, j * 128:(j + 1) * 128], 'pq%d_%d' % (i, j), buf=self.ps[i].b, excl=True)
                   for j in range(4) for i in range(8)]
        self.pql = self.pq[-4:]
        self.pq = self.pq[:-4]
        self.pqi = 0
        self.pqli = 0
        NT = self.NT
        self.S['pf'] = self.dscr('pf', [self.PF_ROWS, NT])
        self.S['pt'] = self.dscr('pt', [NT, self.PT_COLS])
        self.S['ob'] = self.dscr('ob', [1536, NT])
        self.S['br'] = self.dscr('br', [1536, NT])

    def pnl(self):
        self.pqli = (self.pqli + 1) % len(self.pql)
        return self.pql[self.pqli]

    def pn(self):
        self.pqi = (self.pqi + 1) % len(self.pq)
        return self.pq[self.pqi]

    def stage_mixproj(self, l):
        C, nc = self.C, self.nc
        mi = self.I['mix_in']
        pf, pt = self.S['pf'], self.S['pt']
        PF, PT = self.PF, self.PT
        fm_blocks = [(0, 512, PF['hg_q'], ['silu'] * 4), (1024, 512, PF['hg_k'], ['hgk0'] * 4),
                     (1536, 512, PF['hg_k'] + 512, ['hgk1'] * 4), (2048, 512, PF['hg_g'], ['silu'] * 4)]
        fm_blocks += [(2560 + 512 * i, 512, PF['gd_qkv'] + 512 * i, ['copy'] * 4) for i in range(3)]
        fm_blocks += [(4096, 512, PF['gd_z'], ['silu'] * 4)]
        fm_blocks += [(4624 + 512 * i, 512, PF['rw_rkv'] + 512 * i, ['copy'] * 4) for i in range(3)]
        fm_blocks += [(6160, 384, PF['rw_wd'], ['tanh', 'copy', 'sigmoid'])]
        fm_blocks += [(6544 + 512 * i, 512, PF['merge'] + 512 * i, ['sigmoid'] * 4) for i in range(12)]
        tm_blocks = [(512, 512, 'v'), (1024, 512, 'f0'), (1536, 512, 'f1'), (4608, 16, 'ab')]
        funcs = {'silu': AF.Silu, 'copy': AF.Copy, 'tanh': AF.Tanh, 'sigmoid': AF.Sigmoid, 'hgk0': AF.Sigmoid,
                 'hgk1': AF.Sigmoid}
        with ExitStack() as st:
            xt = self.sb(st, 'xt', [128, KC, 512])
            hT = self.sb(st, 'hT', [128, KC, 512], BF16)
            sq = [self.sb(st, 'sq%d' % i, [128, 512]) for i in range(2)]
            tmp = [self.sb(st, 'tmp%d' % i, [128, 512]) for i in range(2)]
            rstd = self.sb(st, 'rstd', [128, 512])
            stg = [self.sb(st, 'stg%d' % i, [128, 512]) for i in range(3)]
            tst = [self.sb(st, 'tst%d' % i, [128, 1024]) for i in range(2)]
            omlbr = self.sb(st, 'omlbr', [128, 1024])
            C.op('dve', lambda: nc.vector.tensor_scalar(omlbr[:], self.lbr[:, l * 1024:(l + 1) * 1024], -1.0, 1.0,
                                                        ALU.mult, ALU.add), r=[self.lbr], w=[omlbr])
            nA = ns = nt = 0
            for (tok0, TT, ci) in self.token_tiles():
                self.load_norm(tok0, TT, 1, ci, xt, hT, sq, rstd, tmp)
                for (c0, ncol, prow, kinds) in fm_blocks:
                    wt = self.wA[nA % 2]
                    nA += 1
                    self.wload(wt, wt[:, :, 0:ncol], mi, mi[l, :, c0:c0 + ncol].rearrange("(kc p) n -> p kc n", p=128))
                    for ch, kind in enumerate(kinds):
                        pp = self.ps[ns % 4]
                        sg = stg[ns % 3]
                        ns += 1
                        for kc in range(KC):
                            C.op('pe', lambda: nc.tensor.matmul(pp[:, 0:TT], wt[:, kc, ch * 128:(ch + 1) * 128],
                                                                hT[:, kc, 0:TT], start=(kc == 0), stop=(kc == KC - 1)),
                                 r=[wt, hT], w=[pp])
                        C.op('act', lambda: nc.scalar.activation(sg[:, 0:TT], pp[:, 0:TT], funcs[kind]), r=[pp], w=[sg])
                        if kind in ('hgk0', 'hgk1'):
                            d = int(kind[-1])
                            col = l * 8 + d * 4 + ch
                            C.op('dve', lambda: nc.vector.tensor_scalar(sg[:, 0:TT], sg[:, 0:TT], -1.0, 1.0,
                                                                        ALU.mult, ALU.add), r=[sg], w=[sg])
                            C.op('dve', lambda: nc.vector.tensor_scalar(sg[:, 0:TT], sg[:, 0:TT],
                                                                        self.omlbf[:, col:col + 1], None, ALU.mult),
                                 r=[sg, self.omlbf], w=[sg])
                        r0 = prow + ch * 128
                        C.dma('sp', sg.b.name, pf[r0:r0 + 128, tok0:tok0 + TT], sg[:, 0:TT], r=[sg], w=[pf])
                for (c0, ncol, kind) in tm_blocks:
                    wt = self.wA[nA % 2]
                    nA += 1
                    self.wload(wt, wt[:, :, 0:ncol], mi, mi[l, :, c0:c0 + ncol].rearrange("(kc p) n -> p kc n", p=128))
                    for sub in range(TT // 128):
                        pp = self.ps[4 + nt % 3]
                        ts_ = tst[nt % 2]
                        nt += 1
                        t0 = tok0 + sub * 128
                        for kc in range(KC):
                            C.op('pe', lambda: nc.tensor.matmul(pp[:, 0:ncol], hT[:, kc, sub * 128:(sub + 1) * 128],
                                                                wt[:, kc, 0:ncol], start=(kc == 0), stop=(kc == KC - 1)),
                                 r=[wt, hT], w=[pp])
                        if kind == 'v':
                            C.op('act', lambda: nc.scalar.copy(ts_[:, 0:512], pp[:]), r=[pp], w=[ts_])
                            C.dma('sp', ts_.b.name, pt[t0:t0 + 128, PT['v']:PT['v'] + 512], ts_[:, 0:512], r=[ts_], w=[pt])
                        elif kind in ('f0', 'f1'):
                            d = int(kind[-1])
                            o = l * 1024 + d * 512
                            f, k = ts_[:, 0:512], ts_[:, 512:1024]
                            C.op('act', lambda: nc.scalar.activation(f, pp[:], AF.Sigmoid), r=[pp], w=[ts_])
                            C.op('dve', lambda: nc.vector.tensor_tensor(f, f, omlbr[:, d * 512:(d + 1) * 512], ALU.mult),
                                 r=[ts_, omlbr], w=[ts_])
                            C.op('dve', lambda: nc.vector.tensor_tensor(f, f, self.lbr[:, o:o + 512], ALU.add),
                                 r=[ts_, self.lbr], w=[ts_])
                            C.op('dve', lambda: nc.vector.tensor_scalar(k, f, -1.0, 1.0, ALU.mult, ALU.add),
                                 r=[ts_], w=[ts_])
                            C.op('dve', lambda: nc.vector.tensor_scalar(f, f, 1e-30, None, ALU.max), r=[ts_], w=[ts_])
                            C.op('act', lambda: nc.scalar.activation(f, f, AF.Ln), r=[ts_], w=[ts_])
                            C.dma('sp', ts_.b.name, pt[t0:t0 + 128, PT['logf'] + d * 512:PT['logf'] + (d + 1) * 512], f,
                                  r=[ts_], w=[pt])
                            C.dma('sp', ts_.b.name, pt[t0:t0 + 128, PT['k'] + d * 512:PT['k'] + (d + 1) * 512], k,
                                  r=[ts_], w=[pt])
                        else:
                            a, b = ts_[:, 0:8], ts_[:, 8:16]
                            C.op('dve', lambda: nc.vector.tensor_tensor(a, pp[:, 0:8], self.dtb[:, l * 8:l * 8 + 8], ALU.add),
                                 r=[pp, self.dtb], w=[ts_])
                            C.op('act', lambda: nc.scalar.activation(a, a, AF.Exp), r=[ts_], w=[ts_])
                            C.op('act', lambda: nc.scalar.activation(a, a, AF.Ln, bias=self.onec[:], scale=1.0),
                                 r=[ts_, self.onec], w=[ts_])
                            C.op('dve', lambda: nc.vector.tensor_tensor(a, a, self.negA[:, l * 8:l * 8 + 8], ALU.mult),
                                 r=[ts_, self.negA], w=[ts_])
                            C.op('act', lambda: nc.scalar.activation(b, pp[:, 8:16], AF.Sigmoid), r=[pp], w=[ts_])
                            C.dma('sp', ts_.b.name, pt[t0:t0 + 128, PT['lg']:PT['lg'] + 16], ts_[:, 0:16], r=[ts_], w=[pt])
            C.barrier()

    def mm(self, pt_, out_ap, lt, lhsT, rt, rhs, start=True, stop=True):
        nc = self.nc
        return self.C.op('pe', lambda: nc.tensor.matmul(out_ap, lhsT, rhs, start=start, stop=stop),
                         r=[lt, rt], w=[pt_])

    def vtt(self, ot, out, at, a, bt, b, op, eng='dve'):
        nc = self.nc
        e = nc.vector if eng == 'dve' else nc.gpsimd
        return self.C.op(eng, lambda: e.tensor_tensor(out, a, b, op), r=[at, bt], w=[ot])

    def act(self, ot, out, it, in_, func, bias=None, scale=1.0, extra=()):
        nc = self.nc
        if bias is None:
            return self.C.op('act', lambda: nc.scalar.activation(out, in_, func, scale=scale), r=[it], w=[ot])
        return self.C.op('act', lambda: nc.scalar.activation(out, in_, func, bias=bias, scale=scale),
                         r=[it] + list(extra), w=[ot])

    def stage_hgrn(self, l):
        C, nc = self.C, self.nc
        pf, pt, ob, br = self.S['pf'], self.S['pt'], self.S['ob'], self.S['br']
        PF, PT = self.PF, self.PT
        cst = self.cst
        seqs = self.seqs()
        with ExitStack() as st:
            Sst = {(si, h): self.sb(st, 'hS%d_%d' % (si, h), [128, 128]) for si in range(len(seqs)) for h in range(4)}
            NB = 3
            qT4 = [self.sb(st, 'hq%d' % i, [128, 4, 128]) for i in range(NB)]
            kT4 = [self.sb(st, 'hk%d' % i, [128, 4, 128]) for i in range(NB)]
            kt4 = [self.sb(st, 'hkt%d' % i, [128, 512]) for i in range(NB)]
            lf4 = [self.sb(st, 'hlf%d' % i, [128, 512]) for i in range(NB)]
            vt4 = [self.sb(st, 'hv%d' % i, [128, 512]) for i in range(NB)]
            ob4 = [self.sb(st, 'hob%d' % i, [128, 4, 128]) for i in range(NB)]
            gT4 = [self.sb(st, 'hg%d' % i, [128, 4, 128]) for i in range(NB)]
            oo4 = [self.sb(st, 'hoo%d' % i, [128, 4, 128]) for i in range(NB)]
            R = 4
            ebT = [self.sb(st, 'heb%d' % i, [128, 128]) for i in range(R)]
            enb = [self.sb(st, 'henb%d' % i, [128, 128]) for i in range(R)]
            qeT = [self.sb(st, 'hqe%d' % i, [128, 128]) for i in range(R)]
            keT = [self.sb(st, 'hke%d' % i, [128, 128]) for i in range(R)]
            elm = [self.sb(st, 'helm%d' % i, [128, 128]) for i in range(R)]
            kl4 = [self.sb(st, 'hkl%d' % i, [128, 4, 128]) for i in range(R)]
            scm = [self.sb(st, 'hscm%d' % i, [128, 128]) for i in range(R)]
            sqt = [self.sb(st, 'hsq%d' % i, [128, 128]) for i in range(R)]
            rst = [self.sb(st, 'hrs%d' % i, [128, 128]) for i in range(R)]
            hn = self.P['hgrn_norm']
            u = 0
            nld = 0
            for d in (1, 0):
                tri = 'hg_tri_f' if d == 0 else 'hg_tri_b'
                trx = 'hg_trx_f' if d == 0 else 'hg_trx_b'
                for si, (tok0, L, kind, idx) in enumerate(seqs):
                    for h in range(4):
                        S = Sst[(si, h)]
                        if kind == 's':
                            C.dma('sp', S.b.name, S[:], self.I['state_hgrn'][l, d, h], r=[self.I['state_hgrn']], w=[S])
                        else:
                            C.op('dve', lambda: nc.vector.memset(S[:], 0.0), w=[S])
                maxt = max(L // 128 for (_, L, _, _) in seqs)
                for step in range(maxt):
                    for si, (tok0, L, kind, idx) in enumerate(seqs):
                        ntile = L // 128
                        if step >= ntile:
                            continue
                        ti = step if d == 0 else ntile - 1 - step
                        tok = tok0 + ti * 128
                        b = nld % NB
                        nld += 1
                        q4, k4, kt, lf, vt = qT4[b], kT4[b], kt4[b], lf4[b], vt4[b]
                        C.dma('sp', q4.b.name, q4[:], pf[PF['hg_q']:PF['hg_q'] + 512, tok:tok + 128]
                              .rearrange("(h p) t -> p h t", p=128), r=[pf], w=[q4])
                        r0 = PF['hg_k'] + d * 512
                        C.dma('sp', k4.b.name, k4[:], pf[r0:r0 + 512, tok:tok + 128].rearrange("(h p) t -> p h t", p=128),
                              r=[pf], w=[k4])
                        c0 = PT['k'] + d * 512
                        C.dma('sp', kt.b.name, kt[:], pt[tok:tok + 128, c0:c0 + 512], r=[pt], w=[kt])
                        c0 = PT['logf'] + d * 512
                        C.dma('sp', lf.b.name, lf[:], pt[tok:tok + 128, c0:c0 + 512], r=[pt], w=[lf])
                        C.dma('sp', vt.b.name, vt[:], pt[tok:tok + 128, PT['v']:PT['v'] + 512], r=[pt], w=[vt])
                        if d == 0:
                            o4, g4 = ob4[b], gT4[b]
                            C.dma('sp', o4.b.name, o4[:], ob[0:512, tok:tok + 128].rearrange("(h p) t -> p h t", p=128),
                                  r=[ob], w=[o4])
                            C.dma('sp', g4.b.name, g4[:], pf[PF['hg_g']:PF['hg_g'] + 512, tok:tok + 128]
                                  .rearrange("(h p) t -> p h t", p=128), r=[pf], w=[g4])
                        oo = oo4[b]
                        for h in range(4):
                            S = Sst[(si, h)]
                            r = u % R
                            u += 1
                            hs = slice(h * 128, (h + 1) * 128)
                            p_b, p_l, p_s, p_o = self.pn(), self.pn(), self.pn(), self.pn()
                            self.mm(p_b, p_b[:], lf, lf[:, hs], cst, self.cc(tri))
                            self.mm(p_l, p_l[:], cst, self.cc(trx), lf, lf[:, hs])
                            self.act(ebT[r], ebT[r][:], p_b, p_b[:], AF.Exp)
                            self.act(enb[r], enb[r][:], p_b, p_b[:], AF.Exp, scale=-1.0)
                            self.vtt(qeT[r], qeT[r][:], q4, q4[:, h, :], ebT[r], ebT[r][:], ALU.mult)
                            self.vtt(keT[r], keT[r][:], k4, k4[:, h, :], enb[r], enb[r][:], ALU.mult, eng='pool')
                            self.act(elm[r], elm[r][:], p_l, p_l[:], AF.Exp)
                            self.vtt(elm[r], elm[r][:], kt, kt[:, hs], elm[r], elm[r][:], ALU.mult)
                            C.op('pool', lambda: nc.gpsimd.tensor_tensor(
                                kl4[r][:], elm[r][:].unsqueeze(1).to_broadcast([128, 4, 128]),
                                self.cc('hg_cm').rearrange("p (c s) -> p c s", c=4)[:, :, 0:1].to_broadcast([128, 4, 128]),
                                ALU.mult), r=[elm[r], cst], w=[kl4[r]])
                            self.mm(p_s, p_s[:], keT[r], keT[r][:], qeT[r], qeT[r][:])
                            self.vtt(scm[r], scm[r][:], p_s, p_s[:], cst, self.cc(tri), ALU.mult)
                            self.mm(p_o, p_o[:], vt, vt[:, hs], scm[r], scm[r][:], start=True, stop=False)
                            order = range(4) if d == 0 else range(3, -1, -1)
                            for n_, c in enumerate(order):
                                cs = slice(c * 32, (c + 1) * 32)
                                self.mm(p_o, p_o[:, cs], S, S[:], qeT[r], qeT[r][:, cs], start=False, stop=(n_ == 3))
                                p_u = self.pn()
                                self.mm(p_u, p_u[:], kl4[r], kl4[r][:, c, :], vt, vt[:, hs])
                                col = c * 32 + 31 if d == 0 else c * 32
                                C.op('dve', lambda: nc.vector.scalar_tensor_tensor(
                                    S[:], S[:], ebT[r][:, col:col + 1], p_u[:], ALU.mult, ALU.add),
                                    r=[S, ebT[r], p_u], w=[S])
                            if d == 1:
                                C.op('act', lambda: nc.scalar.copy(oo[:, h, :], p_o[:]), r=[p_o], w=[oo])
                            else:
                                osum = qeT[r]
                                self.vtt(osum, osum[:], p_o, p_o[:], o4, o4[:, h, :], ALU.add)
                                self.act(sqt[r], sqt[r][:], osum, osum[:], AF.Square)
                                p_m = self.pn()
                                self.mm(p_m, p_m[:], self.ones128, self.ones128[:], sqt[r], sqt[r][:])
                                self.rsqrt(rst[r], rst[r][:], p_m, p_m[:], self.eps128)
                                C.op('dve', lambda: nc.vector.scalar_tensor_tensor(
                                    osum[:], osum[:], hn[:, l * 4 + h:l * 4 + h + 1], rst[r][:], ALU.mult, ALU.mult),
                                    r=[osum, hn, rst[r]], w=[osum])
                                self.vtt(oo, oo[:, h, :], osum, osum[:], g4, g4[:, h, :], ALU.mult)
                            if kind == 'p' and step == ntile - 1:
                                dst = self.O['ns_hgrn']
                                C.dma('sp', S.b.name, dst[idx, l, d, h], S[:], r=[S], w=[dst])
                        tgt = ob if d == 1 else br
                        C.dma('sp', oo.b.name, tgt[0:512, tok:tok + 128].rearrange("(h p) t -> p h t", p=128), oo[:],
                              r=[oo], w=[tgt])
            C.barrier()

    def stage_gdn_pre(self, l):
        C, nc = self.C, self.nc
        pf = self.S['pf']
        PF = self.PF
        if 'gt' not in self.S:
            self.S['gt'] = self.dscr('gt', [self.NT, 1024])
        gt = self.S['gt']
        cw = self.P['gdn_conv']
        with ExitStack() as st:
            LM = max(self.LP, self.LS)
            xin = [self.sb(st, 'gx%d' % i, [128, LM]) for i in range(2)]
            xo = [self.sb(st, 'gxo%d' % i, [128, LM]) for i in range(2)]
            sq = [self.sb(st, 'gsq%d' % i, [128, 512]) for i in range(2)]
            rs = [self.sb(st, 'grs%d' % i, [128, 512]) for i in range(2)]
            tr = [self.sb(st, 'gtr%d' % i, [128, 128]) for i in range(3)]
            n = nq = ntr = 0
            for (tok0, L, kind, idx) in self.seqs():
                if kind == 's':
                    rows, W, dys = L // 64, 64, (0, 1, 2)
                else:
                    rows, W, dys = 1, L, (1,)
                for cc in range(12):
                    xi, xo_ = xin[n % 2], xo[n % 2]
                    n += 1
                    r0 = PF['gd_qkv'] + cc * 128
                    C.dma('sp', xi.b.name, xi[:, 0:L], pf[r0:r0 + 128, tok0:tok0 + L], r=[pf], w=[xi])
                    wc = lambda dy, dx: cw[:, l * 108 + (dy * 3 + dx) * 12 + cc:l * 108 + (dy * 3 + dx) * 12 + cc + 1]
                    C.op('dve', lambda: nc.vector.tensor_scalar(xo_[:, 0:L], xi[:, 0:L], wc(1, 1), None, ALU.mult),
                         r=[xi, cw], w=[xo_])
                    x3 = xi[:, 0:L].rearrange("p (r w) -> p r w", w=W)
                    o3 = xo_[:, 0:L].rearrange("p (r w) -> p r w", w=W)
                    for dy in dys:
                        for dx in range(3):
                            if dy == 1 and dx == 1:
                                continue
                            oy, ox = dy - 1, dx - 1
                            ra, rb = max(0, -oy), rows - max(0, oy)
                            ca, cb = max(0, -ox), W - max(0, ox)
                            if rb <= ra:
                                continue
                            C.op('dve', lambda: nc.vector.scalar_tensor_tensor(
                                o3[:, ra:rb, ca:cb], x3[:, ra + oy:rb + oy, ca + ox:cb + ox], wc(dy, dx),
                                o3[:, ra:rb, ca:cb], ALU.mult, ALU.add), r=[xi, cw, xo_], w=[xo_])
                    self.act(xo_, xo_[:, 0:L], xo_, xo_[:, 0:L], AF.Silu)
                    if cc < 8:
                        for t0 in range(0, L, 512):
                            tt = min(512, L - t0)
                            s_, r_ = sq[nq % 2], rs[nq % 2]
                            pp = self.ps[nq % 2]
                            nq += 1
                            self.act(s_, s_[:, 0:tt], xo_, xo_[:, t0:t0 + tt], AF.Square)
                            self.mm(pp, pp[:, 0:tt], self.cst, self.cc('ones'), s_, s_[:, 0:tt])
                            self.rsqrt(r_, r_[:, 0:tt], pp, pp[:, 0:tt], self.epsc)
                            self.vtt(xo_, xo_[:, t0:t0 + tt], xo_, xo_[:, t0:t0 + tt], r_, r_[:, 0:tt], ALU.mult)
                    C.dma('sp', xo_.b.name, pf[r0:r0 + 128, tok0:tok0 + L], xo_[:, 0:L], r=[xo_], w=[pf])
                    if cc >= 4:
                        for t0 in range(0, L, 128):
                            p_ = self.pn()
                            t_ = tr[ntr % 3]
                            ntr += 1
                            self.mm(p_, p_[:], xo_, xo_[:, t0:t0 + 128], self.cst, self.cc('ident'))
                            C.op('act', lambda: nc.scalar.copy(t_[:], p_[:]), r=[p_], w=[t_])
                            c0 = (cc - 4) * 128
                            C.dma('sp', t_.b.name, gt[tok0 + t0:tok0 + t0 + 128, c0:c0 + 128], t_[:], r=[t_], w=[gt])
            C.barrier()

    def finalize_branch(self, p_o, o4h, gh, ncol_t, ncol, eps_t, out_t, out_ap, scr, sqt_, rst_, onesm):
        C, nc = self.C, self.nc
        self.vtt(scr, scr[:], p_o, p_o[:], o4h[0], o4h[1], ALU.add)
        self.act(sqt_, sqt_[:], scr, scr[:], AF.Square)
        p_m = self.pn()
        self.mm(p_m, p_m[:], onesm, onesm[:], sqt_, sqt_[:])
        self.rsqrt(rst_, rst_[:], p_m, p_m[:], eps_t)
        C.op('dve', lambda: nc.vector.scalar_tensor_tensor(scr[:], scr[:], ncol, rst_[:], ALU.mult, ALU.mult),
             r=[scr, ncol_t, rst_], w=[scr])
        self.vtt(out_t, out_ap, scr, scr[:], gh[0], gh[1], ALU.mult)

    def stage_gdn(self, l):
        C, nc = self.C, self.nc
        pf, pt, gt, ob, br = self.S['pf'], self.S['pt'], self.S['gt'], self.S['ob'], self.S['br']
        PF, PT = self.PF, self.PT
        cst = self.cst
        seqs = self.seqs()
        ident = self.cc('ident')
        with ExitStack() as st:
            Sst = {(si, h): self.sb(st, 'gS%d_%d' % (si, h), [128, 128]) for si in range(len(seqs)) for h in range(4)}
            NB = 3
            qT4 = [self.sb(st, 'gq%d' % i, [128, 4, 128]) for i in range(NB)]
            kT4 = [self.sb(st, 'gk%d' % i, [128, 4, 128]) for i in range(NB)]
            kv4 = [self.sb(st, 'gkv%d' % i, [128, 1024]) for i in range(NB)]
            ab4 = [self.sb(st, 'gab%d' % i, [128, 16]) for i in range(NB)]
            nb4 = [self.sb(st, 'gnb%d' % i, [128, 8]) for i in range(NB)]
            ob4 = [self.sb(st, 'gob%d' % i, [128, 4, 128]) for i in range(NB)]
            gT4 = [self.sb(st, 'gg%d' % i, [128, 4, 128]) for i in range(NB)]
            oo4 = [self.sb(st, 'goo%d' % i, [128, 4, 128]) for i in range(NB)]
            R = 3
            names = ['lgb', 'lgbn', 'DT', 'DTs', 'EG', 'PT', 'Ua', 'Ub', 'La', 'Lb', 'Pa', 'Pb', 'keg', 'qd', 'kd', 'X',
                     'vn', 'scr', 'sq', 'rs']
            W = {nm: [self.sb(st, 'g%s%d' % (nm, i), [128, 128]) for i in range(R)] for nm in names}
            sc = [self.sb(st, 'gsc%d' % i, [128, 4]) for i in range(R)]
            gn = self.P['gdn_norm']
            u = nld = 0
            for d in (1, 0):
                tri = self.cc('gd_tri_f' if d == 0 else 'gd_tri_b')
                neg = self.cc('gd_neg_f' if d == 0 else 'gd_neg_b')
                stm = self.cc('gd_st_f' if d == 0 else 'gd_st_b')
                for si, (tok0, L, kind, idx) in enumerate(seqs):
                    for h in range(4):
                        S = Sst[(si, h)]
                        if kind == 's':
                            C.dma('sp', S.b.name, S[:], self.I['state_gdn'][l, d, h], r=[self.I['state_gdn']], w=[S])
                        else:
                            C.op('dve', lambda: nc.vector.memset(S[:], 0.0), w=[S])
                maxt = max(L // 128 for (_, L, _, _) in seqs)
                for step in range(maxt):
                    for si, (tok0, L, kind, idx) in enumerate(seqs):
                        ntile = L // 128
                        if step >= ntile:
                            continue
                        ti = step if d == 0 else ntile - 1 - step
                        tok = tok0 + ti * 128
                        b = nld % NB
                        nld += 1
                        q4, k4, kv, ab, nb_ = qT4[b], kT4[b], kv4[b], ab4[b], nb4[b]
                        r0 = PF['gd_qkv']
                        C.dma('sp', q4.b.name, q4[:], pf[r0:r0 + 512, tok:tok + 128].rearrange("(h p) t -> p h t", p=128),
                              r=[pf], w=[q4])
                        C.dma('sp', k4.b.name, k4[:], pf[r0 + 512:r0 + 1024, tok:tok + 128]
                              .rearrange("(h p) t -> p h t", p=128), r=[pf], w=[k4])
                        C.dma('sp', kv.b.name, kv[:], gt[tok:tok + 128, :], r=[gt], w=[kv])
                        C.dma('sp', ab.b.name, ab[:], pt[tok:tok + 128, PT['lg']:PT['lg'] + 16], r=[pt], w=[ab])
                        C.op('dve', lambda: nc.vector.tensor_scalar(nb_[:], ab[:, 8:16], -1.0, None, ALU.mult),
                             r=[ab], w=[nb_])
                        if d == 0:
                            o4, g4 = ob4[b], gT4[b]
                            C.dma('sp', o4.b.name, o4[:], ob[512:1024, tok:tok + 128].rearrange("(h p) t -> p h t", p=128),
                                  r=[ob], w=[o4])
                            C.dma('sp', g4.b.name, g4[:], pf[PF['gd_z']:PF['gd_z'] + 512, tok:tok + 128]
                                  .rearrange("(h p) t -> p h t", p=128), r=[pf], w=[g4])
                        oo = oo4[b]
                        for h in range(4):
                            S = Sst[(si, h)]
                            r = u % R
                            u += 1
                            w = {nm: W[nm][r] for nm in names}
                            hs = slice(h * 128, (h + 1) * 128)
                            lgc = ab[:, d * 4 + h:d * 4 + h + 1]
                            bc = ab[:, 8 + d * 4 + h:8 + d * 4 + h + 1]
                            nbc = nb_[:, d * 4 + h:d * 4 + h + 1]
                            kT, qT = k4[:, h, :], q4[:, h, :]
                            ktok, vtok = kv[:, hs], kv[:, 512 + h * 128:512 + (h + 1) * 128]
                            C.op('dve', lambda: nc.vector.tensor_scalar(w['lgb'][:], self.cc('ones'), lgc, None, ALU.mult),
                                 r=[cst, ab], w=[w['lgb']])
                            C.op('pool', lambda: nc.gpsimd.tensor_scalar(w['lgbn'][:], w['lgb'][:], -1.0, None, ALU.mult),
                                 r=[w['lgb']], w=[w['lgbn']])
                            p_d, p_g, p_c, p_kk, p_qk = self.pn(), self.pn(), self.pn(), self.pn(), self.pn()
                            self.mm(p_d, p_d[:], w['lgb'], w['lgb'][:], cst, tri, True, False)
                            self.mm(p_d, p_d[:], cst, tri, w['lgbn'], w['lgbn'][:], False, False)
                            self.mm(p_d, p_d[:], cst, ident, cst, neg, False, True)
                            self.mm(p_g, p_g[:], w['lgb'], w['lgb'][:], cst, tri)
                            self.mm(p_c, p_c[:, 0:1], cst, tri, ab, lgc)
                            self.mm(p_c, p_c[:, 1:2], cst, self.cc('ones'), ab, lgc)
                            self.mm(p_kk, p_kk[:], k4, kT, k4, kT)
                            self.mm(p_qk, p_qk[:], k4, kT, q4, qT)
                            self.act(w['DT'], w['DT'][:], p_d, p_d[:], AF.Exp)
                            self.act(w['EG'], w['EG'][:], p_g, p_g[:], AF.Exp)
                            s_ = sc[r]
                            C.op('dve', lambda: nc.vector.tensor_copy(s_[:, 0:1], p_c[:, 1:2]), r=[p_c], w=[s_])
                            self.act(s_, s_[:, 1:2], p_c, p_c[:, 1:2], AF.Exp)
                            self.act(s_, s_[:, 2:3], p_c, p_c[:, 0:1], AF.Exp, bias=s_[:, 0:1], scale=-1.0, extra=[s_])
                            self.vtt(w['DTs'], w['DTs'][:], w['DT'], w['DT'][:], cst, stm, ALU.mult, eng='pool')
                            U, Lm, P_ = w['Ua'], w['La'], w['Pa']
                            U2, L2, P2 = w['Ub'], w['Lb'], w['Pb']
                            C.op('dve', lambda: nc.vector.scalar_tensor_tensor(U[:], p_kk[:], nbc, w['DTs'][:],
                                                                               ALU.mult, ALU.mult),
                                 r=[p_kk, nb_, w['DTs']], w=[U])
                            self.vtt(w['PT'], w['PT'][:], p_qk, p_qk[:], w['DT'], w['DT'][:], ALU.mult)
                            p_t = self.pn()
                            self.mm(p_t, p_t[:], U, U[:], cst, ident)
                            C.op('act', lambda: nc.scalar.copy(Lm[:], p_t[:]), r=[p_t], w=[Lm])
                            self.vtt(P_, P_[:], U, U[:], cst, ident, ALU.add, eng='pool')
                            for n_ in range(6):
                                p_u, p_l, p_p = self.pn(), self.pn(), self.pn()
                                self.mm(p_u, p_u[:], Lm, Lm[:], U, U[:])
                                self.mm(p_l, p_l[:], U, U[:], Lm, Lm[:])
                                C.op('act', lambda: nc.scalar.copy(U2[:], p_u[:]), r=[p_u], w=[U2])
                                C.op('dve', lambda: nc.vector.tensor_copy(L2[:], p_l[:]), r=[p_l], w=[L2])
                                self.mm(p_p, p_p[:], L2, L2[:], P_, P_[:])
                                self.vtt(P2, P2[:], p_p, p_p[:], P_, P_[:], ALU.add)
                                U, U2, Lm, L2, P_, P2 = U2, U, L2, Lm, P2, P_
                            self.vtt(w['keg'], w['keg'][:], k4, kT, w['EG'], w['EG'][:], ALU.mult, eng='pool')
                            self.vtt(w['qd'], w['qd'][:], q4, qT, w['EG'], w['EG'][:], ALU.mult, eng='pool')
                            C.op('dve', lambda: nc.vector.tensor_scalar(w['kd'][:], ktok, s_[:, 2:3], None, ALU.mult),
                                 r=[kv, s_], w=[w['kd']])
                            p_x, p_v, p_o, p_k = self.pn(), self.pn(), self.pn(), self.pn()
                            self.mm(p_x, p_x[:], w['keg'], w['keg'][:], S, S[:])
                            self.vtt(w['X'], w['X'][:], kv, vtok, p_x, p_x[:], ALU.subtract)
                            self.mm(p_v, p_v[:], P_, P_[:], w['X'], w['X'][:])
                            C.op('dve', lambda: nc.vector.tensor_scalar(w['vn'][:], p_v[:], bc, None, ALU.mult),
                                 r=[p_v, ab], w=[w['vn']])
                            self.mm(p_o, p_o[:], S, S[:], w['qd'], w['qd'][:], True, False)
                            self.mm(p_o, p_o[:], w['vn'], w['vn'][:], w['PT'], w['PT'][:], False, True)
                            self.mm(p_k, p_k[:], w['kd'], w['kd'][:], w['vn'], w['vn'][:])
                            C.op('dve', lambda: nc.vector.scalar_tensor_tensor(S[:], S[:], s_[:, 1:2], p_k[:],
                                                                               ALU.mult, ALU.add),
                                 r=[S, s_, p_k], w=[S])
                            if d == 1:
                                C.op('act', lambda: nc.scalar.copy(oo[:, h, :], p_o[:]), r=[p_o], w=[oo])
                            else:
                                self.finalize_branch(p_o, (o4, o4[:, h, :]), (g4, g4[:, h, :]), gn, gn[:, l:l + 1],
                                                     self.eps128, oo, oo[:, h, :], w['scr'], w['sq'], w['rs'],
                                                     self.ones128)
                            if kind == 'p' and step == ntile - 1:
                                dst = self.O['ns_gdn']
                                C.dma('sp', S.b.name, dst[idx, l, d, h], S[:], r=[S], w=[dst])
                        tgt = ob if d == 1 else br
                        C.dma('sp', oo.b.name, tgt[512:1024, tok:tok + 128].rearrange("(h p) t -> p h t", p=128), oo[:],
                              r=[oo], w=[tgt])
            C.barrier()

    def stage_rwkv(self, l):
        C, nc = self.C, self.nc
        pf, ob, br = self.S['pf'], self.S['ob'], self.S['br']
        if 'bb' not in self.S:
            self.S['bb'] = self.dscr('bb', [512, self.NT])
        bb = self.S['bb']
        PF = self.PF
        cst = self.cst
        ident = self.cc('ident')
        bones = self.cc('bones64')
        blk4 = self.cc('rw_blk').rearrange("p (c h s) -> p c h s", c=2, h=2)
        seqs = self.seqs()
        I = self.I
        CW = 0.6065306597126334
        P = self.P
        with ExitStack() as st:
            w2t = self.sb(st, 'rw2', [128, 512])
            a2t = self.sb(st, 'ra2', [128, 512])
            g2t = self.sb(st, 'rg2', [128, 512])
            w0r = self.sb(st, 'rw0r', [128, 1024])
            omka = self.sb(st, 'romka', [128, 4])
            C.dma('sp', 'rw2', w2t[:], I['rwkv_w2'][l].rearrange("d r c -> (d r) c"), r=[I['rwkv_w2']], w=[w2t])
            C.dma('sp', 'ra2', a2t[:], I['rwkv_a2'][l].rearrange("d r c -> (d r) c"), r=[I['rwkv_a2']], w=[a2t])
            C.dma('sp', 'rg2', g2t[:], I['rwkv_g2'][l], r=[I['rwkv_g2']], w=[g2t])
            C.dma('sp', 'rw0r', w0r[:], I['rwkv_w0'][l].rearrange("d c -> (d c)").partition_broadcast(128),
                  r=[I['rwkv_w0']], w=[w0r])
            ka = P['rwkv_ka']
            C.op('dve', lambda: nc.vector.tensor_scalar(omka[:], ka[:, l * 4:l * 4 + 4], -1.0, 1.0, ALU.mult, ALU.add),
                 r=[ka], w=[omka])
            Zst = {(si, p): self.sb(st, 'rZ%d_%d' % (si, p), [128, 128]) for si in range(len(seqs)) for p in range(4)}
            NB = 2
            zstg = [self.sb(st, 'rzs%d' % i, [128, 128]) for i in range(2)]
            r4 = [self.sb(st, 'rr%d' % i, [128, 4, 128]) for i in range(NB)]
            k4 = [self.sb(st, 'rk%d' % i, [128, 4, 128]) for i in range(NB)]
            v4 = [self.sb(st, 'rv%d' % i, [128, 4, 128]) for i in range(NB)]
            wl = [self.sb(st, 'rwl%d' % i, [128, 128]) for i in range(NB)]
            al = [self.sb(st, 'ral%d' % i, [128, 128]) for i in range(NB)]
            gd = [self.sb(st, 'rgd%d' % i, [128, 128]) for i in range(NB)]
            yb4 = [self.sb(st, 'ryb%d' % i, [128, 4, 128]) for i in range(NB)]
            bb4 = [self.sb(st, 'rbb%d' % i, [128, 4, 128]) for i in range(NB)]
            oo4 = [self.sb(st, 'roo%d' % i, [128, 4, 128]) for i in range(NB)]
            pb4 = [self.sb(st, 'rpb%d' % i, [128, 4, 128]) for i in range(NB)]
            R = 2
            n128 = ['kk', 'sq', 'rs', 'sz', 'Ep', 'Em', 'Ex', 'El', 'icl', 't1', 'kd', 'bT', 'rt', 'og', 'ys', 'yc', 'pr',
                    'X', 'Ub', 'M0', 'Aak', 'Ua', 'Ub2', 'La', 'Lb', 'Pa', 'Pb', 'Bbd', 'Kbd', 'Vbd0', 'Vbd1']
            W = {nm: [self.sb(st, 'r%s%d' % (nm, i), [128, 128]) for i in range(R)] for nm in n128}
            n256 = ['ve', 'at2', 'ate', 'bte', 'kte', 'Bhe', 'Khe', 'tmp']
            W2 = {nm: [self.sb(st, 'r%s%d' % (nm, i), [128, 256]) for i in range(R)] for nm in n256}
            n64 = ['Arb', 'Ark']
            W3 = {nm: [self.sb(st, 'r%s%d' % (nm, i), [128, 64]) for i in range(R)] for nm in n64}
            u = nld = 0

            def expand(dst, src_t, src_ap, eng='pool'):
                e = nc.gpsimd if eng == 'pool' else nc.vector
                C.op(eng, lambda: e.tensor_tensor(
                    dst[:].rearrange("p (c h s) -> p c h s", c=2, h=2),
                    src_ap.rearrange("p (c s) -> p c s", c=2).unsqueeze(2).to_broadcast([128, 2, 2, 64]),
                    blk4, ALU.mult), r=[src_t, cst], w=[dst])

            def ch(t2, c):
                return t2[:].rearrange("p (c m) -> p c m", c=2)[:, c, :]

            for d in (1, 0):
                f = (d == 0)
                tri = self.cc('rw_tri_f' if f else 'rw_tri_b')
                trx = self.cc('rw_trx_f' if f else 'rw_trx_b')
                tra = self.cc('rw_trx_b' if f else 'rw_trx_f')
                bst = self.cc('rw_bst_f' if f else 'rw_bst_b')
                sin = self.cc('rw_sin_f' if f else 'rw_sin_b')
                for si, (tok0, L, kind, idx) in enumerate(seqs):
                    for p in range(4):
                        Z = Zst[(si, p)]
                        if kind == 's':
                            zs = zstg[(si + p) % 2]
                            C.op('dve', lambda: nc.vector.memset(zs[:], 0.0), w=[zs])
                            for hp in range(2):
                                C.dma('sp', zs.b.name, zs[hp * 64:(hp + 1) * 64, hp * 64:(hp + 1) * 64],
                                      I['state_rwkv'][l, d, p * 2 + hp], r=[I['state_rwkv']], w=[zs])
                            p_ = self.pn()
                            self.mm(p_, p_[:], zs, zs[:], cst, ident)
                            C.op('dve', lambda: nc.vector.tensor_copy(Z[:], p_[:]), r=[p_], w=[Z])
                        else:
                            C.op('dve', lambda: nc.vector.memset(Z[:], 0.0), w=[Z])
                maxt = max(L // 128 for (_, L, _, _) in seqs)
                for step in range(maxt):
                    for si, (tok0, L, kind, idx) in enumerate(seqs):
                        ntile = L // 128
                        if step >= ntile:
                            continue
                        ti = step if f else ntile - 1 - step
                        tok = tok0 + ti * 128
                        b = nld % NB
                        nld += 1
                        rr, kk_, vv, wl_, al_, gd_ = r4[b], k4[b], v4[b], wl[b], al[b], gd[b]
                        r0 = PF['rw_rkv']
                        for j_, t_ in enumerate((rr, kk_, vv)):
                            C.dma('sp', t_.b.name, t_[:], pf[r0 + j_ * 512:r0 + (j_ + 1) * 512, tok:tok + 128]
                                  .rearrange("(h p) t -> p h t", p=128), r=[pf], w=[t_])
                        C.dma('sp', wl_.b.name, wl_[:], pf[PF['rw_wd']:PF['rw_wd'] + 128, tok:tok + 128], r=[pf], w=[wl_])
                        C.dma('sp', al_.b.name, al_[:], pf[PF['rw_ad']:PF['rw_ad'] + 128, tok:tok + 128], r=[pf], w=[al_])
                        if f:
                            C.dma('sp', gd_.b.name, gd_[:], pf[PF['rw_gd']:PF['rw_gd'] + 128, tok:tok + 128],
                                  r=[pf], w=[gd_])
                            yb, bbt = yb4[b], bb4[b]
                            C.dma('sp', yb.b.name, yb[:], ob[1024:1536, tok:tok + 128].rearrange("(h p) t -> p h t", p=128),
                                  r=[ob], w=[yb])
                            C.dma('sp', bbt.b.name, bbt[:], bb[:, tok:tok + 128].rearrange("(h p) t -> p h t", p=128),
                                  r=[bb], w=[bbt])
                        oo, pbo = oo4[b], pb4[b]
                        for p in range(4):
                            Z = Zst[(si, p)]
                            rix = u % R
                            u += 1
                            w = {nm: W[nm][rix] for nm in n128}
                            w.update({nm: W2[nm][rix] for nm in n256})
                            w.update({nm: W3[nm][rix] for nm in n64})
                            ps_ = slice(p * 128, (p + 1) * 128)
                            rT, kT, vT = rr[:, p, :], kk_[:, p, :], vv[:, p, :]
                            col = lambda nm: P[nm][:, l * 4 + p:l * 4 + p + 1]
                            C.op('dve', lambda: nc.vector.tensor_scalar(w['kk'][:], kT, col('rwkv_kk'), None, ALU.mult),
                                 r=[kk_, P['rwkv_kk']], w=[w['kk']])
                            self.act(w['sq'], w['sq'][:], w['kk'], w['kk'][:], AF.Square)
                            p_ = self.pn()
                            self.mm(p_, p_[:], cst, bones, w['sq'], w['sq'][:])
                            self.rsqrt(w['rs'], w['rs'][:], p_, p_[:], self.epsc)
                            self.vtt(w['kk'], w['kk'][:], w['kk'], w['kk'][:], w['rs'], w['rs'][:], ALU.mult, eng='pool')
                            expand(w['ve'], vv, vT)
                            for c in range(2):
                                p_ = self.pn()
                                self.mm(p_, p_[:], w['ve'], ch(w['ve'], c), cst, ident)
                                vb = w['Vbd%d' % c]
                                C.op('act', lambda: nc.scalar.copy(vb[:], p_[:]), r=[p_], w=[vb])
                            if RWL < 2:
                                continue
                            dsl = slice(d * 64, (d + 1) * 64)
                            p_z = self.pn()
                            self.mm(p_z, p_z[:], wl_, wl_[dsl, :], w2t, w2t[dsl, ps_])
                            self.vtt(w['sz'], w['sz'][:], p_z, p_z[:], w0r, w0r[:, d * 512 + p * 128:d * 512 + (p + 1) * 128],
                                     ALU.add)
                            self.act(w['sz'], w['sz'][:], w['sz'], w['sz'][:], AF.Sigmoid)
                            p_c, p_x_, p_a = self.pn(), self.pn(), self.pn()
                            self.mm(p_c, p_c[:], w['sz'], w['sz'][:], cst, tri)
                            self.mm(p_x_, p_x_[:], w['sz'], w['sz'][:], cst, trx)
                            self.mm(p_a, p_a[:], w['sz'], w['sz'][:], cst, tra)
                            self.act(w['Ep'], w['Ep'][:], p_c, p_c[:], AF.Exp, scale=-CW)
                            self.act(w['Em'], w['Em'][:], p_c, p_c[:], AF.Exp, scale=CW)
                            self.act(w['Ex'], w['Ex'][:], p_x_, p_x_[:], AF.Exp, scale=-CW)
                            self.act(w['El'], w['El'][:], p_a, p_a[:], AF.Exp, scale=-CW)
                            p_i = self.pn()
                            self.mm(p_i, p_i[:], a2t, a2t[dsl, ps_], al_, al_[dsl, :])
                            a0 = P['rwkv_a0']
                            self.act(w['icl'], w['icl'][:], p_i, p_i[:], AF.Sigmoid,
                                     bias=a0[:, l * 8 + d * 4 + p:l * 8 + d * 4 + p + 1], extra=[a0])
                            C.op('dve', lambda: nc.vector.tensor_scalar(w['t1'][:], w['icl'][:], col('rwkv_ka'),
                                                                        omka[:, p:p + 1], ALU.mult, ALU.add),
                                 r=[w['icl'], ka, omka], w=[w['t1']])
                            self.vtt(w['kd'], w['kd'][:], kk_, kT, w['t1'], w['t1'][:], ALU.mult, eng='pool')
                            self.vtt(w['bT'], w['bT'][:], w['kk'], w['kk'][:], w['icl'], w['icl'][:], ALU.mult, eng='pool')
                            C.op('dve', lambda: nc.vector.scalar_tensor_tensor(w['pr'][:], rr[:, p, :], col('rwkv_rk'),
                                                                               w['kd'][:], ALU.mult, ALU.mult),
                                 r=[rr, P['rwkv_rk'], w['kd']], w=[w['pr']])
                            if RWL < 3:
                                continue
                            self.vtt(w['rt'], w['rt'][:], rr, rT, w['Ep'], w['Ep'][:], ALU.mult)
                            tmp = w['tmp']
                            C.op('dve', lambda: nc.vector.scalar_tensor_tensor(w['t1'][:], w['kk'][:], -1.0, w['Ex'][:],
                                                                               ALU.mult, ALU.mult),
                                 r=[w['kk'], w['Ex']], w=[w['t1']])
                            expand(w['ate'], w['t1'], w['t1'][:])
                            C.op('pool', lambda: nc.gpsimd.tensor_copy(
                                w['at2'][:].rearrange("p (c h s) -> p c h s", c=2, h=2),
                                w['t1'][:].rearrange("p (c s) -> p c s", c=2).unsqueeze(2).to_broadcast([128, 2, 2, 64])),
                                r=[w['t1']], w=[w['at2']])
                            self.vtt(w['sq'], w['sq'][:], w['bT'], w['bT'][:], w['Em'], w['Em'][:], ALU.mult)
                            expand(w['bte'], w['sq'], w['sq'][:])
                            self.vtt(w['rs'], w['rs'][:], w['kd'], w['kd'][:], w['Em'], w['Em'][:], ALU.mult)
                            expand(w['kte'], w['rs'], w['rs'][:])
                            self.vtt(w['ys'], w['ys'][:], w['bT'], w['bT'][:], w['El'], w['El'][:], ALU.mult)
                            expand(w['Bhe'], w['ys'], w['ys'][:])
                            self.vtt(w['yc'], w['yc'][:], w['kd'], w['kd'][:], w['El'], w['El'][:], ALU.mult)
                            expand(w['Khe'], w['yc'], w['yc'][:])
                            if RWL < 4:
                                continue
                            p_ys = (self.pnl(), self.pnl())
                            for c in ((0, 1) if f else (1, 0)):
                                p_y = p_ys[c]
                                cs_ = slice(c * 64, (c + 1) * 64)
                                vb = w['Vbd%d' % c]
                                p_ab, p_ak, p_rb, p_rk = self.pn(), self.pn(), self.pn(), self.pn()
                                self.mm(p_ab, p_ab[:], w['bte'], ch(w['bte'], c), w['at2'], ch(w['at2'], c))
                                self.mm(p_ak, p_ak[:], w['kte'], ch(w['kte'], c), w['at2'], ch(w['at2'], c))
                                self.mm(p_rb, p_rb[:, 0:64], w['bte'], ch(w['bte'], c), w['rt'], w['rt'][:, cs_])
                                self.mm(p_rk, p_rk[:, 0:64], w['kte'], ch(w['kte'], c), w['rt'], w['rt'][:, cs_])
                                U, Lm, P_ = w['Ua'], w['La'], w['Pa']
                                U2, L2, P2 = w['Ub2'], w['Lb'], w['Pb']
                                self.vtt(U, U[:], p_ab, p_ab[:], cst, bst, ALU.mult)
                                self.vtt(w['Aak'], w['Aak'][:], p_ak, p_ak[:], cst, bst, ALU.mult)
                                self.vtt(w['Arb'], w['Arb'][:], p_rb, p_rb[:, 0:64], cst, sin, ALU.mult)
                                self.vtt(w['Ark'], w['Ark'][:], p_rk, p_rk[:, 0:64], cst, sin, ALU.mult)
                                if RWL < 5:
                                    continue
                                p_t = self.pn()
                                self.mm(p_t, p_t[:], U, U[:], cst, ident)
                                C.op('act', lambda: nc.scalar.copy(Lm[:], p_t[:]), r=[p_t], w=[Lm])
                                self.vtt(P_, P_[:], U, U[:], cst, ident, ALU.add, eng='pool')
                                for n_ in range(5):
                                    p_u, p_l, p_p = self.pn(), self.pn(), self.pn()
                                    self.mm(p_u, p_u[:], Lm, Lm[:], U, U[:])
                                    self.mm(p_l, p_l[:], U, U[:], Lm, Lm[:])
                                    C.op('act', lambda: nc.scalar.copy(U2[:], p_u[:]), r=[p_u], w=[U2])
                                    C.op('dve', lambda: nc.vector.tensor_copy(L2[:], p_l[:]), r=[p_l], w=[L2])
                                    self.mm(p_p, p_p[:], L2, L2[:], P_, P_[:])
                                    self.vtt(P2, P2[:], p_p, p_p[:], P_, P_[:], ALU.add)
                                    U, U2, Lm, L2, P_, P2 = U2, U, L2, Lm, P2, P_
                                if RWL < 6:
                                    continue
                                p_b1, p_k1 = self.pn(), self.pn()
                                self.mm(p_b1, p_b1[:], w['Bhe'], ch(w['Bhe'], c), cst, ident)
                                self.mm(p_k1, p_k1[:], w['Khe'], ch(w['Khe'], c), cst, ident)
                                C.op('act', lambda: nc.scalar.copy(w['Bbd'][:], p_b1[:]), r=[p_b1], w=[w['Bbd']])
                                C.op('dve', lambda: nc.vector.tensor_copy(w['Kbd'][:], p_k1[:]), r=[p_k1], w=[w['Kbd']])
                                if RWL < 7:
                                    continue
                                p_x, p_u2, p_zz = self.pn(), self.pn(), self.pn()
                                self.mm(p_x, p_x[:], w['ate'], ch(w['ate'], c), Z, Z[:], True, False)
                                self.mm(p_x, p_x[:], w['Aak'], w['Aak'][:], vb, vb[:], False, True)
                                C.op('act', lambda: nc.scalar.copy(w['X'][:], p_x[:]), r=[p_x], w=[w['X']])
                                self.mm(p_u2, p_u2[:], P_, P_[:], w['X'], w['X'][:])
                                C.op('dve', lambda: nc.vector.tensor_copy(w['Ub'][:], p_u2[:]), r=[p_u2], w=[w['Ub']])
                                if RWL < 8:
                                    continue
                                RWV = int(_os.environ.get('RWV', '0'))
                                if RWV == 0:
                                    self.mm(p_y, p_y[:, 0:64], Z, Z[:], w['rt'], w['rt'][:, cs_], True, False)
                                    self.mm(p_y, p_y[:, 0:64], w['Ub'], w['Ub'][:], w['Arb'], w['Arb'][:], False, False)
                                    self.mm(p_y, p_y[:, 0:64], vb, vb[:], w['Ark'], w['Ark'][:], False, True)
                                elif RWV == 1:
                                    self.mm(p_y, p_y[:, 0:64], Z, Z[:], w['rt'], w['rt'][:, cs_], True, True)
                                elif RWV == 2:
                                    self.mm(p_y, p_y[:, 0:64], w['Ub'], w['Ub'][:], w['Arb'], w['Arb'][:], True, True)
                                elif RWV == 3:
                                    self.mm(p_y, p_y[:, 0:64], vb, vb[:], w['Ark'], w['Ark'][:], True, True)
                                if RWL < 9:
                                    continue
                                self.mm(p_zz, p_zz[:], w['Bbd'], w['Bbd'][:], w['Ub'], w['Ub'][:], True, False)
                                self.mm(p_zz, p_zz[:], w['Kbd'], w['Kbd'][:], vb, vb[:], False, True)
                                gcol = c * 64 + 63 if f else c * 64
                                C.op('dve', lambda: nc.vector.scalar_tensor_tensor(
                                    Z[:], Z[:], w['Ep'][:, gcol:gcol + 1], p_zz[:], ALU.mult, ALU.add),
                                    r=[Z, w['Ep'], p_zz], w=[Z])
                            if RWL < 10:
                                continue
                            if not f:
                                for c in range(2):
                                    C.op('act', lambda: nc.scalar.copy(oo[:, p, c * 64:(c + 1) * 64], p_ys[c][:, 0:64]),
                                         r=[p_ys[c]], w=[oo])
                                C.op('pool', lambda: nc.gpsimd.tensor_copy(pbo[:, p, :], w['pr'][:]), r=[w['pr']], w=[pbo])
                            else:
                                p_g = self.pn()
                                self.mm(p_g, p_g[:], g2t, g2t[:, ps_], gd_, gd_[:])
                                C.op('act', lambda: nc.scalar.copy(w['og'][:], p_g[:]), r=[p_g], w=[w['og']])
                                for c in range(2):
                                    self.vtt(w['ys'], w['ys'][:, c * 64:(c + 1) * 64], p_ys[c], p_ys[c][:, 0:64], yb,
                                             yb[:, p, c * 64:(c + 1) * 64], ALU.add)
                                p_m = self.pn()
                                self.mm(p_m, p_m[:], cst, bones, w['ys'], w['ys'][:])
                                C.op('dve', lambda: nc.vector.scalar_tensor_tensor(
                                    w['yc'][:], p_m[:], -1.0 / 64, w['ys'][:], ALU.mult, ALU.add),
                                    r=[p_m, w['ys']], w=[w['yc']])
                                self.act(w['sq'], w['sq'][:], w['yc'], w['yc'][:], AF.Square)
                                p_v = self.pn()
                                self.mm(p_v, p_v[:], cst, bones, w['sq'], w['sq'][:])
                                self.rsqrt(w['rs'], w['rs'][:], p_v, p_v[:], self.epsgn, scale=1.0 / 64)
                                C.op('dve', lambda: nc.vector.scalar_tensor_tensor(
                                    w['yc'][:], w['yc'][:], col('rwkv_ln_g'), w['rs'][:], ALU.mult, ALU.mult),
                                    r=[w['yc'], P['rwkv_ln_g'], w['rs']], w=[w['yc']])
                                self.vtt(w['pr'], w['pr'][:], w['pr'], w['pr'][:], bbt, bbt[:, p, :], ALU.add, eng='pool')
                                p_bn = self.pn()
                                self.mm(p_bn, p_bn[:], cst, bones, w['pr'], w['pr'][:])
                                self.vtt(w['ys'], w['ys'][:], p_bn, p_bn[:], vv, vT, ALU.mult)
                                C.op('dve', lambda: nc.vector.scalar_tensor_tensor(
                                    w['yc'][:], w['yc'][:], col('rwkv_ln_b'), w['ys'][:], ALU.add, ALU.add),
                                    r=[w['yc'], P['rwkv_ln_b'], w['ys']], w=[w['yc']])
                                self.vtt(oo, oo[:, p, :], w['yc'], w['yc'][:], w['og'], w['og'][:], ALU.mult, eng='pool')
                            if kind == 'p' and step == ntile - 1:
                                dst = self.O['ns_rwkv']
                                zs = zstg[(si + p) % 2]
                                p_ = self.pn()
                                self.mm(p_, p_[:], Z, Z[:], cst, ident)
                                C.op('dve', lambda: nc.vector.tensor_copy(zs[:], p_[:]), r=[p_], w=[zs])
                                for hp in range(2):
                                    C.dma('sp', zs.b.name, dst[idx, l, d, p * 2 + hp],
                                          zs[hp * 64:(hp + 1) * 64, hp * 64:(hp + 1) * 64], r=[zs], w=[dst])
                        if not f:
                            C.dma('sp', oo.b.name, ob[1024:1536, tok:tok + 128].rearrange("(h p) t -> p h t", p=128), oo[:],
                                  r=[oo], w=[ob])
                            C.dma('sp', pbo.b.name, bb[:, tok:tok + 128].rearrange("(h p) t -> p h t", p=128), pbo[:],
                                  r=[pbo], w=[bb])
                        else:
                            C.dma('sp', oo.b.name, br[1024:1536, tok:tok + 128].rearrange("(h p) t -> p h t", p=128), oo[:],
                                  r=[oo], w=[br])
            C.barrier()

    def stage_merge(self, l):
        C, nc = self.C, self.nc
        pf, br = self.S['pf'], self.S['br']
        PF = self.PF
        wb, wo = self.I['mix_branch'], self.I['mix_out']
        with ExitStack() as st:
            xt = self.sb(st, 'xt', [128, KC, 512])
            brb = self.sb(st, 'brb', [128, 12, 512], BF16)
            gt_ = self.sb(st, 'mg', [128, KC, 512])
            mg = self.sb(st, 'mgd', [128, KC, 512])
            mT = self.sb(st, 'mT', [128, KC, 512], BF16)
            tm = [self.sb(st, 'mtm%d' % i, [128, 512]) for i in range(2)]
            nA = n = 0
            xs = self.S['xT']
            for (tok0, TT, ci) in self.token_tiles():
                C.dma('sp', 'xt', xt[:, :, 0:TT], xs[:, tok0:tok0 + TT].rearrange("(kc p) t -> p kc t", p=128),
                      r=[xs], w=[xt])
                C.dma('pool', 'brb', brb[:, :, 0:TT], br[:, tok0:tok0 + TT].rearrange("(c p) t -> p c t", p=128),
                      r=[br], w=[brb])
                for bi in range(3):
                    wt = self.wA[nA % 2]
                    nA += 1
                    wv = wt[:].rearrange("p a b -> p (a b)").rearrange("p (k n) -> p k n", k=4)
                    self.wload(wt, wv, wb, wb[l, bi].rearrange("(kc p) n -> p kc n", p=128))
                    r0 = PF['merge'] + bi * D
                    C.dma('sp', 'mg', gt_[:, :, 0:TT], pf[r0:r0 + D, tok0:tok0 + TT].rearrange("(c p) t -> p c t", p=128),
                          r=[pf], w=[gt_])
                    for oc in range(KC):
                        pp = self.ps[n % 4]
                        n += 1
                        for kc in range(4):
                            C.op('pe', lambda: nc.tensor.matmul(pp[:, 0:TT], wv[:, kc, oc * 128:(oc + 1) * 128],
                                                                brb[:, bi * 4 + kc, 0:TT], start=(kc == 0), stop=(kc == 3)),
                                 r=[wt, brb], w=[pp])
                        if bi == 0:
                            self.vtt(mg, mg[:, oc, 0:TT], pp, pp[:, 0:TT], gt_, gt_[:, oc, 0:TT], ALU.mult)
                        else:
                            t_ = tm[n % 2]
                            self.vtt(t_, t_[:, 0:TT], pp, pp[:, 0:TT], gt_, gt_[:, oc, 0:TT], ALU.mult)
                            self.vtt(mg, mg[:, oc, 0:TT], mg, mg[:, oc, 0:TT], t_, t_[:, 0:TT], ALU.add, eng='pool')
                C.op('act', lambda: nc.scalar.copy(mT[:, :, 0:TT], mg[:, :, 0:TT]), r=[mg], w=[mT])
                for ob_ in range(4):
                    wt = self.wA[nA % 2]
                    nA += 1
                    self.wload(wt, wt[:], wo, wo[l, :, ob_ * 512:(ob_ + 1) * 512].rearrange("(kc p) n -> p kc n", p=128))
                    for oc in range(4):
                        pp = self.ps[4 + oc]
                        for kc in range(KC):
                            C.op('pe', lambda: nc.tensor.matmul(pp[:, 0:TT], wt[:, kc, oc * 128:(oc + 1) * 128],
                                                                mT[:, kc, 0:TT], start=(kc == 0), stop=(kc == KC - 1)),
                                 r=[wt, mT], w=[pp])
                        ko = ob_ * 4 + oc
                        C.op('dve', lambda: nc.vector.scalar_tensor_tensor(
                            xt[:, ko, 0:TT], pp[:, 0:TT], self.gat[:, 1, ko, ci:ci + 1], xt[:, ko, 0:TT],
                            ALU.mult, ALU.add), r=[pp, self.gat, xt], w=[xt])
                self.store_x(tok0, TT, xt)
            C.barrier()

    def stage_mix(self, l):
        parts = self.mix_parts
        self.stage_mixproj(l)
        if 'hgrn' in parts:
            self.stage_hgrn(l)
        if 'gdn' in parts:
            self.stage_gdn_pre(l)
            self.stage_gdn(l)
        if 'rwkv' in parts:
            self.stage_rwkv(l)
        if 'merge' in parts:
            self.stage_merge(l)

    def stage_final(self):
        C, nc = self.C, self.nc
        fn = self.P['final_norm']
        with ExitStack() as st:
            xt = self.sb(st, 'xt', [128, KC, 512])
            sq = [self.sb(st, 'sq%d' % i, [128, 512]) for i in range(2)]
            rstd = self.sb(st, 'rstd', [128, 512])
            yo = [self.sb(st, 'yo%d' % i, [128, D]) for i in range(2)]
            n = 0
            for (tok0, TT, ci) in self.token_tiles():
                self.load_norm(tok0, TT, 0, ci, xt, None, sq, rstd, None)
                for kc in range(KC):
                    C.op('dve', lambda: nc.vector.scalar_tensor_tensor(
                        xt[:, kc, 0:TT], xt[:, kc, 0:TT], fn[:, kc:kc + 1], rstd[:, 0:TT], ALU.mult, ALU.mult),
                        r=[xt, fn, rstd], w=[xt])
                for sub in range(TT // 128):
                    y = yo[n % 2]
                    n += 1
                    for g in range(4):
                        ps = self.ps[g]
                        for q in range(4):
                            kc = g * 4 + q
                            C.op('pe', lambda: nc.tensor.matmul(ps[:, q * 128:(q + 1) * 128],
                                                                xt[:, kc, sub * 128:(sub + 1) * 128], self.cc('ident'),
                                                                start=True, stop=True), r=[xt, self.cst], w=[ps])
                        if g % 2 == 0:
                            C.op('dve', lambda: nc.vector.tensor_copy(y[:, g * 512:(g + 1) * 512], ps[:]), r=[ps], w=[y])
                        else:
                            C.op('act', lambda: nc.scalar.copy(y[:, g * 512:(g + 1) * 512], ps[:]), r=[ps], w=[y])
                    t0 = tok0 + sub * 128
                    if t0 < self.NTP:
                        dst_t, dst = self.O['y_prompt'], self.O['y_prompt'][t0:t0 + 128, :]
                    else:
                        dst_t, dst = self.O['y_sample'], self.O['y_sample'][t0 - self.NTP:t0 - self.NTP + 128, :]
                    C.dma('sp', y.b.name, dst, y[:], r=[y], w=[dst_t])
            C.barrier()

    def build(self):
        self.setup()
        self.wA = [self.sb(self.es, 'wA%d' % i, [128, KC, 512], BF16) for i in range(2)]
        if self.do_mix:
            self.setup_mix()
        self.stage_x0()
        for l in range(self.depth):
            self.mod_layer(l, None)
            self.stage_ffn(l, 0)
            if self.do_mix:
                self.stage_mix(l)
            self.stage_ffn(l, 2)
        self.stage_final()
        self.C.barrier()
        self.es.close()
        return self.nc


WEIGHT_KEYS = ['mod_w', 'mod_b', 'norm_g', 'ffn_up', 'ffn_down', 'mix_in', 'hgrn_lb', 'hgrn_norm', 'gdn_conv',
               'gdn_a_log', 'gdn_dt_bias', 'gdn_norm', 'rwkv_w0', 'rwkv_w2', 'rwkv_a0', 'rwkv_a2', 'rwkv_g2',
               'rwkv_kk', 'rwkv_ka', 'rwkv_ln_g', 'rwkv_ln_b', 'mix_branch', 'mix_out', 'final_norm']


def make_in_maps(inp, n_cores, NP, wdepth=DEPTH):
    f = lambda a: np.ascontiguousarray(np.asarray(a, dtype=np.float32))
    shared = {k: f(inp[k]) for k in WEIGHT_KEYS}
    shared['rwkv_rk'] = f(inp['rwkv_rk']).reshape(DEPTH, BW)
    shared['consts'] = CONST_ARR
    for k in ('mod_w', 'ffn_up', 'ffn_down', 'mix_in', 'mix_branch', 'mix_out'):
        shared[k] = shared[k][:wdepth]
    xp, xs = f(inp['x_prompt']), f(inp['x_sample'])
    maps = []
    for i in range(n_cores):
        m = dict(shared)
        m['x_prompt'] = np.ascontiguousarray(xp[i * NP:(i + 1) * NP].reshape(-1, D))
        m['x_sample'] = np.ascontiguousarray(xs[i])
        m['state_hgrn'] = f(inp['state_hgrn'][i])
        m['state_gdn'] = f(inp['state_gdn'][i])
        m['state_rwkv'] = f(inp['state_rwkv'][i])
        m['cond'] = np.ascontiguousarray(np.stack([f(inp['c_ctx']), f(inp['c'])[i]], axis=0))
        maps.append(m)
    return maps


def kernel(**inp):
    n = 8
    NP = inp['x_prompt'].shape[0] // n
    LP = inp['x_prompt'].shape[1]
    LS = inp['x_sample'].shape[1]
    b = Builder(NP=NP, LP=LP, LS=LS)
    nc = b.build()
    maps = make_in_maps(inp, n, NP)
    res = run_bass_kernel_spmd(nc, maps, core_ids=list(range(n)))
    R = res.results
    yp = np.concatenate([r['y_prompt'].reshape(NP, LP, D) for r in R], axis=0)
    ys = np.stack([r['y_sample'] for r in R], axis=0)
    hg = np.concatenate([r['ns_hgrn'] for r in R], axis=0)
    gd = np.concatenate([r['ns_gdn'] for r in R], axis=0)
    rw = np.concatenate([r['ns_rwkv'] for r in R], axis=0)
    return (yp.astype(np.float32), ys.astype(np.float32), hg.astype(np.float32), gd.astype(np.float32),
            rw.astype(np.float32))
```

```python
import numpy as np
from contextlib import ExitStack
import concourse.bass as bass
import concourse.mybir as mybir
from concourse.bass_utils import run_bass_kernel_spmd

F32 = mybir.dt.float32
BF16 = mybir.dt.bfloat16
AF = mybir.ActivationFunctionType
ALU = mybir.AluOpType
import threading

D = 2048
KC = D // 128
DFF = 5632
DEPTH = 4
NMOD = 9
BW = 512
EPS = 1e-6
IN_SPLITS = (('hg_q', 512), ('hg_i', 512), ('hg_f', 1024), ('hg_g', 512), ('gd_qkv', 1536), ('gd_z', 512),
             ('gd_a', 8), ('gd_b', 8), ('rw_rkv', 1536), ('rw_wd', 128), ('rw_ad', 128), ('rw_gd', 128),
             ('merge', 6144))
OFF = {}
_o = 0
for _n, _s in IN_SPLITS:
    OFF[_n] = _o
    _o += _s
INW = _o


class Buf:
    __slots__ = ('name', 'w', 'r')

    def __init__(self, name):
        self.name = name
        self.w = {}
        self.r = {}


class T:
    def __init__(self, h, name, buf=None, excl=False):
        self.h = h
        self.b = buf if buf is not None else Buf(name)
        self.excl = excl

    def __getitem__(self, k):
        return self.h[k]


class Ctx:
    def __init__(self, nc, es):
        self.nc = nc
        self.es = es
        self.eng = {'pe': nc.tensor, 'dve': nc.vector, 'act': nc.scalar, 'pool': nc.gpsimd, 'sp': nc.sync}
        self.sem = {k: es.enter_context(nc.semaphore('e_' + k)) for k in self.eng}
        self.cnt = {k: 0 for k in self.eng}
        self.seen = {k: {} for k in self.eng}
        self.dsem = {}
        self.dtot = {}
        self.gmap = {}
        self.nins = 0
        self.tls = threading.local()

    def _wait(self, e, need):
        for k, v in need.items():
            if k == 'pe' and e == 'pe':
                continue
            if k[0] == 'd' and k[1] == ':':
                v = self.dtot[k]
                sem = self.dsem[k]
            else:
                sem = self.sem[k]
            if self.seen[e].get(k, 0) >= v:
                continue
            self.eng[e].wait_ge(sem, v)
            self.seen[e][k] = v

    @staticmethod
    def _deps(r, w):
        need = {}
        for t in r:
            for k, v in t.b.w.items():
                if need.get(k, 0) < v:
                    need[k] = v
        for t in w:
            for k, v in t.b.w.items():
                if need.get(k, 0) < v:
                    need[k] = v
            for k, v in t.b.r.items():
                if need.get(k, 0) < v:
                    need[k] = v
        return need

    @staticmethod
    def _record(ev, r, w):
        k, v = ev
        for t in r:
            t.b.r[k] = v
        for t in w:
            t.b.w = {k: v}
            t.b.r = {}

    def op(self, e, fn, r=(), w=(), noyield=False):
        ex = [t for t in r if t.excl]
        if ex:
            w = list(w) + ex
        self._wait(e, self._deps(r, w))
        ins = fn()
        self.cnt[e] += 1
        ins.then_inc(self.sem[e], 1)
        self._record((e, self.cnt[e]), r, w)
        self.nins += 1
        hk = getattr(self.tls, 'hook', None)
        if hk is not None and not noyield:
            hk()
        return ins

    def run_interleaved(self, fns):
        n = len(fns)
        st = {'turn': 0, 'alive': [True] * n, 'err': None}
        cv = threading.Condition()

        def nxt(i):
            for k in range(1, n + 1):
                j = (i + k) % n
                if st['alive'][j]:
                    st['turn'] = j
                    return
            st['turn'] = -1

        def mk_hook(i):
            def hook():
                with cv:
                    nxt(i)
                    cv.notify_all()
                    while st['turn'] != i:
                        cv.wait()
            return hook

        def worker(i):
            with cv:
                while st['turn'] != i:
                    cv.wait()
            self.tls.hook = mk_hook(i)
            try:
                fns[i]()
            except BaseException as e:
                st['err'] = e
            finally:
                self.tls.hook = None
                with cv:
                    st['alive'][i] = False
                    nxt(i)
                    cv.notify_all()

        ths = [threading.Thread(target=worker, args=(i,)) for i in range(n)]
        for t in ths:
            t.start()
        for t in ths:
            t.join()
        if st['err'] is not None:
            raise st['err']

    MAXDSEM = 40

    def dma(self, q, g, out, in_, r=(), w=(), **kw):
        if g not in self.gmap:
            if len(self.dsem) < self.MAXDSEM:
                key = 'd:%d' % len(self.dsem)
                self.dsem[key] = self.es.enter_context(self.nc.semaphore('d_%d' % len(self.dsem)))
                self.dtot[key] = 0
            else:
                key = 'd:%d' % (8 + (len(self.gmap) % (self.MAXDSEM - 8)))
            self.gmap[g] = key
        key = self.gmap[g]
        self._wait(q, self._deps(r, w))
        ins = self.eng[q].dma_start(out=out, in_=in_, **kw)
        self.dtot[key] += 16
        ins.then_inc(self.dsem[key], 16)
        self._record((key, self.dtot[key]), r, w)
        self.nins += 1
        hk = getattr(self.tls, 'hook', None)
        if hk is not None:
            hk()
        return ins

    def barrier(self):
        need = {k: v for k, v in self.cnt.items() if v > 0}
        need.update({k: v for k, v in self.dtot.items() if v > 0})
        for e in self.eng:
            self._wait(e, dict(need))


def host_consts():
    c = {}
    c['ident'] = np.eye(128, dtype=np.float32)
    c['ones'] = np.ones((128, 128), np.float32)
    i = np.arange(128)
    s, t = i[:, None], i[None, :]
    c['bones64'] = ((s // 64) == (t // 64)).astype(np.float32)
    same32 = (s // 32) == (t // 32)
    c['hg_tri_f'] = (same32 & (s <= t)).astype(np.float32)
    c['hg_tri_b'] = (same32 & (s >= t)).astype(np.float32)
    c['hg_trx_f'] = (same32 & (s > t)).astype(np.float32)
    c['hg_trx_b'] = (same32 & (s < t)).astype(np.float32)
    c['hg_cm'] = np.repeat(((i[:, None] // 32) == np.arange(4)[None, :]).astype(np.float32), 32, axis=1)
    c['gd_tri_f'] = (s <= t).astype(np.float32)
    c['gd_tri_b'] = (s >= t).astype(np.float32)
    c['gd_neg_f'] = np.where(s <= t, 0.0, -30000.0).astype(np.float32)
    c['gd_neg_b'] = np.where(s >= t, 0.0, -30000.0).astype(np.float32)
    c['gd_st_f'] = (s < t).astype(np.float32)
    c['gd_st_b'] = (s > t).astype(np.float32)
    same64 = (s // 64) == (t // 64)
    c['rw_tri_f'] = (same64 & (s <= t)).astype(np.float32)
    c['rw_tri_b'] = (same64 & (s >= t)).astype(np.float32)
    c['rw_trx_f'] = (same64 & (s < t)).astype(np.float32)
    c['rw_trx_b'] = (same64 & (s > t)).astype(np.float32)
    ss, tt = s % 64, t % 64
    c['rw_bst_f'] = (same64 & (ss < tt)).astype(np.float32)
    c['rw_bst_b'] = (same64 & (ss > tt)).astype(np.float32)
    t64 = np.arange(64)[None, :]
    c['rw_sin_f'] = ((s % 64) <= t64).astype(np.float32)
    c['rw_sin_b'] = ((s % 64) >= t64).astype(np.float32)
    blk = np.zeros((128, 2, 2, 64), np.float32)
    blk[:64, :, 0, :] = 1.0
    blk[64:, :, 1, :] = 1.0
    c['rw_blk'] = blk.reshape(128, 256)
    names, cols, off = [], {}, 0
    arrs = []
    for k, v in c.items():
        cols[k] = (off, v.shape[1])
        off += v.shape[1]
        arrs.append(v)
    return np.concatenate(arrs, axis=1), cols


CONST_ARR, CONST_COLS = host_consts()


class Builder:
    def __init__(self, NP=4, LP=256, LS=2048, depth=DEPTH, do_mix=True, debug_outs=(),
                 mix_parts=('hgrn', 'gdn', 'rwkv', 'merge'), wdepth=DEPTH):
        self.wdepth = wdepth
        self.NP, self.LP, self.LS, self.depth, self.do_mix = NP, LP, LS, depth, do_mix
        self.NTP = NP * LP
        self.NT = self.NTP + LS
        self.debug_outs = debug_outs
        self.mix_parts = mix_parts
        self.nc = bass.Bass("TRN2", target_bir_lowering=False)
        self.es = ExitStack()
        self.C = Ctx(self.nc, self.es)
        self.declare()

    def sb(self, st, name, shape, dt=F32):
        self.uid = getattr(self, 'uid', 0) + 1
        return T(st.enter_context(self.nc.sbuf_tensor('%s_u%d' % (name, self.uid), list(shape), dt)), name)

    def din(self, name, shape, dt=F32):
        return T(self.nc.dram_tensor(name, list(shape), dt, kind="ExternalInput").ap(), name)

    def dout(self, name, shape, dt=F32):
        return T(self.nc.dram_tensor(name, list(shape), dt, kind="ExternalOutput").ap(), name)

    def dscr(self, name, shape, dt=F32):
        kind = "ExternalOutput" if name in self.debug_outs else None
        if kind:
            return T(self.nc.dram_tensor(name, list(shape), dt, kind=kind).ap(), name)
        return T(self.nc.dram_tensor(name, list(shape), dt).ap(), name)

    def declare(self):
        NP, LP, LS, NT, dp = self.NP, self.LP, self.LS, self.NT, self.depth
        I = {}
        I['x_prompt'] = self.din('x_prompt', [NP * LP, D])
        I['x_sample'] = self.din('x_sample', [LS, D])
        I['state_hgrn'] = self.din('state_hgrn', [DEPTH, 2, 4, 128, 128])
        I['state_gdn'] = self.din('state_gdn', [DEPTH, 2, 4, 128, 128])
        I['state_rwkv'] = self.din('state_rwkv', [DEPTH, 2, 8, 64, 64])
        I['cond'] = self.din('cond', [2, D])
        I['mod_w'] = self.din('mod_w', [self.wdepth, D, NMOD * D])
        I['mod_b'] = self.din('mod_b', [DEPTH, NMOD * D])
        I['norm_g'] = self.din('norm_g', [DEPTH, 3, D])
        I['ffn_up'] = self.din('ffn_up', [self.wdepth, 2, D, 2 * DFF])
        I['ffn_down'] = self.din('ffn_down', [self.wdepth, 2, DFF, D])
        I['mix_in'] = self.din('mix_in', [self.wdepth, D, INW])
        I['hgrn_lb'] = self.din('hgrn_lb', [DEPTH, 2, BW])
        I['hgrn_norm'] = self.din('hgrn_norm', [DEPTH, BW])
        I['gdn_conv'] = self.din('gdn_conv', [DEPTH, 3, 3, 1536])
        I['gdn_a_log'] = self.din('gdn_a_log', [DEPTH, 2, 4])
        I['gdn_dt_bias'] = self.din('gdn_dt_bias', [DEPTH, 2, 4])
        I['gdn_norm'] = self.din('gdn_norm', [DEPTH, 128])
        I['rwkv_w0'] = self.din('rwkv_w0', [DEPTH, 2, BW])
        I['rwkv_w2'] = self.din('rwkv_w2', [DEPTH, 2, 64, BW])
        I['rwkv_a0'] = self.din('rwkv_a0', [DEPTH, 2, BW])
        I['rwkv_a2'] = self.din('rwkv_a2', [DEPTH, 2, 64, BW])
        I['rwkv_g2'] = self.din('rwkv_g2', [DEPTH, 128, BW])
        I['rwkv_kk'] = self.din('rwkv_kk', [DEPTH, BW])
        I['rwkv_ka'] = self.din('rwkv_ka', [DEPTH, BW])
        I['rwkv_rk'] = self.din('rwkv_rk', [DEPTH, BW])
        I['rwkv_ln_g'] = self.din('rwkv_ln_g', [DEPTH, BW])
        I['rwkv_ln_b'] = self.din('rwkv_ln_b', [DEPTH, BW])
        I['mix_branch'] = self.din('mix_branch', [self.wdepth, 3, BW, D])
        I['mix_out'] = self.din('mix_out', [self.wdepth, D, D])
        I['final_norm'] = self.din('final_norm', [D])
        I['consts'] = self.din('consts', list(CONST_ARR.shape))
        self.I = I
        O = {}
        O['y_prompt'] = self.dout('y_prompt', [NP * LP, D])
        O['y_sample'] = self.dout('y_sample', [LS, D])
        O['ns_hgrn'] = self.dout('ns_hgrn', [NP, DEPTH, 2, 4, 128, 128])
        O['ns_gdn'] = self.dout('ns_gdn', [NP, DEPTH, 2, 4, 128, 128])
        O['ns_rwkv'] = self.dout('ns_rwkv', [NP, DEPTH, 2, 8, 64, 64])
        self.O = O
        S = {}
        S['xT'] = self.dscr('xT', [D, NT])
        self.S = S

    def setup(self):
        C, nc, es = self.C, self.nc, self.es
        self.cst = self.sb(es, 'cst', CONST_ARR.shape)
        C.dma('sp', 'cst', self.cst[:], self.I['consts'][:, :], r=[self.I['consts']], w=[self.cst])
        self.ps = [T(es.enter_context(nc.psum_tensor('ps%d' % i, [128, 512], F32)), 'ps%d' % i, excl=True)
                   for i in range(8)]
        self.onesD = self.sb(es, 'onesD', [128, 128])
        C.op('dve', lambda: nc.vector.tensor_scalar(self.onesD[:], self.cc('ones'), 1.0 / D, None, ALU.mult),
             r=[self.cst], w=[self.onesD])
        self.ones128 = self.sb(es, 'ones128', [128, 128])
        C.op('dve', lambda: nc.vector.tensor_scalar(self.ones128[:], self.cc('ones'), 1.0 / 128, None, ALU.mult),
             r=[self.cst], w=[self.ones128])
        self.epsc = self.sb(es, 'epsc', [128, 1])
        C.op('dve', lambda: nc.vector.memset(self.epsc[:], EPS), w=[self.epsc])
        self.stg = self.sb(es, 'fmstg', [128, 128])
        self.P = {}
        dp = DEPTH
        self.P['norm_g'] = self.load_fm('norm_g', self.I['norm_g'], "l j (c p) -> (l j c) p", dp * 3 * KC)
        self.P['final_norm'] = self.load_fm('final_norm', self.I['final_norm'], "(c p) -> c p", KC)
        self.P['mod_b'] = self.load_fm('mod_b', self.I['mod_b'], "l (c p) -> (l c) p", dp * 144)
        cf = self.load_fm('condf', self.I['cond'], "r (c p) -> (r c) p", 2 * KC)
        self.cond = self.sb(es, 'condb', [128, KC, 2], BF16)
        C.op('act', lambda: nc.scalar.activation(self.cond[:].rearrange("p c r -> p r c"),
                                                 cf[:].rearrange("p (r c) -> p r c", r=2), AF.Silu),
             r=[cf], w=[self.cond])
        self.modFM = self.sb(es, 'modFM', [128, 144, 2])
        self.scl = self.sb(es, 'scl', [128, 3, KC, 2])
        self.gat = self.sb(es, 'gat', [128, 3, KC, 2])

    def rsqrt(self, out_t, out_ap, in_t, in_ap, eps_t, scale=1.0):
        C, nc = self.C, self.nc
        npart = out_ap.shape[0]
        C.op('act', lambda: nc.scalar.activation(out_ap, in_ap, AF.Ln, bias=eps_t[0:npart, :], scale=scale),
             r=[in_t, eps_t], w=[out_t])
        C.op('act', lambda: nc.scalar.activation(out_ap, out_ap, AF.Exp, scale=-0.5), r=[out_t], w=[out_t])

    def cc(self, name, c0=0, n=None):
        o, w = CONST_COLS[name]
        if n is None:
            n = w
        return self.cst[:, o + c0:o + c0 + n]

    def load_fm(self, name, src, pattern, nrows):
        C, nc = self.C, self.nc
        dst = self.sb(self.es, 'P_' + name, [128, nrows])
        view = src[:].rearrange(pattern, p=128)
        for r0 in range(0, nrows, 128):
            n = min(128, nrows - r0)
            C.dma('sp', 'fmstg', self.stg[0:n, :], view[r0:r0 + n, :], r=[src], w=[self.stg])
            ps = self.ps[0]
            C.op('pe', lambda: nc.tensor.matmul(ps[:, 0:n], self.stg[0:n, :], self.cc('ident')[0:n, 0:n],
                                                start=True, stop=True), r=[self.stg, self.cst], w=[ps])
            C.op('dve', lambda: nc.vector.tensor_copy(dst[:, r0:r0 + n], ps[:, 0:n]), r=[ps], w=[dst])
        return dst

    def wload(self, wt, dst_ap, src_t, src_ap):
        self.C.dma('pool', wt.b.name, dst_ap, src_ap, r=[src_t], w=[wt])

    def mod_layer(self, l, st):
        C, nc = self.C, self.nc
        ps = self.ps[1]
        mw = self.I['mod_w']
        for nb in range(36):
            wt = self.wA[nb % 2]
            self.wload(wt, wt[:], mw, mw[l, :, nb * 512:(nb + 1) * 512].rearrange("(kc p) n -> p kc n", p=128))
            for c4 in range(4):
                ch = nb * 4 + c4
                for kc in range(KC):
                    C.op('pe', lambda: nc.tensor.matmul(ps[:, ch * 2:ch * 2 + 2], wt[:, kc, c4 * 128:(c4 + 1) * 128],
                                                        self.cond[:, kc, :], start=(kc == 0), stop=(kc == KC - 1)),
                         r=[wt, self.cond], w=[ps])
        mb = self.P['mod_b']
        C.op('dve', lambda: nc.vector.tensor_tensor(
            self.modFM[:], ps[:, 0:288].rearrange("p (c r) -> p c r", r=2),
            mb[:, l * 144:(l + 1) * 144].unsqueeze(2).to_broadcast([128, 144, 2]), ALU.add),
            r=[ps, mb], w=[self.modFM])
        ng = self.P['norm_g']
        for j in range(3):
            sc = self.modFM[:, (3 * j + 1) * KC:(3 * j + 2) * KC, :]
            gt = self.modFM[:, (3 * j + 2) * KC:(3 * j + 3) * KC, :]
            ngj = ng[:, (l * 3 + j) * KC:(l * 3 + j + 1) * KC].unsqueeze(2).to_broadcast([128, KC, 2])
            C.op('dve', lambda: nc.vector.scalar_tensor_tensor(self.scl[:, j], sc, 1.0, ngj, ALU.add, ALU.mult),
                 r=[self.modFM, ng], w=[self.scl])
            C.op('dve', lambda: nc.vector.tensor_scalar(self.gat[:, j], gt, 0.5 if j != 1 else 1.0, None, ALU.mult),
                 r=[self.modFM], w=[self.gat])

    def shift(self, j, kc, ci):
        return self.modFM[:, 3 * j * KC + kc, ci:ci + 1]

    def load_norm(self, tok0, TT, j, ci, xt, hT, sq, rstd, tmp, hoff=0):
        C, nc = self.C, self.nc
        xs = self.S['xT']
        C.dma('sp', 'xt', xt[:, :, 0:TT], xs[:, tok0:tok0 + TT].rearrange("(kc p) t -> p kc t", p=128),
              r=[xs], w=[xt])
        ps = self.ps[7]
        for kc in range(KC):
            s = sq[kc % 2]
            C.op('act', lambda: nc.scalar.activation(s[:, 0:TT], xt[:, kc, 0:TT], AF.Square), r=[xt], w=[s])
            C.op('pe', lambda: nc.tensor.matmul(ps[:, 0:TT], self.onesD[:], s[:, 0:TT], start=(kc == 0),
                                                stop=(kc == KC - 1)), r=[self.onesD, s], w=[ps])
        self.rsqrt(rstd, rstd[:, 0:TT], ps, ps[:, 0:TT], self.epsc)
        if hT is None:
            return
        for kc in range(KC):
            tm = tmp[kc % 2]
            C.op('dve', lambda: nc.vector.scalar_tensor_tensor(tm[:, 0:TT], xt[:, kc, 0:TT], self.scl[:, j, kc, ci:ci + 1],
                                                               rstd[:, 0:TT], ALU.mult, ALU.mult),
                 r=[xt, self.scl, rstd], w=[tm])
            C.op('act', lambda: nc.scalar.activation(hT[:, kc, hoff:hoff + TT], tm[:, 0:TT], AF.Identity,
                                                     bias=self.shift(j, kc, ci), scale=1.0),
                 r=[tm, self.modFM], w=[hT])

    def store_x(self, tok0, TT, xt):
        xs = self.S['xT']
        self.C.dma('sp', 'xt', xs[:, tok0:tok0 + TT].rearrange("(kc p) t -> p kc t", p=128), xt[:, :, 0:TT],
                   r=[xt], w=[xs])

    def token_tiles(self):
        out = []
        for t0 in range(0, self.NTP, 512):
            out.append((t0, min(512, self.NTP - t0), 0))
        for t0 in range(self.NTP, self.NT, 512):
            out.append((t0, min(512, self.NT - t0), 1))
        return out

    def stage_x0(self):
        C, nc = self.C, self.nc
        with ExitStack() as st:
            xr = [self.sb(st, 'x0r%d' % i, [128, D]) for i in range(2)]
            xo = [self.sb(st, 'x0o%d' % i, [128, KC, 128]) for i in range(2)]
            xs = self.S['xT']
            for ti in range(self.NT // 128):
                tok0 = ti * 128
                if tok0 < self.NTP:
                    src_t, src = self.I['x_prompt'], self.I['x_prompt'][tok0:tok0 + 128, :]
                else:
                    src_t, src = self.I['x_sample'], self.I['x_sample'][tok0 - self.NTP:tok0 - self.NTP + 128, :]
                a, o = xr[ti % 2], xo[ti % 2]
                C.dma('sp', a.b.name, a[:], src, r=[src_t], w=[a])
                for g in range(4):
                    ps = self.ps[(ti % 2) * 4 + g]
                    for q in range(4):
                        kc = g * 4 + q
                        C.op('pe', lambda: nc.tensor.matmul(ps[:, q * 128:(q + 1) * 128], a[:, kc * 128:(kc + 1) * 128],
                                                            self.cc('ident'), start=True, stop=True),
                             r=[a, self.cst], w=[ps])
                    eng = 'dve' if g % 2 == 0 else 'act'
                    if eng == 'dve':
                        C.op('dve', lambda: nc.vector.tensor_copy(o[:, g * 4:(g + 1) * 4, :],
                                                                  ps[:].rearrange("p (q t) -> p q t", q=4)),
                             r=[ps], w=[o])
                    else:
                        C.op('act', lambda: nc.scalar.copy(o[:, g * 4:(g + 1) * 4, :],
                                                           ps[:].rearrange("p (q t) -> p q t", q=4)),
                             r=[ps], w=[o])
                C.dma('sp', o.b.name, xs[:, tok0:tok0 + 128].rearrange("(kc p) t -> p kc t", p=128), o[:],
                      r=[o], w=[xs])
            C.barrier()

    def stage_ffn(self, l, j):
        C, nc = self.C, self.nc
        fi = 0 if j == 0 else 1
        up, dn = self.I['ffn_up'], self.I['ffn_down']
        NT = self.NT
        if 'actT' not in self.S:
            self.S['actT'] = self.dscr('actT', [DFF, NT], BF16)
        actT = self.S['actT']
        xs = self.S['xT']
        tiles = self.token_tiles()
        with ExitStack() as st:
            hA = self.sb(st, 'hA', [128, KC, NT], BF16)
            with ExitStack() as st1:
                xt = self.sb(st1, 'xt', [128, KC, 512])
                sq = [self.sb(st1, 'sq%d' % i, [128, 512]) for i in range(2)]
                tmp = [self.sb(st1, 'tmp%d' % i, [128, 512]) for i in range(2)]
                rstd = self.sb(st1, 'rstd', [128, 512])
                for (tok0, TT, ci) in tiles:
                    self.load_norm(tok0, TT, j, ci, xt, hA, sq, rstd, tmp, hoff=tok0)
                C.barrier()
            with ExitStack() as st2:
                sa = [self.sb(st2, 'sa%d' % i, [128, 512]) for i in range(2)]
                ast = [self.sb(st2, 'ast%d' % i, [128, 2, 512], BF16) for i in range(3)]
                n = na = 0
                for fb in range(22):
                    wt = self.wA[fb % 2]
                    self.wload(wt, wt[:, :, 0:256], up,
                               up[l, fi, :, fb * 256:(fb + 1) * 256].rearrange("(kc p) n -> p kc n", p=128))
                    self.wload(wt, wt[:, :, 256:512], up,
                               up[l, fi, :, DFF + fb * 256:DFF + (fb + 1) * 256].rearrange("(kc p) n -> p kc n", p=128))
                    for (tok0, TT, ci) in tiles:
                        a_ = ast[na % 3]
                        na += 1
                        for c2 in range(2):
                            k = n % 2
                            n += 1
                            pa, pb = self.ps[k * 2], self.ps[k * 2 + 1]
                            for half, pp in ((0, pa), (1, pb)):
                                for kc in range(KC):
                                    C.op('pe', lambda: nc.tensor.matmul(
                                        pp[:, 0:TT], wt[:, kc, half * 256 + c2 * 128:half * 256 + (c2 + 1) * 128],
                                        hA[:, kc, tok0:tok0 + TT], start=(kc == 0), stop=(kc == KC - 1)),
                                        r=[wt, hA], w=[pp])
                            s_ = sa[k]
                            C.op('act', lambda: nc.scalar.activation(s_[:, 0:TT], pa[:, 0:TT], AF.Silu), r=[pa], w=[s_])
                            C.op('dve', lambda: nc.vector.tensor_tensor(a_[:, c2, 0:TT], s_[:, 0:TT], pb[:, 0:TT], ALU.mult),
                                 r=[s_, pb], w=[a_])
                        C.dma('sp', a_.b.name, actT[fb * 256:(fb + 1) * 256, tok0:tok0 + TT]
                              .rearrange("(c p) t -> p c t", p=128), a_[:, :, 0:TT], r=[a_], w=[actT])
                C.barrier()
        with ExitStack() as st:
            wD = self.sb(st, 'wD', [128, 44, 512], BF16)
            at = [self.sb(st, 'at%d' % i, [128, 44, 512], BF16) for i in range(2)]
            xc = [self.sb(st, 'xc%d' % i, [128, 512]) for i in range(3)]
            n = nx = 0
            for ob in range(4):
                for qk in range(4):
                    self.wload(wD, wD[:, qk * 11:(qk + 1) * 11, :], dn,
                               dn[l, fi, qk * 1408:(qk + 1) * 1408, ob * 512:(ob + 1) * 512]
                               .rearrange("(kc p) n -> p kc n", p=128))
                for (tok0, TT, ci) in tiles:
                    a_ = at[n % 2]
                    n += 1
                    for hh in range(2):
                        C.dma('sp', a_.b.name, a_[:, hh * 22:(hh + 1) * 22, 0:TT],
                              actT[hh * 2816:(hh + 1) * 2816, tok0:tok0 + TT].rearrange("(c p) t -> p c t", p=128),
                              r=[actT], w=[a_])
                    for oc in range(4):
                        pp = self.ps[4 + oc]
                        kc = ob * 4 + oc
                        x_ = xc[nx % 3]
                        nx += 1
                        C.dma('sp', x_.b.name, x_[:, 0:TT], xs[kc * 128:(kc + 1) * 128, tok0:tok0 + TT], r=[xs], w=[x_])
                        for kk in range(44):
                            C.op('pe', lambda: nc.tensor.matmul(pp[:, 0:TT], wD[:, kk, oc * 128:(oc + 1) * 128],
                                                                a_[:, kk, 0:TT], start=(kk == 0), stop=(kk == 43)),
                                 r=[wD, a_], w=[pp])
                        C.op('dve', lambda: nc.vector.scalar_tensor_tensor(
                            x_[:, 0:TT], pp[:, 0:TT], self.gat[:, j, kc, ci:ci + 1], x_[:, 0:TT], ALU.mult, ALU.add),
                            r=[pp, self.gat, x_], w=[x_])
                        C.dma('sp', x_.b.name, xs[kc * 128:(kc + 1) * 128, tok0:tok0 + TT], x_[:, 0:TT], r=[x_], w=[xs])
            C.barrier()

    PF = {'hg_q': 0, 'hg_k': 512, 'hg_g': 1536, 'gd_qkv': 2048, 'gd_z': 3584, 'rw_rkv': 4096, 'rw_wd': 5632,
          'rw_ad': 5760, 'rw_gd': 5888, 'merge': 6016}
    PF_ROWS = 12160
    PT = {'v': 0, 'logf': 512, 'k': 1536, 'lg': 2560, 'beta': 2568}
    PT_COLS = 2576

    def seqs(self):
        out = [(i * self.LP, self.LP, 'p', i) for i in range(self.NP)]
        out.append((self.NTP, self.LS, 's', 0))
        return out

    def softmax_cum(self, x, e, L, R):
        C, nc = self.C, self.nc
        C.op('act', lambda: nc.scalar.activation(e[:], x[:], AF.Exp), r=[x], w=[e])
        C.op('dve', lambda: nc.vector.tensor_tensor(x[:, 0], e[:, 0], e[:, 1], ALU.add), r=[e], w=[x])
        for l in range(2, L):
            C.op('dve', lambda: nc.vector.tensor_tensor(x[:, 0], x[:, 0], e[:, l], ALU.add), r=[e, x], w=[x])
        C.op('dve', lambda: nc.vector.reciprocal(x[:, 0], x[:, 0]), r=[x], w=[x])
        for l in range(1, L):
            C.op('dve', lambda: nc.vector.tensor_tensor(e[:, l], e[:, l], x[:, 0], ALU.mult), r=[e, x], w=[e])
        C.op('dve', lambda: nc.vector.memset(x[:, 0], 0.0), w=[x])
        for l in range(1, L):
            C.op('dve', lambda: nc.vector.tensor_tensor(x[:, l], x[:, l - 1], e[:, l], ALU.add), r=[e, x], w=[x])

    def bcast_load(self, name, src_t, flat_ap, n):
        t = self.sb(self.es, name, [128, n])
        self.C.dma('sp', name, t[:], flat_ap.partition_broadcast(128), r=[src_t], w=[t])
        return t

    def setup_mix(self):
        C, nc, es, I = self.C, self.nc, self.es, self.I
        dp = DEPTH
        lbf = self.load_fm('lbf', I['hgrn_lb'], "l d (c p) -> (l d c) p", dp * 8)
        ef = self.sb(es, 'lbf_e', [128, dp * 8])
        self.softmax_cum(T(lbf[:].rearrange("p (l r) -> p l r", l=dp), 'lbfv', buf=lbf.b),
                         T(ef[:].rearrange("p (l r) -> p l r", l=dp), 'lbfe', buf=ef.b), dp, 8)
        self.lbf = lbf
        self.omlbf = self.sb(es, 'omlbf', [128, dp * 8])
        C.op('dve', lambda: nc.vector.tensor_scalar(self.omlbf[:], lbf[:], -1.0, 1.0, ALU.mult, ALU.add),
             r=[lbf], w=[self.omlbf])
        lbr = self.bcast_load('lbr', I['hgrn_lb'], I['hgrn_lb'][:].rearrange("l d c -> (l d c)"), dp * 1024)
        with ExitStack() as st_:
            er = self.sb(st_, 'lbr_e', [128, dp * 1024])
            self.softmax_cum(T(lbr[:].rearrange("p (l r) -> p l r", l=dp), 'lbrv', buf=lbr.b),
                             T(er[:].rearrange("p (l r) -> p l r", l=dp), 'lbre', buf=er.b), dp, 1024)
            C.barrier()
        self.lbr = lbr
        self.negA = self.bcast_load('negA', I['gdn_a_log'], I['gdn_a_log'][:].rearrange("l d h -> (l d h)"), dp * 8)
        C.op('act', lambda: nc.scalar.activation(self.negA[:], self.negA[:], AF.Exp), r=[self.negA], w=[self.negA])
        C.op('dve', lambda: nc.vector.tensor_scalar(self.negA[:], self.negA[:], -1.0, None, ALU.mult),
             r=[self.negA], w=[self.negA])
        self.dtb = self.bcast_load('dtb', I['gdn_dt_bias'], I['gdn_dt_bias'][:].rearrange("l d h -> (l d h)"), dp * 8)
        self.onec = self.sb(es, 'onec', [128, 1])
        C.op('dve', lambda: nc.vector.memset(self.onec[:], 1.0), w=[self.onec])
        self.eps128 = self.sb(es, 'eps128', [128, 1])
        C.op('dve', lambda: nc.vector.memset(self.eps128[:], 128.0 * EPS), w=[self.eps128])
        self.epsgn = self.sb(es, 'epsgn', [128, 1])
        C.op('dve', lambda: nc.vector.memset(self.epsgn[:], 64e-5), w=[self.epsgn])
        P = self.P
        P['hgrn_norm'] = self.load_fm('hgrn_norm', I['hgrn_norm'], "l (c p) -> (l c) p", dp * 4)
        P['gdn_conv'] = self.load_fm('gdn_conv', I['gdn_conv'], "l a b (c p) -> (l a b c) p", dp * 108)
        P['gdn_norm'] = self.load_fm('gdn_norm', I['gdn_norm'], "l p -> l p", dp)
        P['rwkv_a0'] = self.load_fm('rwkv_a0', I['rwkv_a0'], "l d (c p) -> (l d c) p", dp * 8)
        for k in ('rwkv_kk', 'rwkv_ka', 'rwkv_rk', 'rwkv_ln_g', 'rwkv_ln_b'):
            P[k] = self.load_fm(k, I[k], "l (c p) -> (l c) p", dp * 4)
        self.qt = [[T(self.ps[i][:, j * 128:(j + 1) * 128], 'pq%d_%d' % (i, j), buf=self.ps[i].b, excl=True)
                    for j in range(4)] for i in range(8)]
        self.set_pools('gdn')
        NT = self.NT
        self.S['pf'] = self.dscr('pf', [self.PF_ROWS, NT])
        self.S['pt'] = self.dscr('pt', [NT, self.PT_COLS])
        self.S['ob'] = self.dscr('ob', [1536, NT])
        self.S['br'] = self.dscr('br', [1536, NT])

    def set_pools(self, kind):
        qt = self.qt
        if kind == 'hgrn':
            self.pgrp = {g: [qt[i][g] for i in range(4)] for g in range(4)}
            self.pql = [qt[4 + h][0] for h in range(4)]
        elif kind == 'rwkv':
            self.pgrp = {4: [qt[i][j] for j in (0, 2) for i in range(7)], 5: [qt[i][j] for j in (1, 3) for i in range(7)]}
            self.pql = [qt[7][j] for j in range(4)]
        else:
            self.pgrp = {g: [qt[i][g] for i in range(8)] for g in range(4)}
            self.pql = []
        self.pgrp[None] = [t for g in sorted(k for k in self.pgrp) for t in self.pgrp[g]]
        self.pgi = {g: 0 for g in self.pgrp}
        self.pqli = 0

    def pnl(self):
        self.pqli = (self.pqli + 1) % len(self.pql)
        return self.pql[self.pqli]

    def pn(self):
        g = getattr(self.C.tls, 'grp', None)
        self.pgi[g] = (self.pgi[g] + 1) % len(self.pgrp[g])
        return self.pgrp[g][self.pgi[g]]

    def stage_mixproj(self, l):
        C, nc = self.C, self.nc
        mi = self.I['mix_in']
        pf, pt = self.S['pf'], self.S['pt']
        PF, PT = self.PF, self.PT
        fm_blocks = [(0, 512, PF['hg_q'], ['silu'] * 4), (1024, 512, PF['hg_k'], ['hgk0'] * 4),
                     (1536, 512, PF['hg_k'] + 512, ['hgk1'] * 4), (2048, 512, PF['hg_g'], ['silu'] * 4)]
        fm_blocks += [(2560 + 512 * i, 512, PF['gd_qkv'] + 512 * i, ['copy'] * 4) for i in range(3)]
        fm_blocks += [(4096, 512, PF['gd_z'], ['silu'] * 4)]
        fm_blocks += [(4624 + 512 * i, 512, PF['rw_rkv'] + 512 * i, ['copy'] * 4) for i in range(3)]
        fm_blocks += [(6160, 384, PF['rw_wd'], ['tanh', 'copy', 'sigmoid'])]
        fm_blocks += [(6544 + 512 * i, 512, PF['merge'] + 512 * i, ['sigmoid'] * 4) for i in range(12)]
        tm_blocks = [(512, 512, 'v'), (1024, 512, 'f0'), (1536, 512, 'f1'), (4608, 16, 'ab')]
        funcs = {'silu': AF.Silu, 'copy': AF.Copy, 'tanh': AF.Tanh, 'sigmoid': AF.Sigmoid, 'hgk0': AF.Sigmoid,
                 'hgk1': AF.Sigmoid}
        with ExitStack() as st:
            xt = self.sb(st, 'xt', [128, KC, 512])
            hT = self.sb(st, 'hT', [128, KC, 512], BF16)
            sq = [self.sb(st, 'sq%d' % i, [128, 512]) for i in range(2)]
            tmp = [self.sb(st, 'tmp%d' % i, [128, 512]) for i in range(2)]
            rstd = self.sb(st, 'rstd', [128, 512])
            stg = [self.sb(st, 'stg%d' % i, [128, 512]) for i in range(3)]
            tst = [self.sb(st, 'tst%d' % i, [128, 1024]) for i in range(2)]
            omlbr = self.sb(st, 'omlbr', [128, 1024])
            C.op('dve', lambda: nc.vector.tensor_scalar(omlbr[:], self.lbr[:, l * 1024:(l + 1) * 1024], -1.0, 1.0,
                                                        ALU.mult, ALU.add), r=[self.lbr], w=[omlbr])
            nA = ns = nt = 0
            for (tok0, TT, ci) in self.token_tiles():
                self.load_norm(tok0, TT, 1, ci, xt, hT, sq, rstd, tmp)
                for (c0, ncol, prow, kinds) in fm_blocks:
                    wt = self.wA[nA % 2]
                    nA += 1
                    self.wload(wt, wt[:, :, 0:ncol], mi, mi[l, :, c0:c0 + ncol].rearrange("(kc p) n -> p kc n", p=128))
                    for ch, kind in enumerate(kinds):
                        pp = self.ps[ns % 4]
                        sg = stg[ns % 3]
                        ns += 1
                        for kc in range(KC):
                            C.op('pe', lambda: nc.tensor.matmul(pp[:, 0:TT], wt[:, kc, ch * 128:(ch + 1) * 128],
                                                                hT[:, kc, 0:TT], start=(kc == 0), stop=(kc == KC - 1)),
                                 r=[wt, hT], w=[pp])
                        C.op('act', lambda: nc.scalar.activation(sg[:, 0:TT], pp[:, 0:TT], funcs[kind]), r=[pp], w=[sg])
                        if kind in ('hgk0', 'hgk1'):
                            d = int(kind[-1])
                            col = l * 8 + d * 4 + ch
                            C.op('dve', lambda: nc.vector.tensor_scalar(sg[:, 0:TT], sg[:, 0:TT], -1.0, 1.0,
                                                                        ALU.mult, ALU.add), r=[sg], w=[sg])
                            C.op('dve', lambda: nc.vector.tensor_scalar(sg[:, 0:TT], sg[:, 0:TT],
                                                                        self.omlbf[:, col:col + 1], None, ALU.mult),
                                 r=[sg, self.omlbf], w=[sg])
                        r0 = prow + ch * 128
                        C.dma('sp', sg.b.name, pf[r0:r0 + 128, tok0:tok0 + TT], sg[:, 0:TT], r=[sg], w=[pf])
                for (c0, ncol, kind) in tm_blocks:
                    wt = self.wA[nA % 2]
                    nA += 1
                    self.wload(wt, wt[:, :, 0:ncol], mi, mi[l, :, c0:c0 + ncol].rearrange("(kc p) n -> p kc n", p=128))
                    for sub in range(TT // 128):
                        pp = self.ps[4 + nt % 3]
                        ts_ = tst[nt % 2]
                        nt += 1
                        t0 = tok0 + sub * 128
                        for kc in range(KC):
                            C.op('pe', lambda: nc.tensor.matmul(pp[:, 0:ncol], hT[:, kc, sub * 128:(sub + 1) * 128],
                                                                wt[:, kc, 0:ncol], start=(kc == 0), stop=(kc == KC - 1)),
                                 r=[wt, hT], w=[pp])
                        if kind == 'v':
                            C.op('act', lambda: nc.scalar.copy(ts_[:, 0:512], pp[:]), r=[pp], w=[ts_])
                            C.dma('sp', ts_.b.name, pt[t0:t0 + 128, PT['v']:PT['v'] + 512], ts_[:, 0:512], r=[ts_], w=[pt])
                        elif kind in ('f0', 'f1'):
                            d = int(kind[-1])
                            o = l * 1024 + d * 512
                            f, k = ts_[:, 0:512], ts_[:, 512:1024]
                            C.op('act', lambda: nc.scalar.activation(f, pp[:], AF.Sigmoid), r=[pp], w=[ts_])
                            C.op('dve', lambda: nc.vector.tensor_tensor(f, f, omlbr[:, d * 512:(d + 1) * 512], ALU.mult),
                                 r=[ts_, omlbr], w=[ts_])
                            C.op('dve', lambda: nc.vector.tensor_tensor(f, f, self.lbr[:, o:o + 512], ALU.add),
                                 r=[ts_, self.lbr], w=[ts_])
                            C.op('dve', lambda: nc.vector.tensor_scalar(k, f, -1.0, 1.0, ALU.mult, ALU.add),
                                 r=[ts_], w=[ts_])
                            C.op('dve', lambda: nc.vector.tensor_scalar(f, f, 1e-30, None, ALU.max), r=[ts_], w=[ts_])
                            C.op('act', lambda: nc.scalar.activation(f, f, AF.Ln), r=[ts_], w=[ts_])
                            C.dma('sp', ts_.b.name, pt[t0:t0 + 128, PT['logf'] + d * 512:PT['logf'] + (d + 1) * 512], f,
                                  r=[ts_], w=[pt])
                            C.dma('sp', ts_.b.name, pt[t0:t0 + 128, PT['k'] + d * 512:PT['k'] + (d + 1) * 512], k,
                                  r=[ts_], w=[pt])
                        else:
                            a, b = ts_[:, 0:8], ts_[:, 8:16]
                            C.op('dve', lambda: nc.vector.tensor_tensor(a, pp[:, 0:8], self.dtb[:, l * 8:l * 8 + 8], ALU.add),
                                 r=[pp, self.dtb], w=[ts_])
                            C.op('act', lambda: nc.scalar.activation(a, a, AF.Exp), r=[ts_], w=[ts_])
                            C.op('act', lambda: nc.scalar.activation(a, a, AF.Ln, bias=self.onec[:], scale=1.0),
                                 r=[ts_, self.onec], w=[ts_])
                            C.op('dve', lambda: nc.vector.tensor_tensor(a, a, self.negA[:, l * 8:l * 8 + 8], ALU.mult),
                                 r=[ts_, self.negA], w=[ts_])
                            C.op('act', lambda: nc.scalar.activation(b, pp[:, 8:16], AF.Sigmoid), r=[pp], w=[ts_])
                            C.dma('sp', ts_.b.name, pt[t0:t0 + 128, PT['lg']:PT['lg'] + 16], ts_[:, 0:16], r=[ts_], w=[pt])
            C.barrier()

    def mm(self, pt_, out_ap, lt, lhsT, rt, rhs, start=True, stop=True):
        nc = self.nc
        return self.C.op('pe', lambda: nc.tensor.matmul(out_ap, lhsT, rhs, start=start, stop=stop),
                         r=[lt, rt], w=[pt_], noyield=(not stop))

    def vtt(self, ot, out, at, a, bt, b, op, eng='dve'):
        nc = self.nc
        e = nc.vector if eng == 'dve' else nc.gpsimd
        return self.C.op(eng, lambda: e.tensor_tensor(out, a, b, op), r=[at, bt], w=[ot])

    def act(self, ot, out, it, in_, func, bias=None, scale=1.0, extra=()):
        nc = self.nc
        if bias is None:
            return self.C.op('act', lambda: nc.scalar.activation(out, in_, func, scale=scale), r=[it], w=[ot])
        return self.C.op('act', lambda: nc.scalar.activation(out, in_, func, bias=bias, scale=scale),
                         r=[it] + list(extra), w=[ot])

    def stage_hgrn(self, l):
        C, nc = self.C, self.nc
        self.set_pools('hgrn')
        pf, pt, ob, br = self.S['pf'], self.S['pt'], self.S['ob'], self.S['br']
        PF, PT = self.PF, self.PT
        cst = self.cst
        seqs = self.seqs()
        with ExitStack() as st:
            Sst = {(si, h): self.sb(st, 'hS%d_%d' % (si, h), [128, 128]) for si in range(len(seqs)) for h in range(4)}
            NB = 3
            qT4 = [self.sb(st, 'hq%d' % i, [128, 4, 128]) for i in range(NB)]
            kT4 = [self.sb(st, 'hk%d' % i, [128, 4, 128]) for i in range(NB)]
            kt4 = [self.sb(st, 'hkt%d' % i, [128, 512]) for i in range(NB)]
            lf4 = [self.sb(st, 'hlf%d' % i, [128, 512]) for i in range(NB)]
            vt4 = [self.sb(st, 'hv%d' % i, [128, 512]) for i in range(NB)]
            ob4 = [self.sb(st, 'hob%d' % i, [128, 4, 128]) for i in range(NB)]
            gT4 = [self.sb(st, 'hg%d' % i, [128, 4, 128]) for i in range(NB)]
            oo4 = [self.sb(st, 'hoo%d' % i, [128, 4, 128]) for i in range(NB)]
            R = 4
            ebT = [self.sb(st, 'heb%d' % i, [128, 128]) for i in range(R)]
            enb = [self.sb(st, 'henb%d' % i, [128, 128]) for i in range(R)]
            qeT = [self.sb(st, 'hqe%d' % i, [128, 128]) for i in range(R)]
            keT = [self.sb(st, 'hke%d' % i, [128, 128]) for i in range(R)]
            elm = [self.sb(st, 'helm%d' % i, [128, 128]) for i in range(R)]
            kl4 = [self.sb(st, 'hkl%d' % i, [128, 4, 128]) for i in range(R)]
            scm = [self.sb(st, 'hscm%d' % i, [128, 128]) for i in range(R)]
            sqt = [self.sb(st, 'hsq%d' % i, [128, 128]) for i in range(R)]
            rst = [self.sb(st, 'hrs%d' % i, [128, 128]) for i in range(R)]
            hn = self.P['hgrn_norm']
            u = 0
            nld = 0
            for d in (1, 0):
                tri = 'hg_tri_f' if d == 0 else 'hg_tri_b'
                trx = 'hg_trx_f' if d == 0 else 'hg_trx_b'
                for si, (tok0, L, kind, idx) in enumerate(seqs):
                    for h in range(4):
                        S = Sst[(si, h)]
                        if kind == 's':
                            C.dma('sp', S.b.name, S[:], self.I['state_hgrn'][l, d, h], r=[self.I['state_hgrn']], w=[S])
                        else:
                            C.op('dve', lambda: nc.vector.memset(S[:], 0.0), w=[S])
                maxt = max(L // 128 for (_, L, _, _) in seqs)
                for step in range(maxt):
                    for si, (tok0, L, kind, idx) in enumerate(seqs):
                        ntile = L // 128
                        if step >= ntile:
                            continue
                        ti = step if d == 0 else ntile - 1 - step
                        tok = tok0 + ti * 128
                        b = nld % NB
                        nld += 1
                        q4, k4, kt, lf, vt = qT4[b], kT4[b], kt4[b], lf4[b], vt4[b]
                        C.dma('sp', q4.b.name, q4[:], pf[PF['hg_q']:PF['hg_q'] + 512, tok:tok + 128]
                              .rearrange("(h p) t -> p h t", p=128), r=[pf], w=[q4])
                        r0 = PF['hg_k'] + d * 512
                        C.dma('sp', k4.b.name, k4[:], pf[r0:r0 + 512, tok:tok + 128].rearrange("(h p) t -> p h t", p=128),
                              r=[pf], w=[k4])
                        c0 = PT['k'] + d * 512
                        C.dma('sp', kt.b.name, kt[:], pt[tok:tok + 128, c0:c0 + 512], r=[pt], w=[kt])
                        c0 = PT['logf'] + d * 512
                        C.dma('sp', lf.b.name, lf[:], pt[tok:tok + 128, c0:c0 + 512], r=[pt], w=[lf])
                        C.dma('sp', vt.b.name, vt[:], pt[tok:tok + 128, PT['v']:PT['v'] + 512], r=[pt], w=[vt])
                        if d == 0:
                            o4, g4 = ob4[b], gT4[b]
                            C.dma('sp', o4.b.name, o4[:], ob[0:512, tok:tok + 128].rearrange("(h p) t -> p h t", p=128),
                                  r=[ob], w=[o4])
                            C.dma('sp', g4.b.name, g4[:], pf[PF['hg_g']:PF['hg_g'] + 512, tok:tok + 128]
                                  .rearrange("(h p) t -> p h t", p=128), r=[pf], w=[g4])
                        oo = oo4[b]
                        def unit(h):
                            C.tls.grp = h
                            S = Sst[(si, h)]
                            r = h
                            hs = slice(h * 128, (h + 1) * 128)
                            p_b, p_l, p_s, p_o = self.pn(), self.pn(), self.pn(), self.pql[h]
                            self.mm(p_b, p_b[:], lf, lf[:, hs], cst, self.cc(tri))
                            self.mm(p_l, p_l[:], cst, self.cc(trx), lf, lf[:, hs])
                            self.act(ebT[r], ebT[r][:], p_b, p_b[:], AF.Exp)
                            self.act(enb[r], enb[r][:], p_b, p_b[:], AF.Exp, scale=-1.0)
                            self.vtt(qeT[r], qeT[r][:], q4, q4[:, h, :], ebT[r], ebT[r][:], ALU.mult)
                            self.vtt(keT[r], keT[r][:], k4, k4[:, h, :], enb[r], enb[r][:], ALU.mult, eng='pool')
                            self.act(elm[r], elm[r][:], p_l, p_l[:], AF.Exp)
                            self.vtt(elm[r], elm[r][:], kt, kt[:, hs], elm[r], elm[r][:], ALU.mult)
                            C.op('pool', lambda: nc.gpsimd.tensor_tensor(
                                kl4[r][:], elm[r][:].unsqueeze(1).to_broadcast([128, 4, 128]),
                                self.cc('hg_cm').rearrange("p (c s) -> p c s", c=4)[:, :, 0:1].to_broadcast([128, 4, 128]),
                                ALU.mult), r=[elm[r], cst], w=[kl4[r]])
                            self.mm(p_s, p_s[:], keT[r], keT[r][:], qeT[r], qeT[r][:])
                            self.vtt(scm[r], scm[r][:], p_s, p_s[:], cst, self.cc(tri), ALU.mult)
                            self.mm(p_o, p_o[:], vt, vt[:, hs], scm[r], scm[r][:], start=True, stop=False)
                            order = range(4) if d == 0 else range(3, -1, -1)
                            for n_, c in enumerate(order):
                                cs = slice(c * 32, (c + 1) * 32)
                                self.mm(p_o, p_o[:, cs], S, S[:], qeT[r], qeT[r][:, cs], start=False, stop=(n_ == 3))
                                p_u = self.pn()
                                self.mm(p_u, p_u[:], kl4[r], kl4[r][:, c, :], vt, vt[:, hs])
                                col = c * 32 + 31 if d == 0 else c * 32
                                C.op('dve', lambda: nc.vector.scalar_tensor_tensor(
                                    S[:], S[:], ebT[r][:, col:col + 1], p_u[:], ALU.mult, ALU.add),
                                    r=[S, ebT[r], p_u], w=[S])
                            if d == 1:
                                C.op('act', lambda: nc.scalar.copy(oo[:, h, :], p_o[:]), r=[p_o], w=[oo])
                            else:
                                osum = qeT[r]
                                self.vtt(osum, osum[:], p_o, p_o[:], o4, o4[:, h, :], ALU.add)
                                self.act(sqt[r], sqt[r][:], osum, osum[:], AF.Square)
                                p_m = self.pn()
                                self.mm(p_m, p_m[:], self.ones128, self.ones128[:], sqt[r], sqt[r][:])
                                self.rsqrt(rst[r], rst[r][:], p_m, p_m[:], self.eps128)
                                C.op('dve', lambda: nc.vector.scalar_tensor_tensor(
                                    osum[:], osum[:], hn[:, l * 4 + h:l * 4 + h + 1], rst[r][:], ALU.mult, ALU.mult),
                                    r=[osum, hn, rst[r]], w=[osum])
                                self.vtt(oo, oo[:, h, :], osum, osum[:], g4, g4[:, h, :], ALU.mult)
                            if kind == 'p' and step == ntile - 1:
                                dst = self.O['ns_hgrn']
                                C.dma('sp', S.b.name, dst[idx, l, d, h], S[:], r=[S], w=[dst])
                        C.run_interleaved([(lambda h=h: unit(h)) for h in range(4)])
                        tgt = ob if d == 1 else br
                        C.dma('sp', oo.b.name, tgt[0:512, tok:tok + 128].rearrange("(h p) t -> p h t", p=128), oo[:],
                              r=[oo], w=[tgt])
            C.barrier()

    def stage_gdn_pre(self, l):
        C, nc = self.C, self.nc
        self.set_pools('gdn')
        pf = self.S['pf']
        PF = self.PF
        if 'gt' not in self.S:
            self.S['gt'] = self.dscr('gt', [self.NT, 1024])
        gt = self.S['gt']
        cw = self.P['gdn_conv']
        with ExitStack() as st:
            LM = max(self.LP, self.LS)
            xin = [self.sb(st, 'gx%d' % i, [128, LM]) for i in range(2)]
            xo = [self.sb(st, 'gxo%d' % i, [128, LM]) for i in range(2)]
            sq = [self.sb(st, 'gsq%d' % i, [128, 512]) for i in range(2)]
            rs = [self.sb(st, 'grs%d' % i, [128, 512]) for i in range(2)]
            tr = [self.sb(st, 'gtr%d' % i, [128, 128]) for i in range(3)]
            n = nq = ntr = 0
            for (tok0, L, kind, idx) in self.seqs():
                if kind == 's':
                    rows, W, dys = L // 64, 64, (0, 1, 2)
                else:
                    rows, W, dys = 1, L, (1,)
                for cc in range(12):
                    xi, xo_ = xin[n % 2], xo[n % 2]
                    n += 1
                    r0 = PF['gd_qkv'] + cc * 128
                    C.dma('sp', xi.b.name, xi[:, 0:L], pf[r0:r0 + 128, tok0:tok0 + L], r=[pf], w=[xi])
                    wc = lambda dy, dx: cw[:, l * 108 + (dy * 3 + dx) * 12 + cc:l * 108 + (dy * 3 + dx) * 12 + cc + 1]
                    C.op('dve', lambda: nc.vector.tensor_scalar(xo_[:, 0:L], xi[:, 0:L], wc(1, 1), None, ALU.mult),
                         r=[xi, cw], w=[xo_])
                    x3 = xi[:, 0:L].rearrange("p (r w) -> p r w", w=W)
                    o3 = xo_[:, 0:L].rearrange("p (r w) -> p r w", w=W)
                    for dy in dys:
                        for dx in range(3):
                            if dy == 1 and dx == 1:
                                continue
                            oy, ox = dy - 1, dx - 1
                            ra, rb = max(0, -oy), rows - max(0, oy)
                            ca, cb = max(0, -ox), W - max(0, ox)
                            if rb <= ra:
                                continue
                            C.op('dve', lambda: nc.vector.scalar_tensor_tensor(
                                o3[:, ra:rb, ca:cb], x3[:, ra + oy:rb + oy, ca + ox:cb + ox], wc(dy, dx),
                                o3[:, ra:rb, ca:cb], ALU.mult, ALU.add), r=[xi, cw, xo_], w=[xo_])
                    self.act(xo_, xo_[:, 0:L], xo_, xo_[:, 0:L], AF.Silu)
                    if cc < 8:
                        for t0 in range(0, L, 512):
                            tt = min(512, L - t0)
                            s_, r_ = sq[nq % 2], rs[nq % 2]
                            pp = self.ps[nq % 2]
                            nq += 1
                            self.act(s_, s_[:, 0:tt], xo_, xo_[:, t0:t0 + tt], AF.Square)
                            self.mm(pp, pp[:, 0:tt], self.cst, self.cc('ones'), s_, s_[:, 0:tt])
                            self.rsqrt(r_, r_[:, 0:tt], pp, pp[:, 0:tt], self.epsc)
                            self.vtt(xo_, xo_[:, t0:t0 + tt], xo_, xo_[:, t0:t0 + tt], r_, r_[:, 0:tt], ALU.mult)
                    C.dma('sp', xo_.b.name, pf[r0:r0 + 128, tok0:tok0 + L], xo_[:, 0:L], r=[xo_], w=[pf])
                    if cc >= 4:
                        for t0 in range(0, L, 128):
                            p_ = self.pn()
                            t_ = tr[ntr % 3]
                            ntr += 1
                            self.mm(p_, p_[:], xo_, xo_[:, t0:t0 + 128], self.cst, self.cc('ident'))
                            C.op('act', lambda: nc.scalar.copy(t_[:], p_[:]), r=[p_], w=[t_])
                            c0 = (cc - 4) * 128
                            C.dma('sp', t_.b.name, gt[tok0 + t0:tok0 + t0 + 128, c0:c0 + 128], t_[:], r=[t_], w=[gt])
            C.barrier()

    def finalize_branch(self, p_o, o4h, gh, ncol_t, ncol, eps_t, out_t, out_ap, scr, sqt_, rst_, onesm):
        C, nc = self.C, self.nc
        self.vtt(scr, scr[:], p_o, p_o[:], o4h[0], o4h[1], ALU.add)
        self.act(sqt_, sqt_[:], scr, scr[:], AF.Square)
        p_m = self.pn()
        self.mm(p_m, p_m[:], onesm, onesm[:], sqt_, sqt_[:])
        self.rsqrt(rst_, rst_[:], p_m, p_m[:], eps_t)
        C.op('dve', lambda: nc.vector.scalar_tensor_tensor(scr[:], scr[:], ncol, rst_[:], ALU.mult, ALU.mult),
             r=[scr, ncol_t, rst_], w=[scr])
        self.vtt(out_t, out_ap, scr, scr[:], gh[0], gh[1], ALU.mult)

    def stage_gdn(self, l):
        C, nc = self.C, self.nc
        self.set_pools('gdn')
        pf, pt, gt, ob, br = self.S['pf'], self.S['pt'], self.S['gt'], self.S['ob'], self.S['br']
        PF, PT = self.PF, self.PT
        cst = self.cst
        seqs = self.seqs()
        ident = self.cc('ident')
        with ExitStack() as st:
            Sst = {(si, h): self.sb(st, 'gS%d_%d' % (si, h), [128, 128]) for si in range(len(seqs)) for h in range(4)}
            NB = 3
            qT4 = [self.sb(st, 'gq%d' % i, [128, 4, 128]) for i in range(NB)]
            kT4 = [self.sb(st, 'gk%d' % i, [128, 4, 128]) for i in range(NB)]
            kv4 = [self.sb(st, 'gkv%d' % i, [128, 1024]) for i in range(NB)]
            ab4 = [self.sb(st, 'gab%d' % i, [128, 16]) for i in range(NB)]
            nb4 = [self.sb(st, 'gnb%d' % i, [128, 8]) for i in range(NB)]
            ob4 = [self.sb(st, 'gob%d' % i, [128, 4, 128]) for i in range(NB)]
            gT4 = [self.sb(st, 'gg%d' % i, [128, 4, 128]) for i in range(NB)]
            oo4 = [self.sb(st, 'goo%d' % i, [128, 4, 128]) for i in range(NB)]
            R = 4
            names = ['lgb', 'lgbn', 'DT', 'DTs', 'EG', 'PT', 'Ua', 'Ub', 'La', 'Lb', 'Pa', 'Pb', 'keg', 'qd', 'kd', 'X',
                     'vn', 'scr', 'sq', 'rs']
            W = {nm: [self.sb(st, 'g%s%d' % (nm, i), [128, 128]) for i in range(R)] for nm in names}
            sc = [self.sb(st, 'gsc%d' % i, [128, 4]) for i in range(R)]
            gn = self.P['gdn_norm']
            u = nld = 0
            for d in (1, 0):
                tri = self.cc('gd_tri_f' if d == 0 else 'gd_tri_b')
                neg = self.cc('gd_neg_f' if d == 0 else 'gd_neg_b')
                stm = self.cc('gd_st_f' if d == 0 else 'gd_st_b')
                for si, (tok0, L, kind, idx) in enumerate(seqs):
                    for h in range(4):
                        S = Sst[(si, h)]
                        if kind == 's':
                            C.dma('sp', S.b.name, S[:], self.I['state_gdn'][l, d, h], r=[self.I['state_gdn']], w=[S])
                        else:
                            C.op('dve', lambda: nc.vector.memset(S[:], 0.0), w=[S])
                maxt = max(L // 128 for (_, L, _, _) in seqs)
                for step in range(maxt):
                    for si, (tok0, L, kind, idx) in enumerate(seqs):
                        ntile = L // 128
                        if step >= ntile:
                            continue
                        ti = step if d == 0 else ntile - 1 - step
                        tok = tok0 + ti * 128
                        b = nld % NB
                        nld += 1
                        q4, k4, kv, ab, nb_ = qT4[b], kT4[b], kv4[b], ab4[b], nb4[b]
                        r0 = PF['gd_qkv']
                        C.dma('sp', q4.b.name, q4[:], pf[r0:r0 + 512, tok:tok + 128].rearrange("(h p) t -> p h t", p=128),
                              r=[pf], w=[q4])
                        C.dma('sp', k4.b.name, k4[:], pf[r0 + 512:r0 + 1024, tok:tok + 128]
                              .rearrange("(h p) t -> p h t", p=128), r=[pf], w=[k4])
                        C.dma('sp', kv.b.name, kv[:], gt[tok:tok + 128, :], r=[gt], w=[kv])
                        C.dma('sp', ab.b.name, ab[:], pt[tok:tok + 128, PT['lg']:PT['lg'] + 16], r=[pt], w=[ab])
                        C.op('dve', lambda: nc.vector.tensor_scalar(nb_[:], ab[:, 8:16], -1.0, None, ALU.mult),
                             r=[ab], w=[nb_])
                        if d == 0:
                            o4, g4 = ob4[b], gT4[b]
                            C.dma('sp', o4.b.name, o4[:], ob[512:1024, tok:tok + 128].rearrange("(h p) t -> p h t", p=128),
                                  r=[ob], w=[o4])
                            C.dma('sp', g4.b.name, g4[:], pf[PF['gd_z']:PF['gd_z'] + 512, tok:tok + 128]
                                  .rearrange("(h p) t -> p h t", p=128), r=[pf], w=[g4])
                        oo = oo4[b]
                        def unit(h):
                            C.tls.grp = h
                            S = Sst[(si, h)]
                            r = h
                            w = {nm: W[nm][r] for nm in names}
                            hs = slice(h * 128, (h + 1) * 128)
                            lgc = ab[:, d * 4 + h:d * 4 + h + 1]
                            bc = ab[:, 8 + d * 4 + h:8 + d * 4 + h + 1]
                            nbc = nb_[:, d * 4 + h:d * 4 + h + 1]
                            kT, qT = k4[:, h, :], q4[:, h, :]
                            ktok, vtok = kv[:, hs], kv[:, 512 + h * 128:512 + (h + 1) * 128]
                            C.op('dve', lambda: nc.vector.tensor_scalar(w['lgb'][:], self.cc('ones'), lgc, None, ALU.mult),
                                 r=[cst, ab], w=[w['lgb']])
                            C.op('pool', lambda: nc.gpsimd.tensor_scalar(w['lgbn'][:], w['lgb'][:], -1.0, None, ALU.mult),
                                 r=[w['lgb']], w=[w['lgbn']])
                            p_d, p_g, p_c, p_kk, p_qk = self.pn(), self.pn(), self.pn(), self.pn(), self.pn()
                            self.mm(p_d, p_d[:], w['lgb'], w['lgb'][:], cst, tri, True, False)
                            self.mm(p_d, p_d[:], cst, tri, w['lgbn'], w['lgbn'][:], False, False)
                            self.mm(p_d, p_d[:], cst, ident, cst, neg, False, True)
                            self.mm(p_g, p_g[:], w['lgb'], w['lgb'][:], cst, tri)
                            self.mm(p_c, p_c[:, 0:1], cst, tri, ab, lgc)
                            self.mm(p_c, p_c[:, 1:2], cst, self.cc('ones'), ab, lgc)
                            self.mm(p_kk, p_kk[:], k4, kT, k4, kT)
                            self.mm(p_qk, p_qk[:], k4, kT, q4, qT)
                            self.act(w['DT'], w['DT'][:], p_d, p_d[:], AF.Exp)
                            self.act(w['EG'], w['EG'][:], p_g, p_g[:], AF.Exp)
                            s_ = sc[r]
                            C.op('dve', lambda: nc.vector.tensor_copy(s_[:, 0:1], p_c[:, 1:2]), r=[p_c], w=[s_])
                            self.act(s_, s_[:, 1:2], p_c, p_c[:, 1:2], AF.Exp)
                            self.act(s_, s_[:, 2:3], p_c, p_c[:, 0:1], AF.Exp, bias=s_[:, 0:1], scale=-1.0, extra=[s_])
                            self.vtt(w['DTs'], w['DTs'][:], w['DT'], w['DT'][:], cst, stm, ALU.mult, eng='pool')
                            U, Lm, P_ = w['Ua'], w['La'], w['Pa']
                            U2, L2, P2 = w['Ub'], w['Lb'], w['Pb']
                            C.op('dve', lambda: nc.vector.scalar_tensor_tensor(U[:], p_kk[:], nbc, w['DTs'][:],
                                                                               ALU.mult, ALU.mult),
                                 r=[p_kk, nb_, w['DTs']], w=[U])
                            self.vtt(w['PT'], w['PT'][:], p_qk, p_qk[:], w['DT'], w['DT'][:], ALU.mult)
                            p_t = self.pn()
                            self.mm(p_t, p_t[:], U, U[:], cst, ident)
                            C.op('act', lambda: nc.scalar.copy(Lm[:], p_t[:]), r=[p_t], w=[Lm])
                            self.vtt(P_, P_[:], U, U[:], cst, ident, ALU.add, eng='pool')
                            for n_ in range(6):
                                p_u, p_l, p_p = self.pn(), self.pn(), self.pn()
                                self.mm(p_u, p_u[:], Lm, Lm[:], U, U[:])
                                self.mm(p_l, p_l[:], U, U[:], Lm, Lm[:])
                                C.op('act', lambda: nc.scalar.copy(U2[:], p_u[:]), r=[p_u], w=[U2])
                                C.op('dve', lambda: nc.vector.tensor_copy(L2[:], p_l[:]), r=[p_l], w=[L2])
                                self.mm(p_p, p_p[:], L2, L2[:], P_, P_[:])
                                self.vtt(P2, P2[:], p_p, p_p[:], P_, P_[:], ALU.add)
                                U, U2, Lm, L2, P_, P2 = U2, U, L2, Lm, P2, P_
                            self.vtt(w['keg'], w['keg'][:], k4, kT, w['EG'], w['EG'][:], ALU.mult, eng='pool')
                            self.vtt(w['qd'], w['qd'][:], q4, qT, w['EG'], w['EG'][:], ALU.mult, eng='pool')
                            C.op('dve', lambda: nc.vector.tensor_scalar(w['kd'][:], ktok, s_[:, 2:3], None, ALU.mult),
                                 r=[kv, s_], w=[w['kd']])
                            p_x, p_v, p_o, p_k = self.pn(), self.pn(), self.pn(), self.pn()
                            self.mm(p_x, p_x[:], w['keg'], w['keg'][:], S, S[:])
                            self.vtt(w['X'], w['X'][:], kv, vtok, p_x, p_x[:], ALU.subtract)
                            self.mm(p_v, p_v[:], P_, P_[:], w['X'], w['X'][:])
                            C.op('dve', lambda: nc.vector.tensor_scalar(w['vn'][:], p_v[:], bc, None, ALU.mult),
                                 r=[p_v, ab], w=[w['vn']])
                            self.mm(p_o, p_o[:], S, S[:], w['qd'], w['qd'][:], True, False)
                            self.mm(p_o, p_o[:], w['vn'], w['vn'][:], w['PT'], w['PT'][:], False, True)
                            self.mm(p_k, p_k[:], w['kd'], w['kd'][:], w['vn'], w['vn'][:])
                            C.op('dve', lambda: nc.vector.scalar_tensor_tensor(S[:], S[:], s_[:, 1:2], p_k[:],
                                                                               ALU.mult, ALU.add),
                                 r=[S, s_, p_k], w=[S])
                            if d == 1:
                                C.op('act', lambda: nc.scalar.copy(oo[:, h, :], p_o[:]), r=[p_o], w=[oo])
                            else:
                                self.finalize_branch(p_o, (o4, o4[:, h, :]), (g4, g4[:, h, :]), gn, gn[:, l:l + 1],
                                                     self.eps128, oo, oo[:, h, :], w['scr'], w['sq'], w['rs'],
                                                     self.ones128)
                            if kind == 'p' and step == ntile - 1:
                                dst = self.O['ns_gdn']
                                C.dma('sp', S.b.name, dst[idx, l, d, h], S[:], r=[S], w=[dst])
                        C.run_interleaved([(lambda h=h: unit(h)) for h in range(4)])
                        tgt = ob if d == 1 else br
                        C.dma('sp', oo.b.name, tgt[512:1024, tok:tok + 128].rearrange("(h p) t -> p h t", p=128), oo[:],
                              r=[oo], w=[tgt])
            C.barrier()

    def stage_rwkv(self, l):
        C, nc = self.C, self.nc
        self.set_pools('rwkv')
        pf, ob, br = self.S['pf'], self.S['ob'], self.S['br']
        if 'bb' not in self.S:
            self.S['bb'] = self.dscr('bb', [512, self.NT])
        bb = self.S['bb']
        PF = self.PF
        cst = self.cst
        ident = self.cc('ident')
        bones = self.cc('bones64')
        blk4 = self.cc('rw_blk').rearrange("p (c h s) -> p c h s", c=2, h=2)
        seqs = self.seqs()
        I = self.I
        CW = 0.6065306597126334
        P = self.P
        with ExitStack() as st:
            w2t = self.sb(st, 'rw2', [128, 512])
            a2t = self.sb(st, 'ra2', [128, 512])
            g2t = self.sb(st, 'rg2', [128, 512])
            w0r = self.sb(st, 'rw0r', [128, 1024])
            omka = self.sb(st, 'romka', [128, 4])
            C.dma('sp', 'rw2', w2t[:], I['rwkv_w2'][l].rearrange("d r c -> (d r) c"), r=[I['rwkv_w2']], w=[w2t])
            C.dma('sp', 'ra2', a2t[:], I['rwkv_a2'][l].rearrange("d r c -> (d r) c"), r=[I['rwkv_a2']], w=[a2t])
            C.dma('sp', 'rg2', g2t[:], I['rwkv_g2'][l], r=[I['rwkv_g2']], w=[g2t])
            C.dma('sp', 'rw0r', w0r[:], I['rwkv_w0'][l].rearrange("d c -> (d c)").partition_broadcast(128),
                  r=[I['rwkv_w0']], w=[w0r])
            ka = P['rwkv_ka']
            C.op('dve', lambda: nc.vector.tensor_scalar(omka[:], ka[:, l * 4:l * 4 + 4], -1.0, 1.0, ALU.mult, ALU.add),
                 r=[ka], w=[omka])
            Zst = {(si, p): self.sb(st, 'rZ%d_%d' % (si, p), [128, 128]) for si in range(len(seqs)) for p in range(4)}
            NB = 2
            zstg = [self.sb(st, 'rzs%d' % i, [128, 128]) for i in range(2)]
            r4 = [self.sb(st, 'rr%d' % i, [128, 4, 128]) for i in range(NB)]
            k4 = [self.sb(st, 'rk%d' % i, [128, 4, 128]) for i in range(NB)]
            v4 = [self.sb(st, 'rv%d' % i, [128, 4, 128]) for i in range(NB)]
            wl = [self.sb(st, 'rwl%d' % i, [128, 128]) for i in range(NB)]
            al = [self.sb(st, 'ral%d' % i, [128, 128]) for i in range(NB)]
            gd = [self.sb(st, 'rgd%d' % i, [128, 128]) for i in range(NB)]
            yb4 = [self.sb(st, 'ryb%d' % i, [128, 4, 128]) for i in range(NB)]
            bb4 = [self.sb(st, 'rbb%d' % i, [128, 4, 128]) for i in range(NB)]
            oo4 = [self.sb(st, 'roo%d' % i, [128, 4, 128]) for i in range(NB)]
            pb4 = [self.sb(st, 'rpb%d' % i, [128, 4, 128]) for i in range(NB)]
            R = 2
            n128 = ['kk', 'sq', 'rs', 'sz', 'Ep', 'Em', 'Ex', 'El', 'icl', 't1', 'kd', 'bT', 'rt', 'og', 'ys', 'yc', 'pr',
                    'X', 'Ub', 'M0', 'Aak', 'Ua', 'Ub2', 'La', 'Lb', 'Pa', 'Pb', 'Bbd', 'Kbd', 'Vbd0', 'Vbd1']
            W = {nm: [self.sb(st, 'r%s%d' % (nm, i), [128, 128]) for i in range(R)] for nm in n128}
            n256 = ['ve', 'at2', 'ate', 'bte', 'kte', 'Bhe', 'Khe', 'tmp']
            W2 = {nm: [self.sb(st, 'r%s%d' % (nm, i), [128, 256]) for i in range(R)] for nm in n256}
            n64 = ['Arb', 'Ark']
            W3 = {nm: [self.sb(st, 'r%s%d' % (nm, i), [128, 64]) for i in range(R)] for nm in n64}
            u = nld = 0

            def expand(dst, src_t, src_ap, eng='pool'):
                e = nc.gpsimd if eng == 'pool' else nc.vector
                C.op(eng, lambda: e.tensor_tensor(
                    dst[:].rearrange("p (c h s) -> p c h s", c=2, h=2),
                    src_ap.rearrange("p (c s) -> p c s", c=2).unsqueeze(2).to_broadcast([128, 2, 2, 64]),
                    blk4, ALU.mult), r=[src_t, cst], w=[dst])

            def ch(t2, c):
                return t2[:].rearrange("p (c m) -> p c m", c=2)[:, c, :]

            for d in (1, 0):
                f = (d == 0)
                tri = self.cc('rw_tri_f' if f else 'rw_tri_b')
                trx = self.cc('rw_trx_f' if f else 'rw_trx_b')
                tra = self.cc('rw_trx_b' if f else 'rw_trx_f')
                bst = self.cc('rw_bst_f' if f else 'rw_bst_b')
                sin = self.cc('rw_sin_f' if f else 'rw_sin_b')
                for si, (tok0, L, kind, idx) in enumerate(seqs):
                    for p in range(4):
                        Z = Zst[(si, p)]
                        if kind == 's':
                            zs = zstg[(si + p) % 2]
                            C.op('dve', lambda: nc.vector.memset(zs[:], 0.0), w=[zs])
                            for hp in range(2):
                                C.dma('sp', zs.b.name, zs[hp * 64:(hp + 1) * 64, hp * 64:(hp + 1) * 64],
                                      I['state_rwkv'][l, d, p * 2 + hp], r=[I['state_rwkv']], w=[zs])
                            p_ = self.pn()
                            self.mm(p_, p_[:], zs, zs[:], cst, ident)
                            C.op('dve', lambda: nc.vector.tensor_copy(Z[:], p_[:]), r=[p_], w=[Z])
                        else:
                            C.op('dve', lambda: nc.vector.memset(Z[:], 0.0), w=[Z])
                maxt = max(L // 128 for (_, L, _, _) in seqs)
                for step in range(maxt):
                    for si, (tok0, L, kind, idx) in enumerate(seqs):
                        ntile = L // 128
                        if step >= ntile:
                            continue
                        ti = step if f else ntile - 1 - step
                        tok = tok0 + ti * 128
                        b = nld % NB
                        nld += 1
                        rr, kk_, vv, wl_, al_, gd_ = r4[b], k4[b], v4[b], wl[b], al[b], gd[b]
                        r0 = PF['rw_rkv']
                        for j_, t_ in enumerate((rr, kk_, vv)):
                            C.dma('sp', t_.b.name, t_[:], pf[r0 + j_ * 512:r0 + (j_ + 1) * 512, tok:tok + 128]
                                  .rearrange("(h p) t -> p h t", p=128), r=[pf], w=[t_])
                        C.dma('sp', wl_.b.name, wl_[:], pf[PF['rw_wd']:PF['rw_wd'] + 128, tok:tok + 128], r=[pf], w=[wl_])
                        C.dma('sp', al_.b.name, al_[:], pf[PF['rw_ad']:PF['rw_ad'] + 128, tok:tok + 128], r=[pf], w=[al_])
                        if f:
                            C.dma('sp', gd_.b.name, gd_[:], pf[PF['rw_gd']:PF['rw_gd'] + 128, tok:tok + 128],
                                  r=[pf], w=[gd_])
                            yb, bbt = yb4[b], bb4[b]
                            C.dma('sp', yb.b.name, yb[:], ob[1024:1536, tok:tok + 128].rearrange("(h p) t -> p h t", p=128),
                                  r=[ob], w=[yb])
                            C.dma('sp', bbt.b.name, bbt[:], bb[:, tok:tok + 128].rearrange("(h p) t -> p h t", p=128),
                                  r=[bb], w=[bbt])
                        oo, pbo = oo4[b], pb4[b]
                        def unit(p):
                            C.tls.grp = 4 + (p % 2)
                            Z = Zst[(si, p)]
                            rix = p % 2
                            w = {nm: W[nm][rix] for nm in n128}
                            w.update({nm: W2[nm][rix] for nm in n256})
                            w.update({nm: W3[nm][rix] for nm in n64})
                            ps_ = slice(p * 128, (p + 1) * 128)
                            rT, kT, vT = rr[:, p, :], kk_[:, p, :], vv[:, p, :]
                            col = lambda nm: P[nm][:, l * 4 + p:l * 4 + p + 1]
                            C.op('dve', lambda: nc.vector.tensor_scalar(w['kk'][:], kT, col('rwkv_kk'), None, ALU.mult),
                                 r=[kk_, P['rwkv_kk']], w=[w['kk']])
                            self.act(w['sq'], w['sq'][:], w['kk'], w['kk'][:], AF.Square)
                            p_ = self.pn()
                            self.mm(p_, p_[:], cst, bones, w['sq'], w['sq'][:])
                            self.rsqrt(w['rs'], w['rs'][:], p_, p_[:], self.epsc)
                            self.vtt(w['kk'], w['kk'][:], w['kk'], w['kk'][:], w['rs'], w['rs'][:], ALU.mult, eng='pool')
                            expand(w['ve'], vv, vT)
                            for c in range(2):
                                p_ = self.pn()
                                self.mm(p_, p_[:], w['ve'], ch(w['ve'], c), cst, ident)
                                vb = w['Vbd%d' % c]
                                C.op('act', lambda: nc.scalar.copy(vb[:], p_[:]), r=[p_], w=[vb])
                            dsl = slice(d * 64, (d + 1) * 64)
                            p_z = self.pn()
                            self.mm(p_z, p_z[:], wl_, wl_[dsl, :], w2t, w2t[dsl, ps_])
                            self.vtt(w['sz'], w['sz'][:], p_z, p_z[:], w0r, w0r[:, d * 512 + p * 128:d * 512 + (p + 1) * 128],
                                     ALU.add)
                            self.act(w['sz'], w['sz'][:], w['sz'], w['sz'][:], AF.Sigmoid)
                            p_c, p_x_, p_a = self.pn(), self.pn(), self.pn()
                            self.mm(p_c, p_c[:], w['sz'], w['sz'][:], cst, tri)
                            self.mm(p_x_, p_x_[:], w['sz'], w['sz'][:], cst, trx)
                            self.mm(p_a, p_a[:], w['sz'], w['sz'][:], cst, tra)
                            self.act(w['Ep'], w['Ep'][:], p_c, p_c[:], AF.Exp, scale=-CW)
                            self.act(w['Em'], w['Em'][:], p_c, p_c[:], AF.Exp, scale=CW)
                            self.act(w['Ex'], w['Ex'][:], p_x_, p_x_[:], AF.Exp, scale=-CW)
                            self.act(w['El'], w['El'][:], p_a, p_a[:], AF.Exp, scale=-CW)
                            p_i = self.pn()
                            self.mm(p_i, p_i[:], a2t, a2t[dsl, ps_], al_, al_[dsl, :])
                            a0 = P['rwkv_a0']
                            self.act(w['icl'], w['icl'][:], p_i, p_i[:], AF.Sigmoid,
                                     bias=a0[:, l * 8 + d * 4 + p:l * 8 + d * 4 + p + 1], extra=[a0])
                            C.op('dve', lambda: nc.vector.tensor_scalar(w['t1'][:], w['icl'][:], col('rwkv_ka'),
                                                                        omka[:, p:p + 1], ALU.mult, ALU.add),
                                 r=[w['icl'], ka, omka], w=[w['t1']])
                            self.vtt(w['kd'], w['kd'][:], kk_, kT, w['t1'], w['t1'][:], ALU.mult, eng='pool')
                            self.vtt(w['bT'], w['bT'][:], w['kk'], w['kk'][:], w['icl'], w['icl'][:], ALU.mult, eng='pool')
                            C.op('dve', lambda: nc.vector.scalar_tensor_tensor(w['pr'][:], rr[:, p, :], col('rwkv_rk'),
                                                                               w['kd'][:], ALU.mult, ALU.mult),
                                 r=[rr, P['rwkv_rk'], w['kd']], w=[w['pr']])
                            self.vtt(w['rt'], w['rt'][:], rr, rT, w['Ep'], w['Ep'][:], ALU.mult)
                            tmp = w['tmp']
                            C.op('dve', lambda: nc.vector.scalar_tensor_tensor(w['t1'][:], w['kk'][:], -1.0, w['Ex'][:],
                                                                               ALU.mult, ALU.mult),
                                 r=[w['kk'], w['Ex']], w=[w['t1']])
                            expand(w['ate'], w['t1'], w['t1'][:])
                            C.op('pool', lambda: nc.gpsimd.tensor_copy(
                                w['at2'][:].rearrange("p (c h s) -> p c h s", c=2, h=2),
                                w['t1'][:].rearrange("p (c s) -> p c s", c=2).unsqueeze(2).to_broadcast([128, 2, 2, 64])),
                                r=[w['t1']], w=[w['at2']])
                            self.vtt(w['sq'], w['sq'][:], w['bT'], w['bT'][:], w['Em'], w['Em'][:], ALU.mult)
                            expand(w['bte'], w['sq'], w['sq'][:])
                            self.vtt(w['rs'], w['rs'][:], w['kd'], w['kd'][:], w['Em'], w['Em'][:], ALU.mult)
                            expand(w['kte'], w['rs'], w['rs'][:])
                            self.vtt(w['ys'], w['ys'][:], w['bT'], w['bT'][:], w['El'], w['El'][:], ALU.mult)
                            expand(w['Bhe'], w['ys'], w['ys'][:])
                            self.vtt(w['yc'], w['yc'][:], w['kd'], w['kd'][:], w['El'], w['El'][:], ALU.mult)
                            expand(w['Khe'], w['yc'], w['yc'][:])
                            p_ys = (self.pql[(p % 2) * 2], self.pql[(p % 2) * 2 + 1])
                            for c in ((0, 1) if f else (1, 0)):
                                p_y = p_ys[c]
                                cs_ = slice(c * 64, (c + 1) * 64)
                                vb = w['Vbd%d' % c]
                                p_ab, p_ak, p_rb, p_rk = self.pn(), self.pn(), self.pn(), self.pn()
                                self.mm(p_ab, p_ab[:], w['bte'], ch(w['bte'], c), w['at2'], ch(w['at2'], c))
                                self.mm(p_ak, p_ak[:], w['kte'], ch(w['kte'], c), w['at2'], ch(w['at2'], c))
                                self.mm(p_rb, p_rb[:, 0:64], w['bte'], ch(w['bte'], c), w['rt'], w['rt'][:, cs_])
                                self.mm(p_rk, p_rk[:, 0:64], w['kte'], ch(w['kte'], c), w['rt'], w['rt'][:, cs_])
                                U, Lm, P_ = w['Ua'], w['La'], w['Pa']
                                U2, L2, P2 = w['Ub2'], w['Lb'], w['Pb']
                                self.vtt(U, U[:], p_ab, p_ab[:], cst, bst, ALU.mult)
                                self.vtt(w['Aak'], w['Aak'][:], p_ak, p_ak[:], cst, bst, ALU.mult)
                                self.vtt(w['Arb'], w['Arb'][:], p_rb, p_rb[:, 0:64], cst, sin, ALU.mult)
                                self.vtt(w['Ark'], w['Ark'][:], p_rk, p_rk[:, 0:64], cst, sin, ALU.mult)
                                p_t = self.pn()
                                self.mm(p_t, p_t[:], U, U[:], cst, ident)
                                C.op('act', lambda: nc.scalar.copy(Lm[:], p_t[:]), r=[p_t], w=[Lm])
                                self.vtt(P_, P_[:], U, U[:], cst, ident, ALU.add, eng='pool')
                                for n_ in range(5):
                                    p_u, p_l, p_p = self.pn(), self.pn(), self.pn()
                                    self.mm(p_u, p_u[:], Lm, Lm[:], U, U[:])
                                    self.mm(p_l, p_l[:], U, U[:], Lm, Lm[:])
                                    C.op('act', lambda: nc.scalar.copy(U2[:], p_u[:]), r=[p_u], w=[U2])
                                    C.op('dve', lambda: nc.vector.tensor_copy(L2[:], p_l[:]), r=[p_l], w=[L2])
                                    self.mm(p_p, p_p[:], L2, L2[:], P_, P_[:])
                                    self.vtt(P2, P2[:], p_p, p_p[:], P_, P_[:], ALU.add)
                                    U, U2, Lm, L2, P_, P2 = U2, U, L2, Lm, P2, P_
                                p_b1, p_k1 = self.pn(), self.pn()
                                self.mm(p_b1, p_b1[:], w['Bhe'], ch(w['Bhe'], c), cst, ident)
                                self.mm(p_k1, p_k1[:], w['Khe'], ch(w['Khe'], c), cst, ident)
                                C.op('act', lambda: nc.scalar.copy(w['Bbd'][:], p_b1[:]), r=[p_b1], w=[w['Bbd']])
                                C.op('dve', lambda: nc.vector.tensor_copy(w['Kbd'][:], p_k1[:]), r=[p_k1], w=[w['Kbd']])
                                p_x, p_u2, p_zz = self.pn(), self.pn(), self.pn()
                                self.mm(p_x, p_x[:], w['ate'], ch(w['ate'], c), Z, Z[:], True, False)
                                self.mm(p_x, p_x[:], w['Aak'], w['Aak'][:], vb, vb[:], False, True)
                                C.op('act', lambda: nc.scalar.copy(w['X'][:], p_x[:]), r=[p_x], w=[w['X']])
                                self.mm(p_u2, p_u2[:], P_, P_[:], w['X'], w['X'][:])
                                C.op('dve', lambda: nc.vector.tensor_copy(w['Ub'][:], p_u2[:]), r=[p_u2], w=[w['Ub']])
                                self.mm(p_y, p_y[:, 0:64], Z, Z[:], w['rt'], w['rt'][:, cs_], True, False)
                                self.mm(p_y, p_y[:, 0:64], w['Ub'], w['Ub'][:], w['Arb'], w['Arb'][:], False, False)
                                self.mm(p_y, p_y[:, 0:64], vb, vb[:], w['Ark'], w['Ark'][:], False, True)
                                self.mm(p_zz, p_zz[:], w['Bbd'], w['Bbd'][:], w['Ub'], w['Ub'][:], True, False)
                                self.mm(p_zz, p_zz[:], w['Kbd'], w['Kbd'][:], vb, vb[:], False, True)
                                gcol = c * 64 + 63 if f else c * 64
                                C.op('dve', lambda: nc.vector.scalar_tensor_tensor(
                                    Z[:], Z[:], w['Ep'][:, gcol:gcol + 1], p_zz[:], ALU.mult, ALU.add),
                                    r=[Z, w['Ep'], p_zz], w=[Z])
                            if not f:
                                for c in range(2):
                                    C.op('act', lambda: nc.scalar.copy(oo[:, p, c * 64:(c + 1) * 64], p_ys[c][:, 0:64]),
                                         r=[p_ys[c]], w=[oo])
                                C.op('pool', lambda: nc.gpsimd.tensor_copy(pbo[:, p, :], w['pr'][:]), r=[w['pr']], w=[pbo])
                            else:
                                p_g = self.pn()
                                self.mm(p_g, p_g[:], g2t, g2t[:, ps_], gd_, gd_[:])
                                C.op('act', lambda: nc.scalar.copy(w['og'][:], p_g[:]), r=[p_g], w=[w['og']])
                                for c in range(2):
                                    self.vtt(w['ys'], w['ys'][:, c * 64:(c + 1) * 64], p_ys[c], p_ys[c][:, 0:64], yb,
                                             yb[:, p, c * 64:(c + 1) * 64], ALU.add)
                                p_m = self.pn()
                                self.mm(p_m, p_m[:], cst, bones, w['ys'], w['ys'][:])
                                C.op('dve', lambda: nc.vector.scalar_tensor_tensor(
                                    w['yc'][:], p_m[:], -1.0 / 64, w['ys'][:], ALU.mult, ALU.add),
                                    r=[p_m, w['ys']], w=[w['yc']])
                                self.act(w['sq'], w['sq'][:], w['yc'], w['yc'][:], AF.Square)
                                p_v = self.pn()
                                self.mm(p_v, p_v[:], cst, bones, w['sq'], w['sq'][:])
                                self.rsqrt(w['rs'], w['rs'][:], p_v, p_v[:], self.epsgn, scale=1.0 / 64)
                                C.op('dve', lambda: nc.vector.scalar_tensor_tensor(
                                    w['yc'][:], w['yc'][:], col('rwkv_ln_g'), w['rs'][:], ALU.mult, ALU.mult),
                                    r=[w['yc'], P['rwkv_ln_g'], w['rs']], w=[w['yc']])
                                self.vtt(w['pr'], w['pr'][:], w['pr'], w['pr'][:], bbt, bbt[:, p, :], ALU.add, eng='pool')
                                p_bn = self.pn()
                                self.mm(p_bn, p_bn[:], cst, bones, w['pr'], w['pr'][:])
                                self.vtt(w['ys'], w['ys'][:], p_bn, p_bn[:], vv, vT, ALU.mult)
                                C.op('dve', lambda: nc.vector.scalar_tensor_tensor(
                                    w['yc'][:], w['yc'][:], col('rwkv_ln_b'), w['ys'][:], ALU.add, ALU.add),
                                    r=[w['yc'], P['rwkv_ln_b'], w['ys']], w=[w['yc']])
                                self.vtt(oo, oo[:, p, :], w['yc'], w['yc'][:], w['og'], w['og'][:], ALU.mult, eng='pool')
                            if kind == 'p' and step == ntile - 1:
                                dst = self.O['ns_rwkv']
                                zs = zstg[(si + p) % 2]
                                p_ = self.pn()
                                self.mm(p_, p_[:], Z, Z[:], cst, ident)
                                C.op('dve', lambda: nc.vector.tensor_copy(zs[:], p_[:]), r=[p_], w=[zs])
                                for hp in range(2):
                                    C.dma('sp', zs.b.name, dst[idx, l, d, p * 2 + hp],
                                          zs[hp * 64:(hp + 1) * 64, hp * 64:(hp + 1) * 64], r=[zs], w=[dst])
                        for g_ in (0, 2):
                            C.run_interleaved([(lambda p=p: unit(p)) for p in (g_, g_ + 1)])
                        if not f:
                            C.dma('sp', oo.b.name, ob[1024:1536, tok:tok + 128].rearrange("(h p) t -> p h t", p=128), oo[:],
                                  r=[oo], w=[ob])
                            C.dma('sp', pbo.b.name, bb[:, tok:tok + 128].rearrange("(h p) t -> p h t", p=128), pbo[:],
                                  r=[pbo], w=[bb])
                        else:
                            C.dma('sp', oo.b.name, br[1024:1536, tok:tok + 128].rearrange("(h p) t -> p h t", p=128), oo[:],
                                  r=[oo], w=[br])
            C.barrier()

    def stage_merge(self, l):
        C, nc = self.C, self.nc
        pf, br = self.S['pf'], self.S['br']
        PF = self.PF
        wb, wo = self.I['mix_branch'], self.I['mix_out']
        with ExitStack() as st:
            xt = self.sb(st, 'xt', [128, KC, 512])
            brb = self.sb(st, 'brb', [128, 12, 512], BF16)
            gt_ = self.sb(st, 'mg', [128, KC, 512])
            mg = self.sb(st, 'mgd', [128, KC, 512])
            mT = self.sb(st, 'mT', [128, KC, 512], BF16)
            tm = [self.sb(st, 'mtm%d' % i, [128, 512]) for i in range(2)]
            nA = n = 0
            xs = self.S['xT']
            for (tok0, TT, ci) in self.token_tiles():
                C.dma('sp', 'xt', xt[:, :, 0:TT], xs[:, tok0:tok0 + TT].rearrange("(kc p) t -> p kc t", p=128),
                      r=[xs], w=[xt])
                C.dma('pool', 'brb', brb[:, :, 0:TT], br[:, tok0:tok0 + TT].rearrange("(c p) t -> p c t", p=128),
                      r=[br], w=[brb])
                for bi in range(3):
                    wt = self.wA[nA % 2]
                    nA += 1
                    wv = wt[:].rearrange("p a b -> p (a b)").rearrange("p (k n) -> p k n", k=4)
                    self.wload(wt, wv, wb, wb[l, bi].rearrange("(kc p) n -> p kc n", p=128))
                    r0 = PF['merge'] + bi * D
                    C.dma('sp', 'mg', gt_[:, :, 0:TT], pf[r0:r0 + D, tok0:tok0 + TT].rearrange("(c p) t -> p c t", p=128),
                          r=[pf], w=[gt_])
                    for oc in range(KC):
                        pp = self.ps[n % 4]
                        n += 1
                        for kc in range(4):
                            C.op('pe', lambda: nc.tensor.matmul(pp[:, 0:TT], wv[:, kc, oc * 128:(oc + 1) * 128],
                                                                brb[:, bi * 4 + kc, 0:TT], start=(kc == 0), stop=(kc == 3)),
                                 r=[wt, brb], w=[pp])
                        if bi == 0:
                            self.vtt(mg, mg[:, oc, 0:TT], pp, pp[:, 0:TT], gt_, gt_[:, oc, 0:TT], ALU.mult)
                        else:
                            t_ = tm[n % 2]
                            self.vtt(t_, t_[:, 0:TT], pp, pp[:, 0:TT], gt_, gt_[:, oc, 0:TT], ALU.mult)
                            self.vtt(mg, mg[:, oc, 0:TT], mg, mg[:, oc, 0:TT], t_, t_[:, 0:TT], ALU.add, eng='pool')
                C.op('act', lambda: nc.scalar.copy(mT[:, :, 0:TT], mg[:, :, 0:TT]), r=[mg], w=[mT])
                for ob_ in range(4):
                    wt = self.wA[nA % 2]
                    nA += 1
                    self.wload(wt, wt[:], wo, wo[l, :, ob_ * 512:(ob_ + 1) * 512].rearrange("(kc p) n -> p kc n", p=128))
                    for oc in range(4):
                        pp = self.ps[4 + oc]
                        for kc in range(KC):
                            C.op('pe', lambda: nc.tensor.matmul(pp[:, 0:TT], wt[:, kc, oc * 128:(oc + 1) * 128],
                                                                mT[:, kc, 0:TT], start=(kc == 0), stop=(kc == KC - 1)),
                                 r=[wt, mT], w=[pp])
                        ko = ob_ * 4 + oc
                        C.op('dve', lambda: nc.vector.scalar_tensor_tensor(
                            xt[:, ko, 0:TT], pp[:, 0:TT], self.gat[:, 1, ko, ci:ci + 1], xt[:, ko, 0:TT],
                            ALU.mult, ALU.add), r=[pp, self.gat, xt], w=[xt])
                self.store_x(tok0, TT, xt)
            C.barrier()

    def stage_mix(self, l):
        parts = self.mix_parts
        self.stage_mixproj(l)
        if 'hgrn' in parts:
            self.stage_hgrn(l)
        if 'gdn' in parts:
            self.stage_gdn_pre(l)
            self.stage_gdn(l)
        if 'rwkv' in parts:
            self.stage_rwkv(l)
        if 'merge' in parts:
            self.stage_merge(l)

    def stage_final(self):
        C, nc = self.C, self.nc
        fn = self.P['final_norm']
        with ExitStack() as st:
            xt = self.sb(st, 'xt', [128, KC, 512])
            sq = [self.sb(st, 'sq%d' % i, [128, 512]) for i in range(2)]
            rstd = self.sb(st, 'rstd', [128, 512])
            yo = [self.sb(st, 'yo%d' % i, [128, D]) for i in range(2)]
            n = 0
            for (tok0, TT, ci) in self.token_tiles():
                self.load_norm(tok0, TT, 0, ci, xt, None, sq, rstd, None)
                for kc in range(KC):
                    C.op('dve', lambda: nc.vector.scalar_tensor_tensor(
                        xt[:, kc, 0:TT], xt[:, kc, 0:TT], fn[:, kc:kc + 1], rstd[:, 0:TT], ALU.mult, ALU.mult),
                        r=[xt, fn, rstd], w=[xt])
                for sub in range(TT // 128):
                    y = yo[n % 2]
                    n += 1
                    for g in range(4):
                        ps = self.ps[g]
                        for q in range(4):
                            kc = g * 4 + q
                            C.op('pe', lambda: nc.tensor.matmul(ps[:, q * 128:(q + 1) * 128],
                                                                xt[:, kc, sub * 128:(sub + 1) * 128], self.cc('ident'),
                                                                start=True, stop=True), r=[xt, self.cst], w=[ps])
                        if g % 2 == 0:
                            C.op('dve', lambda: nc.vector.tensor_copy(y[:, g * 512:(g + 1) * 512], ps[:]), r=[ps], w=[y])
                        else:
                            C.op('act', lambda: nc.scalar.copy(y[:, g * 512:(g + 1) * 512], ps[:]), r=[ps], w=[y])
                    t0 = tok0 + sub * 128
                    if t0 < self.NTP:
                        dst_t, dst = self.O['y_prompt'], self.O['y_prompt'][t0:t0 + 128, :]
                    else:
                        dst_t, dst = self.O['y_sample'], self.O['y_sample'][t0 - self.NTP:t0 - self.NTP + 128, :]
                    C.dma('sp', y.b.name, dst, y[:], r=[y], w=[dst_t])
            C.barrier()

    def build(self):
        self.setup()
        self.wA = [self.sb(self.es, 'wA%d' % i, [128, KC, 512], BF16) for i in range(2)]
        if self.do_mix:
            self.setup_mix()
        self.stage_x0()
        for l in range(self.depth):
            self.mod_layer(l, None)
            self.stage_ffn(l, 0)
            if self.do_mix:
                self.stage_mix(l)
            self.stage_ffn(l, 2)
        self.stage_final()
        self.C.barrier()
        self.es.close()
        return self.nc


WEIGHT_KEYS = ['mod_w', 'mod_b', 'norm_g', 'ffn_up', 'ffn_down', 'mix_in', 'hgrn_lb', 'hgrn_norm', 'gdn_conv',
               'gdn_a_log', 'gdn_dt_bias', 'gdn_norm', 'rwkv_w0', 'rwkv_w2', 'rwkv_a0', 'rwkv_a2', 'rwkv_g2',
               'rwkv_kk', 'rwkv_ka', 'rwkv_ln_g', 'rwkv_ln_b', 'mix_branch', 'mix_out', 'final_norm']


def make_in_maps(inp, n_cores, NP, wdepth=DEPTH):
    f = lambda a: np.ascontiguousarray(np.asarray(a, dtype=np.float32))
    shared = {k: f(inp[k]) for k in WEIGHT_KEYS}
    shared['rwkv_rk'] = f(inp['rwkv_rk']).reshape(DEPTH, BW)
    shared['consts'] = CONST_ARR
    for k in ('mod_w', 'ffn_up', 'ffn_down', 'mix_in', 'mix_branch', 'mix_out'):
        shared[k] = shared[k][:wdepth]
    xp, xs = f(inp['x_prompt']), f(inp['x_sample'])
    maps = []
    for i in range(n_cores):
        m = dict(shared)
        m['x_prompt'] = np.ascontiguousarray(xp[i * NP:(i + 1) * NP].reshape(-1, D))
        m['x_sample'] = np.ascontiguousarray(xs[i])
        m['state_hgrn'] = f(inp['state_hgrn'][i])
        m['state_gdn'] = f(inp['state_gdn'][i])
        m['state_rwkv'] = f(inp['state_rwkv'][i])
        m['cond'] = np.ascontiguousarray(np.stack([f(inp['c_ctx']), f(inp['c'])[i]], axis=0))
        maps.append(m)
    return maps


def kernel(**inp):
    n = 8
    NP = inp['x_prompt'].shape[0] // n
    LP = inp['x_prompt'].shape[1]
    LS = inp['x_sample'].shape[1]
    b = Builder(NP=NP, LP=LP, LS=LS)
    nc = b.build()
    maps = make_in_maps(inp, n, NP)
    res = run_bass_kernel_spmd(nc, maps, core_ids=list(range(n)))
    R = res.results
    yp = np.concatenate([r['y_prompt'].reshape(NP, LP, D) for r in R], axis=0)
    ys = np.stack([r['y_sample'] for r in R], axis=0)
    hg = np.concatenate([r['ns_hgrn'] for r in R], axis=0)
    gd = np.concatenate([r['ns_gdn'] for r in R], axis=0)
    rw = np.concatenate([r['ns_rwkv'] for r in R], axis=0)
    return (yp.astype(np.float32), ys.astype(np.float32), hg.astype(np.float32), gd.astype(np.float32),
            rw.astype(np.float32))
```
